# Optimizing a Trainium2 kernel written in Bass

```python
import math
import jax, jax.numpy as jnp
from jax import lax
import numpy as np

D_MODEL = 1024
BATCH = 8
SEQ = 2048
DEPTH = 4

CHUNK = 64
Q_BLOCK = 128
N_MEM = 256
EPS = 1e-6
F32 = jnp.float32

GLA_HEADS = 4
GLA_DK = 32
GLA_DV = 64
GLA_GATE_RANK = 16
GLA_GATE_TAU = 16.0
GLA_QK = GLA_HEADS * GLA_DK
GLA_V = GLA_HEADS * GLA_DV
S5_GROUPS = 16
S5_GROUP_CH = 16
S5_STATE = 64
S5_WIDTH = S5_GROUPS * S5_GROUP_CH
S5_DT_MIN = 1e-3
S5_DT_MAX = 1e-1
FOX_HEADS = 8
FOX_DH = 64
FOX_W = FOX_HEADS * FOX_DH
XA_HEADS = 4
XA_DH = D_MODEL // XA_HEADS
D_FF = 2816
N_BRANCH = 3
IN_SIZES = (GLA_QK, GLA_QK, GLA_V, GLA_V, GLA_GATE_RANK, S5_WIDTH, FOX_W, FOX_W, FOX_W, FOX_HEADS, N_BRANCH * D_MODEL)
D_IN = GLA_QK + GLA_QK + GLA_V + GLA_V + GLA_GATE_RANK + S5_WIDTH + 3 * FOX_W + FOX_HEADS + N_BRANCH * D_MODEL

kernel_name = 'hybrid_gla_s5_fox_macaron_sandwich'


def rms_norm(x, g):
    xf = x.astype(F32)
    y = xf * lax.rsqrt(jnp.mean(xf * xf, axis=-1, keepdims=True) + EPS)
    return (y * g.astype(F32)).astype(x.dtype)


def swiglu(h, w_gu, w_down):
    gate, up = jnp.split(h @ w_gu, 2, axis=-1)
    return (jax.nn.silu(gate) * up) @ w_down


def split_cols(p, sizes):
    outs, off = [], 0
    for s in sizes:
        outs.append(p[..., off:off + s])
        off += s
    return outs


def gla_chunked(q, k, v, log_a):
    b, l, h, dk = q.shape
    dv = v.shape[-1]
    n = l // CHUNK

    def to_chunks(t):
        return t.astype(F32).reshape(b, n, CHUNK, h, t.shape[-1]).transpose(0, 3, 1, 2, 4)

    qc = to_chunks(q) * (dk ** -0.5)
    kc = to_chunks(k)
    vc = to_chunks(v)
    g = jnp.cumsum(to_chunks(log_a), axis=3)
    g_last = g[:, :, :, -1:, :]
    eg, ieg = jnp.exp(g), jnp.exp(-g)
    q_fwd = qc * eg
    a_fwd = jnp.einsum('bhnid,bhnjd->bhnij', q_fwd, kc * ieg)
    a_bwd = jnp.einsum('bhnid,bhnjd->bhnij', qc * ieg, kc * eg)
    lower = jnp.tril(jnp.ones((CHUNK, CHUNK), dtype=bool))
    attn = jnp.where(lower, a_fwd, a_bwd)
    o_intra = jnp.einsum('bhnij,bhnje->bhnie', attn, vc)
    ds = jnp.einsum('bhncd,bhnce->nbhde', kc * jnp.exp(g_last - g), vc)
    decay = jnp.exp(g_last[:, :, :, 0, :]).transpose(2, 0, 1, 3)

    def step(s, inp):
        d, dsn = inp
        return d[..., None] * s + dsn, s

    _, s_prev = lax.scan(step, jnp.zeros((b, h, dk, dv), F32), (decay, ds))
    o_inter = jnp.einsum('bhncd,nbhde->bhnce', q_fwd, s_prev)
    o = (o_intra + o_inter).transpose(0, 2, 3, 1, 4).reshape(b, l, h, dv)
    return o.astype(v.dtype)


def s5_ssm(u, a_re, a_im, log_dt, b_re, b_im, c_re, c_im, d_skip):
    bsz, l, _ = u.shape
    uf = u.astype(F32).reshape(bsz, l, S5_GROUPS, S5_GROUP_CH)
    lam_re = jnp.minimum(a_re.astype(F32), -1e-4)
    lam_im = a_im.astype(F32)
    dt = jnp.exp(log_dt.astype(F32))[:, None]
    mag = jnp.exp(lam_re * dt)
    ab_re = mag * jnp.cos(lam_im * dt)
    ab_im = mag * jnp.sin(lam_im * dt)
    den = lam_re * lam_re + lam_im * lam_im
    z_re = ((ab_re - 1.0) * lam_re + ab_im * lam_im) / den
    z_im = (ab_im * lam_re - (ab_re - 1.0) * lam_im) / den
    br, bi = b_re.astype(F32), b_im.astype(F32)
    bb_re = z_re[..., None] * br - z_im[..., None] * bi
    bb_im = z_re[..., None] * bi + z_im[..., None] * br
    bu_re = jnp.einsum('gph,blgh->blgp', bb_re, uf)
    bu_im = jnp.einsum('gph,blgh->blgp', bb_im, uf)

    def combine(e1, e2):
        a1r, a1i, b1r, b1i = e1
        a2r, a2i, b2r, b2i = e2
        return (a2r * a1r - a2i * a1i, a2r * a1i + a2i * a1r,
                a2r * b1r - a2i * b1i + b2r, a2r * b1i + a2i * b1r + b2i)

    shp = bu_re.shape
    _, _, x_re, x_im = lax.associative_scan(
        combine, (jnp.broadcast_to(ab_re, shp), jnp.broadcast_to(ab_im, shp), bu_re, bu_im), axis=1)
    y = (jnp.einsum('ghp,blgp->blgh', c_re.astype(F32), x_re)
         - jnp.einsum('ghp,blgp->blgh', c_im.astype(F32), x_im))
    y = y + d_skip.astype(F32).reshape(S5_GROUPS, S5_GROUP_CH) * uf
    return y.reshape(bsz, l, S5_WIDTH).astype(u.dtype)


def forgetting_attention(q, k, v, log_f):
    b, l, h, dh = q.shape
    fcum = jnp.cumsum(log_f, axis=1).transpose(0, 2, 1)
    qh, kh, vh = (t.transpose(0, 2, 1, 3) for t in (q, k, v))
    scale = dh ** -0.5
    neg = jnp.finfo(F32).min
    outs = []
    for i in range(l // Q_BLOCK):
        s0, s1 = i * Q_BLOCK, (i + 1) * Q_BLOCK
        logits = (jnp.einsum('bhqd,bhkd->bhqk', qh[:, :, s0:s1], kh[:, :, :s1]).astype(F32) * scale
                  + fcum[:, :, s0:s1, None] - fcum[:, :, None, :s1])
        mask = (s0 + jnp.arange(Q_BLOCK))[:, None] >= jnp.arange(s1)[None, :]
        p = jax.nn.softmax(jnp.where(mask, logits, neg), axis=-1)
        outs.append(jnp.einsum('bhqk,bhkd->bhqd', p.astype(vh.dtype), vh[:, :, :s1]))
    o = jnp.concatenate(outs, axis=2)
    return o.transpose(0, 2, 1, 3).reshape(b, l, h * dh)


def hybrid_mixer(h, w_in, gla_gate_w, gla_gate_b, gla_norm_g, w_gla_up,
                 s5_a_re, s5_a_im, s5_log_dt, s5_b_re, s5_b_im, s5_c_re, s5_c_im, s5_d,
                 s5_glu_w, s5_glu_b, w_s5_up, fox_f_b, w_fox_up, w_mix_out):
    b, l, _ = h.shape
    (gq, gk, gv, gr, gdown, su, fq, fk, fv, ff, gates) = split_cols(h @ w_in, IN_SIZES)
    log_a = jax.nn.log_sigmoid((gdown @ gla_gate_w + gla_gate_b).astype(F32)) / GLA_GATE_TAU
    o = gla_chunked(gq.reshape(b, l, GLA_HEADS, GLA_DK), gk.reshape(b, l, GLA_HEADS, GLA_DK),
                    gv.reshape(b, l, GLA_HEADS, GLA_DV), log_a.reshape(b, l, GLA_HEADS, GLA_DK))
    o = rms_norm(o, gla_norm_g.reshape(GLA_HEADS, GLA_DV)).reshape(b, l, GLA_V)
    gla_out = (o * jax.nn.silu(gr)) @ w_gla_up
    y = jax.nn.gelu(s5_ssm(su, s5_a_re, s5_a_im, s5_log_dt, s5_b_re, s5_b_im, s5_c_re, s5_c_im, s5_d))
    s5_out = (y * jax.nn.sigmoid(y @ s5_glu_w + s5_glu_b)) @ w_s5_up
    log_f = jax.nn.log_sigmoid(ff.astype(F32) + fox_f_b.astype(F32))
    fo = forgetting_attention(fq.reshape(b, l, FOX_HEADS, FOX_DH), fk.reshape(b, l, FOX_HEADS, FOX_DH),
                              fv.reshape(b, l, FOX_HEADS, FOX_DH), log_f)
    fox_out = fo @ w_fox_up
    g = jax.nn.sigmoid(gates.reshape(b, l, N_BRANCH, D_MODEL))
    mix = g[:, :, 0] * gla_out + g[:, :, 1] * s5_out + g[:, :, 2] * fox_out
    return mix @ w_mix_out


def memory_cross_attention(h, mem_n, w_q, w_kv, w_o):
    b, l, _ = h.shape
    m = mem_n.shape[1]
    q = (h @ w_q).reshape(b, l, XA_HEADS, XA_DH)
    k, v = jnp.split(mem_n @ w_kv, 2, axis=-1)
    k = k.reshape(b, m, XA_HEADS, XA_DH)
    v = v.reshape(b, m, XA_HEADS, XA_DH)
    logits = jnp.einsum('blhd,bmhd->bhlm', q, k).astype(F32) * (XA_DH ** -0.5)
    p = jax.nn.softmax(logits, axis=-1)
    o = jnp.einsum('bhlm,bmhd->blhd', p.astype(v.dtype), v).reshape(b, l, D_MODEL)
    return o @ w_o


def _normal(key, shape, scale):
    return jax.random.normal(key, shape, F32) * scale


def _gain(key, shape):
    return 1.0 + 0.02 * jax.random.normal(key, shape, F32)


def setup_inputs(seed: int = 0) -> dict:
    key = jax.random.key(seed)
    ks = list(jax.random.split(key, 48))
    L, D, P, G, HC = DEPTH, D_MODEL, S5_STATE, S5_GROUPS, S5_GROUP_CH
    inp = {}
    inp['x'] = _normal(ks[0], (BATCH, SEQ, D), 1.0)
    inp['mem'] = _normal(ks[1], (BATCH, N_MEM, D), 1.0)
    inp['ffn1_pre_g'] = _gain(ks[2], (L, D))
    inp['ffn1_w_gu'] = _normal(ks[3], (L, D, 2 * D_FF), D ** -0.5)
    inp['ffn1_w_down'] = _normal(ks[4], (L, D_FF, D), D_FF ** -0.5)
    inp['ffn1_post_g'] = _gain(ks[5], (L, D))
    inp['mix_pre_g'] = _gain(ks[6], (L, D))
    inp['w_in'] = _normal(ks[7], (L, D, D_IN), D ** -0.5)
    inp['gla_gate_w'] = _normal(ks[8], (L, GLA_GATE_RANK, GLA_QK), GLA_GATE_RANK ** -0.5)
    inp['gla_gate_b'] = _normal(ks[9], (L, GLA_QK), 0.1)
    inp['gla_norm_g'] = _gain(ks[10], (L, GLA_V))
    inp['w_gla_up'] = _normal(ks[11], (L, GLA_V, D), GLA_V ** -0.5)
    inp['s5_a_re'] = -0.5 + _normal(ks[12], (L, G, P), 0.01)
    inp['s5_a_im'] = jnp.pi * jnp.arange(P, dtype=F32) + _normal(ks[13], (L, G, P), 0.01)
    inp['s5_log_dt'] = jax.random.uniform(ks[14], (L, G), F32, math.log(S5_DT_MIN), math.log(S5_DT_MAX))
    inp['s5_b_re'] = _normal(ks[15], (L, G, P, HC), (HC ** -0.5) * math.sqrt(0.5))
    inp['s5_b_im'] = _normal(ks[16], (L, G, P, HC), (HC ** -0.5) * math.sqrt(0.5))
    inp['s5_c_re'] = _normal(ks[17], (L, G, HC, P), (P ** -0.5) * math.sqrt(0.5))
    inp['s5_c_im'] = _normal(ks[18], (L, G, HC, P), (P ** -0.5) * math.sqrt(0.5))
    inp['s5_d'] = _normal(ks[19], (L, S5_WIDTH), 1.0)
    inp['s5_glu_w'] = _normal(ks[20], (L, S5_WIDTH, S5_WIDTH), S5_WIDTH ** -0.5)
    inp['s5_glu_b'] = _normal(ks[21], (L, S5_WIDTH), 0.01)
    inp['w_s5_up'] = _normal(ks[22], (L, S5_WIDTH, D), S5_WIDTH ** -0.5)
    inp['fox_f_b'] = 2.0 + _normal(ks[23], (L, FOX_HEADS), 0.1)
    inp['w_fox_up'] = _normal(ks[24], (L, FOX_W, D), FOX_W ** -0.5)
    inp['w_mix_out'] = _normal(ks[25], (L, D, D), D ** -0.5)
    inp['mix_post_g'] = _gain(ks[26], (L, D))
    inp['xa_pre_g'] = _gain(ks[27], (L, D))
    inp['xa_mem_g'] = _gain(ks[28], (L, D))
    inp['xa_w_q'] = _normal(ks[29], (L, D, D), D ** -0.5)
    inp['xa_w_kv'] = _normal(ks[30], (L, D, 2 * D), D ** -0.5)
    inp['xa_w_o'] = _normal(ks[31], (L, D, D), D ** -0.5)
    inp['xa_post_g'] = _gain(ks[32], (L, D))
    inp['ffn2_pre_g'] = _gain(ks[33], (L, D))
    inp['ffn2_w_gu'] = _normal(ks[34], (L, D, 2 * D_FF), D ** -0.5)
    inp['ffn2_w_down'] = _normal(ks[35], (L, D_FF, D), D_FF ** -0.5)
    inp['ffn2_post_g'] = _gain(ks[36], (L, D))
    return inp


def reference(x, mem, ffn1_pre_g, ffn1_w_gu, ffn1_w_down, ffn1_post_g,
              mix_pre_g, w_in, gla_gate_w, gla_gate_b, gla_norm_g, w_gla_up,
              s5_a_re, s5_a_im, s5_log_dt, s5_b_re, s5_b_im, s5_c_re, s5_c_im, s5_d,
              s5_glu_w, s5_glu_b, w_s5_up, fox_f_b, w_fox_up, w_mix_out, mix_post_g,
              xa_pre_g, xa_mem_g, xa_w_q, xa_w_kv, xa_w_o, xa_post_g,
              ffn2_pre_g, ffn2_w_gu, ffn2_w_down, ffn2_post_g):
    for l in range(DEPTH):
        h = rms_norm(x, ffn1_pre_g[l])
        x = x + 0.5 * rms_norm(swiglu(h, ffn1_w_gu[l], ffn1_w_down[l]), ffn1_post_g[l])
        h = rms_norm(x, mix_pre_g[l])
        y = hybrid_mixer(h, w_in[l], gla_gate_w[l], gla_gate_b[l], gla_norm_g[l], w_gla_up[l],
                         s5_a_re[l], s5_a_im[l], s5_log_dt[l], s5_b_re[l], s5_b_im[l],
                         s5_c_re[l], s5_c_im[l], s5_d[l], s5_glu_w[l], s5_glu_b[l], w_s5_up[l],
                         fox_f_b[l], w_fox_up[l], w_mix_out[l])
        x = x + rms_norm(y, mix_post_g[l])
        h = rms_norm(x, xa_pre_g[l])
        mem_n = rms_norm(mem, xa_mem_g[l])
        x = x + rms_norm(memory_cross_attention(h, mem_n, xa_w_q[l], xa_w_kv[l], xa_w_o[l]), xa_post_g[l])
        h = rms_norm(x, ffn2_pre_g[l])
        x = x + 0.5 * rms_norm(swiglu(h, ffn2_w_gu[l], ffn2_w_down[l]), ffn2_post_g[l])
    return x
```

```python
import numpy as np
from contextlib import ExitStack
import concourse.bass as bass
import concourse.mybir as mybir
from concourse.bass_utils import run_bass_kernel_spmd

F32 = mybir.dt.float32
BF16 = mybir.dt.bfloat16
AF = mybir.ActivationFunctionType
ALU = mybir.AluOpType

DEPTH = 4
D = 1024
SEQ = 2048
NMEM = 256
DFF = 2816
DIN = 5656
EPS = 1e-6
O_GQ, O_GK, O_GV, O_GR, O_GD, O_SU, O_FQ, O_FK, O_FV, O_FF, O_GATES = 0, 128, 256, 512, 768, 784, 1040, 1552, 2064, 2576, 2584

SP_FFN1_PRE, SP_FFN1_POST, SP_MIX_PRE, SP_MIX_POST, SP_XA_PRE, SP_XA_MEM, SP_XA_POST, SP_FFN2_PRE, SP_FFN2_POST = 0, 8, 16, 24, 32, 40, 48, 56, 64
SP_GLA_B, SP_GLA_NG, SP_S5_D, SP_GLU_B, SP_ARE, SP_AIM, SP_LDT = 72, 73, 75, 77, 79, 87, 95
SPL = 103
BP_FOXB, BP_GLAB, BP_ARE, BP_AIM, BP_LDT = 0, 8, 136, 1160, 2184
BPL = 3208
C_ID, C_TI, C_MU, C_ML, C_BO, C_HM, C_SM, C_IT, C_IP, C_SEL, C_M01 = 0, 128, 256, 384, 512, 640, 644, 900, 1028, 1029, 1037
NCONST = 1039

ENGS = ("pe", "act", "dve", "pool", "sp")


class Res:
    __slots__ = ("w", "rs", "excl")

    def __init__(self, rs):
        self.w = None
        self.rs = rs
        self.excl = False


class Tk:
    __slots__ = ("key", "val", "clk")

    def __init__(self, key, val, clk):
        self.key = key
        self.val = val
        self.clk = clk


class Prog:
    def __init__(self, nc):
        self.nc = nc
        self.q = {e: [] for e in ENGS}
        self.sems = {}
        self.cnt = {}
        self.known = {e: {} for e in ENGS}
        self.res = {}
        self.grave = {}
        self.pe_pending = []
        self.n_wait = 0
        self.n_op = 0
        for e in ENGS:
            self._sem("E_" + e)

    def _sem(self, key):
        if key not in self.sems:
            self.sems[key] = self.nc.alloc_semaphore(key)
            self.cnt[key] = 0
        return self.sems[key]

    def R(self, *key):
        r = self.res.get(key)
        if r is None:
            r = Res(list(self.grave.values()))
            r.excl = (key[0] == "ps")
            self.res[key] = r
        return r

    def free(self, prefix):
        dead = [k for k in self.res if k[0] == prefix or (isinstance(k[0], str) and k[0].startswith(prefix + "."))]
        for k in dead:
            r = self.res.pop(k)
            for t in ([r.w] if r.w is not None else []) + r.rs:
                g = self.grave.get(t.key)
                if g is None or g.val < t.val:
                    self.grave[t.key] = t

    def _deps(self, eng, reads, writes):
        if eng != "pe" and self.pe_pending:
            for (rr, ww) in self.pe_pending:
                for w in writes:
                    assert all(w is not x for x in rr) and all(w is not x for x in ww), "write to resource with pending PE access"
                for r in reads:
                    assert all(r is not x for x in ww), "read of resource with pending PE write"
        tks = []
        own = "E_" + eng
        for r in reads:
            if r.w is not None:
                tks.append(r.w)
            if r.excl:
                tks.extend(t for t in r.rs if t.key != own)
        for w in writes:
            if w.w is not None:
                tks.append(w.w)
            tks.extend(w.rs)
        kn = self.known[eng]
        need = {}
        for t in tks:
            if eng == "pe" and t.key == "E_pe":
                continue
            if kn.get(t.key, 0) >= t.val:
                continue
            o = need.get(t.key)
            if o is None or o.val < t.val:
                need[t.key] = t
        for key, t in need.items():
            if kn.get(key, 0) >= t.val:
                continue
            self.q[eng].append(("wait", key, t.val))
            self.n_wait += 1
            for k2, v2 in t.clk.items():
                if kn.get(k2, 0) < v2:
                    kn[k2] = v2
            kn[key] = max(kn.get(key, 0), t.val)

    def op(self, eng, fn, reads=(), writes=(), inc=True):
        self._deps(eng, reads, writes)
        self.n_op += 1
        if not inc:
            self.q[eng].append(("op", fn, None, 0))
            self.pe_pending.append((list(reads), list(writes)))
            return None
        key = "E_" + eng
        self.cnt[key] += 1
        clk = dict(self.known[eng])
        clk[key] = self.cnt[key]
        tk = Tk(key, self.cnt[key], clk)
        self.q[eng].append(("op", fn, key, 1))
        allr = [list(reads)]
        allw = [list(writes)]
        if eng == "pe" and self.pe_pending:
            for (rr, ww) in self.pe_pending:
                allr.append(rr)
                allw.append(ww)
            self.pe_pending = []
        for ww in allw:
            for w in ww:
                w.w = tk
                w.rs = []
        for rr in allr:
            for r in rr:
                if r.w is not tk:
                    r.rs.append(tk)
        return tk

    def dma(self, eng, out, in_, semkey, reads=(), writes=()):
        self._deps(eng, reads, writes)
        self._sem(semkey)
        self.cnt[semkey] += 16
        clk = dict(self.known[eng])
        clk[semkey] = self.cnt[semkey]
        tk = Tk(semkey, self.cnt[semkey], clk)
        self.q[eng].append(("op", lambda e: e.dma_start(out=out, in_=in_), semkey, 16))
        for w in writes:
            w.w = tk
            w.rs = []
        for r in reads:
            r.rs.append(tk)
        return tk

    def mm(self, out, lhsT, rhs, start=True, stop=True, r=(), w=(), inc=True):
        return self.op("pe", lambda e: e.matmul(out, lhsT=lhsT, rhs=rhs, start=start, stop=stop), r, w, inc)

    def tr(self, out, in_, ident, r=(), w=()):
        return self.op("pe", lambda e: e.transpose(out, in_, ident), r, w, True)

    def act(self, out, in_, func, r=(), w=(), bias=None, scale=None):
        kw = {}
        if bias is not None:
            kw["bias"] = bias
        if scale is not None:
            kw["scale"] = scale
        return self.op("act", lambda e: e.activation(out=out, in_=in_, func=func, **kw), r, w)

    def tt(self, out, in0, in1, op, r=(), w=(), eng="dve"):
        return self.op(eng, lambda e: e.tensor_tensor(out=out, in0=in0, in1=in1, op=op), r, w)

    def ts(self, out, in0, s1, op0, s2=None, op1=None, r=(), w=(), eng="dve"):
        if op1 is None:
            return self.op(eng, lambda e: e.tensor_scalar(out=out, in0=in0, scalar1=s1, scalar2=None, op0=op0), r, w)
        return self.op(eng, lambda e: e.tensor_scalar(out=out, in0=in0, scalar1=s1, scalar2=s2, op0=op0, op1=op1), r, w)

    def stt(self, out, in0, scalar, in1, op0, op1, r=(), w=(), eng="dve"):
        return self.op(eng, lambda e: e.scalar_tensor_tensor(out=out, in0=in0, scalar=scalar, in1=in1, op0=op0, op1=op1), r, w)

    def cp(self, out, in_, r=(), w=(), eng="dve"):
        if eng == "act":
            return self.op("act", lambda e: e.copy(out=out, in_=in_), r, w)
        return self.op(eng, lambda e: e.tensor_copy(out=out, in_=in_), r, w)

    def recip(self, out, in_, r=(), w=()):
        return self.op("dve", lambda e: e.reciprocal(out=out, in_=in_), r, w)

    def memset(self, ap, val, w=(), eng="dve"):
        return self.op(eng, lambda e: e.memset(ap, val), (), w)

    def finish(self):
        nc = self.nc
        sems = self.sems
        q = self.q

        def replay(name):
            def body(e):
                for it in q[name]:
                    if it[0] == "wait":
                        e.wait_ge(sems[it[1]], it[2])
                    else:
                        ins = it[1](e)
                        if it[2] is not None:
                            ins.then_inc(sems[it[2]], it[3])
            return body

        with nc.Block() as block:
            block.sync(replay("sp"))
            block.tensor(replay("pe"))
            block.scalar(replay("act"))
            block.vector(replay("dve"))
            block.gpsimd(replay("pool"))


class Ring:
    def __init__(self, P, name, tiles):
        self.P = P
        self.name = name
        self.tiles = tiles
        self.i = 0

    def next(self):
        k = self.i % len(self.tiles)
        self.i += 1
        return self.tiles[k], self.P.R(self.name, k), f"D_{self.name}{k}"


class Ctx:
    pass


_UID = [0]


def sbt(nc, name, shape, dt):
    _UID[0] += 1
    return nc.sbuf_tensor(f"{name}_{_UID[0]}", list(shape), dt)


def build_program(stages, dbg=None, branches=("s5", "gla", "fox"), dbg_branch=None, gla_ntiles=16, gla_stop=99):
    nc = bass.Bass("TRN2", target_bir_lowering=False)
    P = Prog(nc)
    C = Ctx()
    C.nc, C.P = nc, P
    C.branches, C.dbg_branch = branches, dbg_branch
    C.gla_ntiles = gla_ntiles
    C.gla_stop = gla_stop
    dr = {}

    def din(name, shape):
        dr[name] = nc.dram_tensor(name, list(shape), F32, kind="ExternalInput").ap()
        return dr[name]

    din("x", [SEQ, D])
    din("mem", [NMEM, D])
    for w in (1, 2):
        din(f"ffn{w}_w_gu", [DEPTH, D, 2 * DFF])
        din(f"ffn{w}_w_down", [DEPTH, DFF, D])
    din("w_in", [DEPTH, D, DIN])
    din("gla_gate_w", [DEPTH, 128, 128])
    din("w_gla_up", [DEPTH, 256, D])
    din("s5_glu_w", [DEPTH, 256, 256])
    din("w_s5_up", [DEPTH, 256, D])
    din("w_fox_up", [DEPTH, 512, D])
    din("w_mix_out", [DEPTH, D, D])
    din("xa_w_q", [DEPTH, D, D])
    din("xa_w_kv", [DEPTH, D, 2 * D])
    din("xa_w_o", [DEPTH, D, D])
    din("bb_re", [DEPTH, 256, 1024])
    din("bb_im", [DEPTH, 256, 1024])
    din("cc_re", [DEPTH, 1024, 256])
    din("cc_im", [DEPTH, 1024, 256])
    din("sp", [128, DEPTH * SPL])
    din("bp", [128, DEPTH * BPL])
    din("consts", [128, NCONST])
    y = nc.dram_tensor("y", [SEQ, D], F32, kind="ExternalOutput").ap()
    C.dr = dr
    if dbg is not None:
        C.dbg = nc.dram_tensor("dbg", list(dbg), F32, kind="ExternalOutput").ap()

    with ExitStack() as es:
        def sb(name, shape, dt):
            return es.enter_context(sbt(nc, name, shape, dt))

        C.xT = sb("xT", [128, 8, SEQ], F32)
        C.cst = sb("cst", [128, NCONST], F32)
        C.sp = sb("spar", [128, DEPTH * SPL], F32)
        C.onesb = sb("onesb", [128, 128], BF16)
        C.tib = sb("tib", [128, 128], BF16)
        C.mub = sb("mub", [128, 2, 128], BF16)
        C.bob = sb("bob", [128, 128], BF16)
        C.smb = sb("smb", [128, 256], BF16)
        C.epsc = sb("epsc", [128, 1], F32)
        C.ring8 = Ring(P, "r8", [sb(f"r8_{i}", [128, 4096], BF16) for i in range(3)])
        C.ring2 = Ring(P, "r2", [sb(f"r2_{i}", [128, 1024], BF16) for i in range(3)])
        C.ps = [es.enter_context(nc.psum_tensor(f"ps{i}", [128, 512], F32)) for i in range(8)]
        C.psi = 0

        P.dma("sp", C.cst[:], dr["consts"], "D_c0", writes=[P.R("cst")])
        P.dma("sp", C.sp[:], dr["sp"], "D_c1", writes=[P.R("spar")])
        P.memset(C.onesb[:], 1.0, w=[P.R("cstb")])
        P.memset(C.epsc[:], EPS, w=[P.R("cstb")])
        P.cp(C.tib[:], C.cst[:, C_TI:C_TI + 128], r=[P.R("cst")], w=[P.R("cstb")])
        P.cp(C.mub[:, 0, :], C.cst[:, C_MU:C_MU + 128], r=[P.R("cst")], w=[P.R("cstb")])
        P.cp(C.mub[:, 1, :], C.cst[:, C_ML:C_ML + 128], r=[P.R("cst")], w=[P.R("cstb")])
        P.cp(C.bob[:], C.cst[:, C_BO:C_BO + 128], r=[P.R("cst")], w=[P.R("cstb")])
        P.cp(C.smb[:], C.cst[:, C_SM:C_SM + 256], r=[P.R("cst")], w=[P.R("cstb")])
        C.ident = C.cst[:, C_ID:C_ID + 128]

        load_x(C)
        for (kind, l) in stages:
            if kind == "ffn1":
                ffn(C, l, 1)
            elif kind == "ffn2":
                ffn(C, l, 2)
            elif kind == "xa":
                xattn(C, l)
            elif kind == "mix":
                mixer(C, l)
        store_x(C, y)
        P.finish()
    C.stats = (P.n_op, P.n_wait)
    return nc, C


def bank(C, lo=0, hi=4):
    k = lo + (C.psi % (hi - lo))
    C.psi += 1
    return C.ps[k], C.P.R("ps", k)


def load_x(C):
    P, nc = C.P, C.nc
    xd = C.dr["x"]
    with ExitStack() as es:
        st = [es.enter_context(sbt(nc, f"xst{i}", [128, D], F32)) for i in range(2)]
        for i in range(16):
            s = i % 2
            P.dma("sp", st[s][:], xd[i * 128:(i + 1) * 128, :], f"D_xs{s}", writes=[P.R("xst", s)])
            for g in range(2):
                ps, pr = bank(C)
                for c4 in range(4):
                    c = g * 4 + c4
                    P.tr(ps[:, c4 * 128:(c4 + 1) * 128], st[s][:, c * 128:(c + 1) * 128], C.ident,
                         r=[P.R("xst", s), P.R("cst")], w=[pr])
                P.cp(C.xT[:, g * 4:(g + 1) * 4, i * 128:(i + 1) * 128],
                     ps[:].rearrange("p (c t) -> p c t", c=4), r=[pr], w=[P.R("x", i // 4)],
                     eng=("act" if g else "dve"))
        P.free("xst")


def store_x(C, y):
    P, nc = C.P, C.nc
    with ExitStack() as es:
        st = [es.enter_context(sbt(nc, f"yst{i}", [128, D], F32)) for i in range(2)]
        for i in range(16):
            s = i % 2
            for g in range(2):
                ps, pr = bank(C)
                for c4 in range(4):
                    c = g * 4 + c4
                    P.tr(ps[:, c4 * 128:(c4 + 1) * 128], C.xT[:, c, i * 128:(i + 1) * 128], C.ident,
                         r=[P.R("x", i // 4), P.R("cst")], w=[pr])
                P.cp(st[s][:, g * 512:(g + 1) * 512], ps[:], r=[pr], w=[P.R("yst", s)],
                     eng=("act" if g else "dve"))
            P.dma("sp", y[i * 128:(i + 1) * 128, :], st[s][:], f"D_ys{s}", reads=[P.R("yst", s)])
        for s in range(2):
            key = f"D_ys{s}"
            P.q["sp"].append(("wait", key, P.cnt[key]))
        P.free("yst")


def gcol(C, l, off, c):
    return C.sp[:, l * SPL + off + c:l * SPL + off + c + 1]


def norm_stats(C, src3, ntok, sq, rs, r, rres):
    P = C.P
    ps, pr = bank(C)
    for c in range(8):
        k = c % 2
        P.act(sq[:, k, :ntok], src3[:, c, :], AF.Square, r=r, w=[P.R("nrm.sq", k)])
        P.mm(ps[:, :ntok], C.onesb[:], sq[:, k, :ntok], start=(c == 0), stop=(c == 7),
             r=[P.R("nrm.sq", k), P.R("cstb")], w=[pr], inc=True)
    P.act(rs[:, :ntok], ps[:, :ntok], AF.Ln, r=[pr, P.R("cstb")], w=[rres], bias=C.epsc[:], scale=1.0 / D)
    P.act(rs[:, :ntok], rs[:, :ntok], AF.Exp, r=[rres], w=[rres], scale=-0.5)


def prenorm(C, l, goff, src3, ntok, dst3, sq, rs, rsrc, rdst, gsp=None):
    P = C.P
    rres = P.R("nrm.rs")
    norm_stats(C, src3, ntok, sq, rs, rsrc, rres)
    for c in range(8):
        P.stt(dst3[:, c, :], src3[:, c, :], gcol(C, l, goff, c), rs[:, :ntok], ALU.mult, ALU.mult,
              r=list(rsrc) + [rres, P.R("spar")], w=rdst)


def postnorm_add(C, l, goff, ysb3, wgt, tt, sq, rs, ryres):
    P = C.P
    rres = P.R("nrm.rs")
    norm_stats(C, ysb3, 512, sq, rs, [ryres], rres)
    for c in range(8):
        P.stt(ysb3[:, c, :], ysb3[:, c, :], gcol(C, l, goff, c), rs[:, :512], ALU.mult, ALU.mult,
              r=[ryres, rres, P.R("spar")], w=[ryres])
    xt = C.xT[:, :, tt * 512:(tt + 1) * 512]
    P.stt(xt, ysb3, float(wgt), xt, ALU.mult, ALU.add, r=[ryres, P.R("x", tt)], w=[P.R("x", tt)])


def wload(C, dst, src, res, sem):
    C.P.dma("pool", dst, src, sem, writes=[res])


def ffn(C, l, which):
    P, nc = C.P, C.nc
    wgu = C.dr[f"ffn{which}_w_gu"][l].rearrange("(kc p) n -> p kc n", p=128)
    wdn = C.dr[f"ffn{which}_w_down"][l].rearrange("(j p) n -> p j n", p=128)
    pre = SP_FFN1_PRE if which == 1 else SP_FFN2_PRE
    post = SP_FFN1_POST if which == 1 else SP_FFN2_POST
    for half in range(2):
        with ExitStack() as es:
            aT = es.enter_context(sbt(nc, "aT", [128, 22, 1024], BF16))
            sq = es.enter_context(sbt(nc, "sq", [128, 2, 512], BF16))
            rs = es.enter_context(sbt(nc, "rs", [128, 512], F32))
            with ExitStack() as es1:
                hT = es1.enter_context(sbt(nc, "hT", [128, 8, 1024], BF16))
                sg = [es1.enter_context(sbt(nc, f"sg{i}", [128, 512], F32)) for i in range(2)]
                for t2 in range(2):
                    tt = half * 2 + t2
                    prenorm(C, l, pre, C.xT[:, :, tt * 512:(tt + 1) * 512], 512, hT[:, :, t2 * 512:(t2 + 1) * 512],
                            sq, rs, [P.R("x", tt)], [P.R("ffn.h", t2)])
                k = 0
                for jb in range(11):
                    slot, sres, ssem = C.ring8.next()
                    sv = slot[:].rearrange("p (kc g n) -> p kc g n", kc=8, g=2)
                    wload(C, sv[:, :, 0, :], wgu[:, :, jb * 256:(jb + 1) * 256], sres, ssem)
                    wload(C, sv[:, :, 1, :], wgu[:, :, DFF + jb * 256:DFF + (jb + 1) * 256], sres, ssem)
                    for jj in range(2):
                        j = jb * 2 + jj
                        for t2 in range(2):
                            psA, prA = bank(C)
                            psB, prB = bank(C)
                            for g, (ps_, pr_) in enumerate(((psA, prA), (psB, prB))):
                                for kc in range(8):
                                    P.mm(ps_[:], sv[:, kc, g, jj * 128:(jj + 1) * 128], hT[:, kc, t2 * 512:(t2 + 1) * 512],
                                         start=(kc == 0), stop=(kc == 7), r=[sres, P.R("ffn.h", t2)], w=[pr_], inc=(kc == 7))
                            s = k % 2
                            k += 1
                            P.act(sg[s][:], psA[:], AF.Silu, r=[prA], w=[P.R("ffn.sg", s)])
                            P.tt(aT[:, j, t2 * 512:(t2 + 1) * 512], psB[:], sg[s][:], ALU.mult,
                                 r=[prB, P.R("ffn.sg", s)], w=[P.R("ffn.a", j, t2)])
            P.free("ffn.h")
            P.free("ffn.sg")
            ysb = es.enter_context(sbt(nc, "ysb", [128, 8, 1024], F32))
            for mq in range(4):
                acc = [[(C.ps[4 + m2 * 2 + t2], P.R("ps", 4 + m2 * 2 + t2)) for t2 in range(2)] for m2 in range(2)]
                for jb in range(11):
                    slot, sres, ssem = C.ring2.next()
                    sv = slot[:, 0:512].rearrange("p (j n) -> p j n", j=2)
                    wload(C, sv, wdn[:, jb * 2:jb * 2 + 2, mq * 256:(mq + 1) * 256], sres, ssem)
                    for jj in range(2):
                        j = jb * 2 + jj
                        for m2 in range(2):
                            for t2 in range(2):
                                P.mm(acc[m2][t2][0][:], sv[:, jj, m2 * 128:(m2 + 1) * 128], aT[:, j, t2 * 512:(t2 + 1) * 512],
                                     start=(j == 0), stop=(j == 21), r=[sres, P.R("ffn.a", j, t2)], w=[acc[m2][t2][1]],
                                     inc=(j == 21 or (jj == 1 and m2 == 1 and t2 == 1)))
                for m2 in range(2):
                    for t2 in range(2):
                        P.cp(ysb[:, mq * 2 + m2, t2 * 512:(t2 + 1) * 512], acc[m2][t2][0][:], r=[acc[m2][t2][1]],
                             w=[P.R("ffn.y", t2)], eng=("act" if (m2 + t2) % 2 else "dve"))
            for t2 in range(2):
                postnorm_add(C, l, post, ysb[:, :, t2 * 512:(t2 + 1) * 512], 0.5, half * 2 + t2, sq, rs, P.R("ffn.y", t2))
        P.free("ffn")
        P.free("nrm")


def xattn(C, l):
    P, nc = C.P, C.nc
    wq_d = C.dr["xa_w_q"][l].rearrange("(kc p) n -> p kc n", p=128)
    wkv_d = C.dr["xa_w_kv"][l].rearrange("(kc p) n -> p kc n", p=128)
    wo_d = C.dr["xa_w_o"][l].rearrange("(kc p) n -> p kc n", p=128)
    with ExitStack() as es:
        def sb(name, shape, dt):
            return es.enter_context(sbt(nc, name, shape, dt))
        memn = sb("memn", [128, 8, NMEM], BF16)
        kT = sb("kT", [128, 8, NMEM], BF16)
        v = sb("v", [128, 2, D], BF16)
        wq = sb("wq", [128, 8, D], BF16)
        sq = sb("sq", [128, 2, 512], BF16)
        rs = sb("rs", [128, 512], F32)
        hT = sb("hT", [128, 8, 512], BF16)
        qT = sb("qT", [128, 8, 512], BF16)
        oT = sb("oT", [128, 8, 512], BF16)
        ysb = sb("ysb", [128, 8, 512], F32)
        pt = [sb(f"pt{i}", [128, 512], BF16) for i in range(4)]
        rl = [sb(f"rl{i}", [128, 512], F32) for i in range(2)]
        for hf in range(2):
            wload(C, wq[:, :, hf * 512:(hf + 1) * 512], wq_d[:, :, hf * 512:(hf + 1) * 512], P.R("xa.wq"), "D_xawq")
        with ExitStack() as es1:
            memT = es1.enter_context(sbt(nc, "memT", [128, 8, NMEM], F32))
            mst = [es1.enter_context(sbt(nc, f"mst{i}", [128, D], F32)) for i in range(2)]
            for i in range(2):
                P.dma("sp", mst[i][:], C.dr["mem"][i * 128:(i + 1) * 128, :], f"D_ms{i}", writes=[P.R("xa.mst", i)])
                for g in range(2):
                    ps, pr = bank(C)
                    for c4 in range(4):
                        c = g * 4 + c4
                        P.tr(ps[:, c4 * 128:(c4 + 1) * 128], mst[i][:, c * 128:(c + 1) * 128], C.ident,
                             r=[P.R("xa.mst", i), P.R("cst")], w=[pr])
                    P.cp(memT[:, g * 4:(g + 1) * 4, i * 128:(i + 1) * 128], ps[:].rearrange("p (c t) -> p c t", c=4),
                         r=[pr], w=[P.R("xa.memT")])
            prenorm(C, l, SP_XA_MEM, memT[:], NMEM, memn[:], sq, rs, [P.R("xa.memT")], [P.R("xa.memn")])
        P.free("xa.mst")
        P.free("xa.memT")
        for blk in range(4):
            slot, sres, ssem = C.ring8.next()
            sv = slot[:].rearrange("p (kc n) -> p kc n", kc=8)
            wload(C, sv, wkv_d[:, :, blk * 512:(blk + 1) * 512], sres, ssem)
            if blk < 2:
                for o4 in range(4):
                    oc = blk * 4 + o4
                    ps, pr = bank(C)
                    for kc in range(8):
                        P.mm(ps[:, :NMEM], sv[:, kc, o4 * 128:(o4 + 1) * 128], memn[:, kc, :], start=(kc == 0), stop=(kc == 7),
                             r=[sres, P.R("xa.memn")], w=[pr], inc=(kc == 7))
                    P.cp(kT[:, oc, :], ps[:, :NMEM], r=[pr], w=[P.R("xa.kT")], eng="act")
            else:
                vb = blk - 2
                for mt in range(2):
                    ps, pr = bank(C)
                    for kc in range(8):
                        P.mm(ps[:], memn[:, kc, mt * 128:(mt + 1) * 128], sv[:, kc, :], start=(kc == 0), stop=(kc == 7),
                             r=[sres, P.R("xa.memn")], w=[pr], inc=(kc == 7))
                    P.cp(v[:, mt, vb * 512:(vb + 1) * 512], ps[:], r=[pr], w=[P.R("xa.v")], eng="act")
        prenorm(C, l, SP_XA_PRE, C.xT[:, :, 0:512], 512, hT[:], sq, rs, [P.R("x", 0)], [P.R("xa.h")])
        for tt in range(4):
            for oc in range(8):
                ps, pr = bank(C, 0, 8)
                for kc in range(8):
                    P.mm(ps[:], wq[:, kc, oc * 128:(oc + 1) * 128], hT[:, kc, :], start=(kc == 0), stop=(kc == 7),
                         r=[P.R("xa.wq"), P.R("xa.h")], w=[pr], inc=(kc == 7))
                P.cp(qT[:, oc, :], ps[:], r=[pr], w=[P.R("xa.q")], eng=("act" if oc % 2 else "dve"))

            def emit_S(h):
                out = []
                for mt in range(2):
                    ps, pr = bank(C, 0, 8)
                    for dc in range(2):
                        P.mm(ps[:], kT[:, 2 * h + dc, mt * 128:(mt + 1) * 128], qT[:, 2 * h + dc, :], start=(dc == 0), stop=(dc == 1),
                             r=[P.R("xa.kT"), P.R("xa.q")], w=[pr], inc=(dc == 1))
                    out.append((ps, pr))
                return out
            Sp = {0: emit_S(0)}
            for h in range(4):
                if h + 1 < 4:
                    Sp[h + 1] = emit_S(h + 1)
                pts = []
                for mt, (ps, pr) in enumerate(Sp.pop(h)):
                    k = (h * 2 + mt) % 4
                    P.act(pt[k][:], ps[:], AF.Exp, r=[pr], w=[P.R("xa.pt", k)], scale=1.0 / 16.0)
                    pts.append((pt[k], P.R("xa.pt", k)))
                ps, pr = bank(C, 0, 8)
                for mt in range(2):
                    P.mm(ps[:], C.onesb[:], pts[mt][0][:], start=(mt == 0), stop=(mt == 1), r=[pts[mt][1], P.R("cstb")], w=[pr],
                         inc=(mt == 1))
                P.act(rl[h % 2][:], ps[:], AF.Ln, r=[pr], w=[P.R("xa.rl", h % 2)])
                P.act(rl[h % 2][:], rl[h % 2][:], AF.Exp, r=[P.R("xa.rl", h % 2)], w=[P.R("xa.rl", h % 2)], scale=-1.0)
                for ec in range(2):
                    ps, pr = bank(C, 0, 8)
                    for mt in range(2):
                        P.mm(ps[:], v[:, mt, (2 * h + ec) * 128:(2 * h + ec + 1) * 128], pts[mt][0][:], start=(mt == 0), stop=(mt == 1),
                             r=[pts[mt][1], P.R("xa.v")], w=[pr], inc=(mt == 1))
                    P.tt(oT[:, 2 * h + ec, :], ps[:], rl[h % 2][:], ALU.mult, r=[pr, P.R("xa.rl", h % 2)], w=[P.R("xa.o")])
            if tt + 1 < 4:
                prenorm(C, l, SP_XA_PRE, C.xT[:, :, (tt + 1) * 512:(tt + 2) * 512], 512, hT[:], sq, rs,
                        [P.R("x", tt + 1)], [P.R("xa.h")])
            for ob in range(2):
                slot, sres, ssem = C.ring8.next()
                sv = slot[:].rearrange("p (kc n) -> p kc n", kc=8)
                wload(C, sv, wo_d[:, :, ob * 512:(ob + 1) * 512], sres, ssem)
                for m4 in range(4):
                    mc = ob * 4 + m4
                    ps, pr = bank(C)
                    for kc in range(8):
                        P.mm(ps[:], sv[:, kc, m4 * 128:(m4 + 1) * 128], oT[:, kc, :], start=(kc == 0), stop=(kc == 7),
                             r=[sres, P.R("xa.o")], w=[pr], inc=(kc == 7))
                    P.cp(ysb[:, mc, :], ps[:], r=[pr], w=[P.R("xa.y")], eng=("act" if mc % 2 else "dve"))
            postnorm_add(C, l, SP_XA_POST, ysb[:], 1.0, tt, sq, rs, P.R("xa.y"))
    P.free("xa")
    P.free("nrm")


PI = float(np.pi)
TWO_PI = float(2 * np.pi)


def sin_of(C, out, ang, turns, tf, tg, ti, r, w, rt):
    P = C.P
    P.ts(tf, ang, 1.0 / TWO_PI, ALU.mult, float(turns), ALU.add, r=r, w=[rt])
    P.cp(ti, tf, r=[rt], w=[rt])
    P.cp(tg, ti, r=[rt], w=[rt])
    P.tt(tf, tf, tg, ALU.subtract, r=[rt], w=[rt])
    P.ts(tg, tf, 0.5, ALU.is_gt, r=[rt], w=[rt])
    P.tt(tf, tf, tg, ALU.subtract, r=[rt], w=[rt])
    P.ts(tg, tf, -0.5, ALU.is_lt, r=[rt], w=[rt])
    P.tt(tf, tf, tg, ALU.add, r=[rt], w=[rt])
    P.ts(tf, tf, 0.4999999, ALU.min, -0.4999999, ALU.max, r=[rt], w=[rt])
    P.act(out, tf, AF.Sin, r=[rt], w=w, scale=TWO_PI)


def proj_fm(C, wv, wres, c0, ncols, hT, evac):
    P = C.P
    for tt in range(4):
        ps, pr = bank(C)
        for kc in range(8):
            P.mm(ps[:ncols, :], wv[:, kc, c0:c0 + ncols], hT[:, kc, tt * 512:(tt + 1) * 512], start=(kc == 0), stop=(kc == 7),
                 r=[wres, P.R("mx.h", tt)], w=[pr], inc=(kc == 7))
        evac(ps, pr, tt)


def mixer(C, l):
    P, nc = C.P, C.nc
    win = C.dr["w_in"][l].rearrange("(kc p) n -> p kc n", p=128)
    with ExitStack() as es:
        s5T = es.enter_context(sbt(nc, "s5T", [128, 2, SEQ], BF16))
        glaT = es.enter_context(sbt(nc, "glaT", [128, 2, SEQ], BF16))
        foT = es.enter_context(sbt(nc, "foT", [128, 4, SEQ], BF16))
        hT = es.enter_context(sbt(nc, "hTm", [128, 8, SEQ], BF16))
        with ExitStack() as e0:
            sq = e0.enter_context(sbt(nc, "sq", [128, 2, 512], BF16))
            rs = e0.enter_context(sbt(nc, "rs", [128, 512], F32))
            for tt in range(4):
                prenorm(C, l, SP_MIX_PRE, C.xT[:, :, tt * 512:(tt + 1) * 512], 512, hT[:, :, tt * 512:(tt + 1) * 512],
                        sq, rs, [P.R("x", tt)], [P.R("mx.h", tt)])
        P.free("nrm")
        if "s5" in C.branches:
            s5_branch(C, l, win, hT, s5T)
        if "gla" in C.branches:
            gla_branch(C, l, win, hT, glaT)
        if "fox" in C.branches:
            fox_branch(C, l, win, hT, foT)
        if C.dbg_branch is not None:
            src = {"s5": (s5T, 2, "mx.s5"), "gla": (glaT, 2, "mx.gla"), "fox": (foT, 4, "mx.fo")}[C.dbg_branch]
            for tt in range(4):
                rr = [P.R(src[2], m, tt) for m in range(src[1])]
                P.cp(C.xT[:, 0:src[1], tt * 512:(tt + 1) * 512], src[0][:, :, tt * 512:(tt + 1) * 512], r=rr, w=[P.R("x", tt)])
        else:
            merge(C, l, win, hT, s5T, glaT, foT)
    P.free("mx")
    P.free("nrm")


def s5_branch(C, l, win, hT, s5T):
    P, nc = C.P, C.nc
    bbre, bbim = C.dr["bb_re"][l], C.dr["bb_im"][l]
    ccre = C.dr["cc_re"][l].rearrange("(c p) n -> p c n", p=128)
    ccim = C.dr["cc_im"][l].rearrange("(c p) n -> p c n", p=128)
    gluw_d = C.dr["s5_glu_w"][l].rearrange("(kc p) n -> p kc n", p=128)
    b = l * SPL
    with ExitStack() as es:
        def sb(name, shape, dt):
            return es.enter_context(sbt(nc, name, shape, dt))
        uT = sb("uT", [128, SEQ], BF16)
        ENz = sb("ENz", [128, 2, 512], F32)
        EP = sb("EP", [128, 8, 128], F32)
        BB = sb("BB", [128, 2, 512], BF16)
        CC = sb("CC", [128, 4, 2, 128], BF16)
        Tb = [sb(f"Tb{i}", [128, 4, 512], BF16) for i in range(2)]
        ntib = sb("ntib", [128, 128], BF16)
        gt = sb("gt", [128, 256], F32)
        W = sb("W", [128, 8, 128], F32)
        Zb = sb("Zb", [128, 8, 128], BF16)
        tmp = [sb(f"t{i}", [128, 512], F32) for i in range(4)]
        sc = sb("sc", [128, 64], F32)
        carry = sb("carry", [128, 8], F32)
        ysb = sb("ys5", [128, 128], F32)
        ti = sb("ti", [128, 128], mybir.dt.int32)
        rsc, rtb, rtab = P.R("s5.sc"), P.R("s5.tb"), P.R("s5.tab")
        slot, sres, ssem = C.ring8.next()
        wsu = slot[:, 0:2048].rearrange("p (kc n) -> p kc n", kc=8)
        wload(C, wsu, win[:, :, O_SU:O_SU + 256], sres, ssem)
        iota = C.cst[:, C_IT:C_IT + 128]
        P.ts(ntib[:], C.cst[:, C_TI:C_TI + 128], -1.0, ALU.mult, r=[P.R("cst")], w=[P.R("s5.ntib")])
        for hc in range(2):
            def ev(ps, pr, tt):
                P.cp(uT[:, tt * 512:(tt + 1) * 512], ps[:], r=[pr], w=[P.R("s5.u", tt)], eng="act")
            proj_fm(C, wsu, sres, hc * 128, 128, hT, ev)
            are = C.sp[:, b + SP_ARE + 4 * hc:b + SP_ARE + 4 * hc + 4]
            aim = C.sp[:, b + SP_AIM + 4 * hc:b + SP_AIM + 4 * hc + 4]
            ldt = C.sp[:, b + SP_LDT + 4 * hc:b + SP_LDT + 4 * hc + 4]
            col = lambda k: sc[:, 4 * k:4 * k + 4]
            dt_, lamre, lr, li, nlr, mag1, sinli, cosli, abre, abim, den, zre, zim, sA, sB = [col(k) for k in range(15)]
            rw = dict(r=[rsc, P.R("spar")], w=[rsc])
            P.act(dt_, ldt, AF.Exp, **rw)
            P.ts(lamre, are, -1e-4, ALU.min, **rw)
            P.tt(lr, lamre, dt_, ALU.mult, **rw)
            P.tt(li, aim, dt_, ALU.mult, **rw)
            P.ts(nlr, lr, -1.0, ALU.mult, **rw)
            P.act(mag1, lr, AF.Exp, **rw)
            sin_of(C, sinli, li, 0.0, sA, sB, ti[:, 0:4], [rsc], [rsc], rsc)
            sin_of(C, cosli, li, 0.25, sA, sB, ti[:, 0:4], [rsc], [rsc], rsc)
            P.tt(abre, mag1, cosli, ALU.mult, **rw)
            P.tt(abim, mag1, sinli, ALU.mult, **rw)
            P.tt(den, lamre, lamre, ALU.mult, **rw)
            P.tt(sB, aim, aim, ALU.mult, **rw)
            P.tt(den, den, sB, ALU.add, **rw)
            P.recip(den, den, **rw)
            P.ts(abre, abre, -1.0, ALU.add, **rw)
            P.tt(sA, abre, lamre, ALU.mult, **rw)
            P.tt(sB, abim, aim, ALU.mult, **rw)
            P.tt(sA, sA, sB, ALU.add, **rw)
            P.tt(zre, sA, den, ALU.mult, **rw)
            P.tt(sA, abim, lamre, ALU.mult, **rw)
            P.tt(sB, abre, aim, ALU.mult, **rw)
            P.tt(sA, sA, sB, ALU.subtract, **rw)
            P.tt(zim, sA, den, ALU.mult, **rw)
            A_, B_ = tmp[0][:, 0:128], tmp[0][:, 128:256]
            S_, Cc_ = tmp[1][:, 0:128], tmp[1][:, 128:256]
            MP, MN = tmp[2][:, 0:128], tmp[2][:, 128:256]
            RR, E1, E2, RG = tmp[3][:, 0:128], tmp[3][:, 128:256], tmp[3][:, 256:384], tmp[3][:, 384:512]
            tw = dict(r=[rtb, rsc, P.R("cst")], w=[rtb])
            for pc in range(4):
                P.ts(A_, iota, li[:, pc:pc + 1], ALU.mult, **tw)
                P.act(MP, iota, AF.Exp, scale=lr[:, pc:pc + 1], **tw)
                P.act(MN, iota, AF.Exp, scale=nlr[:, pc:pc + 1], **tw)
                sin_of(C, S_, A_, 0.0, RR, RG, ti[:], [rtb], [rtb], rtb)
                sin_of(C, Cc_, A_, 0.25, RR, RG, ti[:], [rtb], [rtb], rtb)
                P.tt(EP[:, pc, :], MP, Cc_, ALU.mult, r=[rtb], w=[rtab])
                P.tt(EP[:, 4 + pc, :], MP, S_, ALU.mult, r=[rtb], w=[rtab])
                P.ts(E1, Cc_, zre[:, pc:pc + 1], ALU.mult, **tw)
                P.stt(E1, S_, zim[:, pc:pc + 1], E1, ALU.mult, ALU.add, **tw)
                P.tt(E1, E1, MN, ALU.mult, **tw)
                P.ts(E2, Cc_, zim[:, pc:pc + 1], ALU.mult, **tw)
                P.ts(B_, S_, zre[:, pc:pc + 1], ALU.mult, **tw)
                P.tt(E2, E2, B_, ALU.subtract, **tw)
                P.tt(E2, E2, MN, ALU.mult, **tw)
                for ri, E in enumerate((E1, E2)):
                    ps, pr = bank(C)
                    P.tr(ps[:, 0:128], E, C.ident, r=[rtb, P.R("cst")], w=[pr])
                    P.cp(ENz[:, ri, pc * 128:(pc + 1) * 128], ps[:, 0:128], r=[pr], w=[rtab], eng="act")
            wload(C, BB[:, 0, :], bbre[hc * 128:(hc + 1) * 128, hc * 512:(hc + 1) * 512], P.R("s5.BB"), "D_s5b")
            wload(C, BB[:, 1, :], bbim[hc * 128:(hc + 1) * 128, hc * 512:(hc + 1) * 512], P.R("s5.BB"), "D_s5b")
            wload(C, CC[:, :, 0, :], ccre[:, 4 * hc:4 * hc + 4, hc * 128:(hc + 1) * 128], P.R("s5.CC"), "D_s5c")
            wload(C, CC[:, :, 1, :], ccim[:, 4 * hc:4 * hc + 4, hc * 128:(hc + 1) * 128], P.R("s5.CC"), "D_s5c")
            P.memset(carry[:], 0.0, w=[P.R("s5.carry")])
            Wre, Wim = W[:, 0:4, :], W[:, 4:8, :]
            EPre, EPim = EP[:, 0:4, :], EP[:, 4:8, :]
            tv = [t[:].rearrange("p (a b) -> p a b", a=4) for t in tmp]
            rt = [P.R("s5.t", i) for i in range(4)]
            if l == 0 and hc == 0:
                print("[sbuf] s5 remaining", nc.sbuf_bytes_remaining)

            def front(c):
                tok = slice(c * 128, (c + 1) * 128)
                tt = c // 4
                Tc = Tb[c % 2]
                rV = P.R("s5.V", c % 2)
                psr, prr = bank(C)
                psi, pri = bank(C)
                P.mm(psr[:], uT[:, tok], BB[:, 0, :], r=[P.R("s5.u", tt), P.R("s5.BB")], w=[prr])
                P.mm(psi[:], uT[:, tok], BB[:, 1, :], r=[P.R("s5.u", tt), P.R("s5.BB")], w=[pri])
                P.tt(Tc[:, 0, :], psr[:], ENz[:, 0, :], ALU.mult, r=[prr, rtab], w=[rV])
                P.tt(Tc[:, 1, :], psi[:], ENz[:, 1, :], ALU.mult, r=[pri, rtab], w=[rV])
                P.tt(Tc[:, 2, :], psi[:], ENz[:, 0, :], ALU.mult, r=[pri, rtab], w=[rV])
                P.tt(Tc[:, 3, :], psr[:], ENz[:, 1, :], ALU.mult, r=[prr, rtab], w=[rV])
                pw = [(C.ps[4 + 2 * (c % 2) + ri], P.R("ps", 4 + 2 * (c % 2) + ri)) for ri in range(2)]
                for ri in range(2):
                    for pc in range(4):
                        cs = slice(pc * 128, (pc + 1) * 128)
                        P.mm(pw[ri][0][:, cs], Tc[:, 2 * ri, cs], C.tib[:], start=True, stop=False,
                             r=[rV, P.R("cstb")], w=[pw[ri][1]], inc=False)
                        P.mm(pw[ri][0][:, cs], Tc[:, 2 * ri + 1, cs], (ntib[:] if ri == 0 else C.tib[:]), start=False, stop=True,
                             r=[rV, P.R("cstb"), P.R("s5.ntib")], w=[pw[ri][1]], inc=(pc == 3))
                return pw

            def back(c, pw):
                tok = slice(c * 128, (c + 1) * 128)
                tt = c // 4
                for ri in range(2):
                    for pc in range(4):
                        k = ri * 4 + pc
                        P.act(W[:, k, :], pw[ri][0][:, pc * 128:(pc + 1) * 128], AF.Identity, bias=carry[:, k:k + 1],
                              r=[pw[ri][1], P.R("s5.carry")], w=[P.R("s5.W")])
                rW = P.R("s5.W")
                P.tt(tv[0], EPre, Wre, ALU.mult, r=[rtab, rW], w=[rt[0]])
                P.tt(tv[1], EPim, Wim, ALU.mult, r=[rtab, rW], w=[rt[1]])
                P.tt(tv[2], EPre, Wim, ALU.mult, r=[rtab, rW], w=[rt[2]])
                P.tt(tv[3], EPim, Wre, ALU.mult, r=[rtab, rW], w=[rt[3]])
                P.tt(Wre, tv[0], tv[1], ALU.subtract, r=[rt[0], rt[1]], w=[rW])
                P.tt(Wim, tv[2], tv[3], ALU.add, r=[rt[2], rt[3]], w=[rW])
                xv = W[:, 0:8, 127].rearrange("p (a b) -> p a b", a=2)
                Lrb = EP[:, 0:4, 1].unsqueeze(1).to_broadcast([128, 2, 4])
                Lib = EP[:, 4:8, 1].unsqueeze(1).to_broadcast([128, 2, 4])
                cw = dict(r=[rW, rtab, rsc], w=[rsc])
                P.tt(sc[:, 0:8].rearrange("p (a b) -> p a b", a=2), Lrb, xv, ALU.mult, **cw)
                P.tt(sc[:, 8:16].rearrange("p (a b) -> p a b", a=2), Lib, xv, ALU.mult, **cw)
                P.tt(carry[:, 0:4], sc[:, 0:4], sc[:, 12:16], ALU.subtract, r=[rsc], w=[P.R("s5.carry")])
                P.tt(carry[:, 4:8], sc[:, 4:8], sc[:, 8:12], ALU.add, r=[rsc], w=[P.R("s5.carry")])
                P.cp(Zb[:, 0:4, :], Wre, r=[rW], w=[P.R("s5.Zb")], eng="act")
                P.op("act", lambda e, o=Zb[:, 4:8, :], i=Wim: e.mul(out=o, in_=i, mul=-1.0), [rW], [P.R("s5.Zb")])
                py, pyr = bank(C)
                for pc in range(4):
                    P.mm(py[:, :128], CC[:, pc, 0, :], Zb[:, pc, :], start=(pc == 0), stop=False,
                         r=[P.R("s5.CC"), P.R("s5.Zb")], w=[pyr], inc=False)
                    P.mm(py[:, :128], CC[:, pc, 1, :], Zb[:, 4 + pc, :], start=False, stop=(pc == 3),
                         r=[P.R("s5.CC"), P.R("s5.Zb")], w=[pyr], inc=(pc == 3))
                return py, pyr

            def tail(c, py, pyr):
                tok = slice(c * 128, (c + 1) * 128)
                tt = c // 4
                P.stt(ysb[:], uT[:, tok], gcol(C, l, SP_S5_D, hc), py[:, :128], ALU.mult, ALU.add,
                      r=[pyr, P.R("s5.u", tt), P.R("spar")], w=[P.R("s5.ys")])
                gelu_tanh(C, s5T[:, hc, tok], ysb[:], gt[:, 0:128], gt[:, 128:256], [P.R("s5.ys")], [P.R("mx.s5", hc, tt)], P.R("s5.gt"))

            pws = {0: front(0)}
            pys = {}
            for c in range(16):
                if c + 1 < 16:
                    pws[c + 1] = front(c + 1)
                pys[c] = back(c, pws.pop(c))
                if c >= 1:
                    tail(c - 1, *pys.pop(c - 1))
            tail(15, *pys.pop(15))
        slot, gres, gsem = C.ring2.next()
        gluw = slot[:, 0:512].rearrange("p (kc n) -> p kc n", kc=2)
        wload(C, gluw, gluw_d, gres, gsem)
        for tt in range(4):
            tl = slice(tt * 512, (tt + 1) * 512)
            for oc in range(2):
                ps, pr = bank(C)
                for kc in range(2):
                    P.mm(ps[:], gluw[:, kc, oc * 128:(oc + 1) * 128], s5T[:, kc, tl], start=(kc == 0), stop=(kc == 1),
                         r=[gres, P.R("mx.s5", kc, tt)], w=[pr], inc=(kc == 1))
                P.act(tmp[oc][:], ps[:], AF.Sigmoid, bias=gcol(C, l, SP_GLU_B, oc), r=[pr, P.R("spar")], w=[rt[oc]])
            for oc in range(2):
                P.tt(s5T[:, oc, tl], s5T[:, oc, tl], tmp[oc][:], ALU.mult, r=[rt[oc], P.R("mx.s5", oc, tt)], w=[P.R("mx.s5", oc, tt)])
    P.free("s5")


def gla_branch(C, l, win, hT, glaT):
    P, nc = C.P, C.nc
    b = l * SPL
    with ExitStack() as es:
        def sb(name, shape, dt):
            return es.enter_context(sbt(nc, name, shape, dt))
        qT = sb("gqT", [128, SEQ], BF16)
        kT = sb("gkT", [128, SEQ], BF16)
        ktok = sb("gktok", [128, 16, 128], BF16)
        gv = sb("gv", [128, 16, 256], BF16)
        gdT = sb("gdT", [128, SEQ], BF16)
        gw = sb("ggw", [128, 128], BF16)
        vpad = sb("gvpad", [128, 4, 128], BF16)
        klc = sb("gklc", [128, 2, 128], BF16)
        klcb = sb("gklcb", [128, 2, 128], BF16)
        qfb = sb("gqfb", [128, 128], BF16)
        bB = sb("gbB", [128, 128], F32)
        onec = sb("onec", [128, 1], F32)
        zt = sb("gzt", [128, 128], F32)
        nla = sb("gnla", [128, 128], BF16)
        eg = [sb(f"geg{i}", [128, 128], F32) for i in range(2)]
        ieg = sb("gieg", [128, 128], F32)
        er = sb("ger", [128, 128], F32)
        qf = sb("gqf", [128, 128], BF16)
        qi = sb("gqi", [128, 128], BF16)
        kih = sb("gkih", [128, 4, 128], BF16)
        kfh = sb("gkfh", [128, 4, 128], BF16)
        kl = sb("gkl", [128, 128], BF16)
        attn = sb("gattn", [128, 4, 2, 128], BF16)
        S = sb("gS", [128, 256], F32)
        Sb = [sb(f"gSb{i}", [128, 256], BF16) for i in range(3)]
        sqg = sb("gsq", [128, 128], BF16)
        rstd = sb("grstd", [128, 128], F32)
        on = sb("gon", [128, 128], F32)
        rtmp = sb("grtmp", [128, 512], F32)
        hm = C.cst[:, C_HM:C_HM + 4]
        MUf = C.cst[:, C_MU:C_MU + 128]
        MLf = C.cst[:, C_ML:C_ML + 128]
        P.memset(onec[:], 1.0, w=[P.R("gla.c")])
        P.memset(S[:], 0.0, w=[P.R("gla.S")])
        P.memset(Sb[0][:], 0.0, w=[P.R("gla.Sb", 0)])
        P.dma("sp", bB[:], C.dr["bp"][:, l * BPL + BP_GLAB:l * BPL + BP_GLAB + 128], "D_gbB", writes=[P.R("gla.bB")])
        P.memset(vpad[:], 0.0, w=[P.R("gla.vpad")])
        P.memset(klc[:], 0.0, w=[P.R("gla.klc", 0)])
        P.memset(klcb[:], 0.0, w=[P.R("gla.klc", 1)])
        wload(C, gw[:], C.dr["gla_gate_w"][l], P.R("gla.gw"), "D_ggw")
        slotA, resA, semA = C.ring8.next()
        wA = slotA[:].rearrange("p (kc n) -> p kc n", kc=8)
        wload(C, wA, win[:, :, 0:512], resA, semA)
        slotD, resD, semD = C.ring8.next()
        win_gd = slotD[:, 0:1024].rearrange("p (kc n) -> p kc n", kc=8)
        wload(C, win_gd, win[:, :, 768:896], resD, semD)
        wB = slotD[:, 1024:3072].rearrange("p (kc n) -> p kc n", kc=8)
        wload(C, wB, win[:, :, 512:768], resD, semD)
        resB = resD

        def ev_q(ps, pr, tt):
            P.cp(qT[:, tt * 512:(tt + 1) * 512], ps[:], r=[pr], w=[P.R("gla.q", tt)], eng="act")

        def ev_k(ps, pr, tt):
            P.cp(kT[:, tt * 512:(tt + 1) * 512], ps[:], r=[pr], w=[P.R("gla.k", tt)], eng="act")
        if C.gla_stop >= 2:
            proj_fm(C, wA, resA, 0, 128, hT, ev_q)
            proj_fm(C, wA, resA, 128, 128, hT, ev_k)
        for m in range(2 if C.gla_stop >= 3 else 0):
            def ev_r(ps, pr, tt, m=m):
                P.act(rtmp[:], ps[:], AF.Sigmoid, r=[pr], w=[P.R("gla.rtmp")])
                P.tt(glaT[:, m, tt * 512:(tt + 1) * 512], ps[:], rtmp[:], ALU.mult, r=[pr, P.R("gla.rtmp")], w=[P.R("mx.gla", m, tt)])
            proj_fm(C, wB, resB, m * 128, 128, hT, ev_r)

        def ev_d(ps, pr, tt):
            P.cp(gdT[:, tt * 512:(tt + 1) * 512], ps[:], r=[pr], w=[P.R("gla.gd", tt)], eng="act")
        if C.gla_stop >= 4:
            proj_fm(C, win_gd, resD, 0, 128, hT, ev_d)
        for i in range(16 if C.gla_stop >= 5 else 0):
            ps, pr = bank(C)
            for kc in range(8):
                P.mm(ps[:, 0:384], hT[:, kc, i * 128:(i + 1) * 128], wA[:, kc, 128:512], start=(kc == 0), stop=(kc == 7),
                     r=[resA, P.R("mx.h", i // 4)], w=[pr], inc=(kc == 7))
            P.cp(ktok[:, i, :], ps[:, 0:128], r=[pr], w=[P.R("gla.ktok", i)], eng="act")
            P.cp(gv[:, i, :], ps[:, 128:384], r=[pr], w=[P.R("gla.gv", i)])
        sbi = [0]
        qf2 = [qf, qfb]
        klc2 = [klc, klcb]

        def phaseA(i):
            tok = slice(i * 128, (i + 1) * 128)
            tt = i // 4
            egi = eg[i % 2]
            reg = P.R("gla.eg", i % 2)
            qf_ = qf2[i % 2]
            rqf = P.R("gla.qf", i % 2)
            klc_ = klc2[i % 2]
            rklc = P.R("gla.klc", i % 2)
            ps1, pr1 = bank(C)
            P.mm(ps1[:, 0:128], gdT[:, tok], gw[:], r=[P.R("gla.gd", tt), P.R("gla.gw")], w=[pr1])
            P.tt(zt[:], ps1[:, 0:128], bB[:], ALU.add, r=[pr1, P.R("gla.bB")], w=[P.R("gla.zt")])
            P.act(zt[:], zt[:], AF.Exp, r=[P.R("gla.zt")], w=[P.R("gla.zt")], scale=-1.0)
            P.act(nla[:], zt[:], AF.Ln, r=[P.R("gla.zt"), P.R("gla.c")], w=[P.R("gla.nla")], bias=onec[:])
            psG, prG = bank(C)
            P.mm(psG[:, 0:128], nla[:], C.mub[:, 0, :], r=[P.R("gla.nla"), P.R("cstb")], w=[prG])
            psR, prR = bank(C)
            P.mm(psR[:, 0:128], C.mub[:, 1, :], nla[:], r=[P.R("gla.nla"), P.R("cstb")], w=[prR])
            P.act(egi[:], psG[:, 0:128], AF.Exp, r=[prG], w=[reg], scale=-1.0 / 16.0)
            P.act(ieg[:], psG[:, 0:128], AF.Exp, r=[prG], w=[P.R("gla.ieg")], scale=1.0 / 16.0)
            P.act(er[:], psR[:, 0:128], AF.Exp, r=[prR], w=[P.R("gla.er")], scale=-1.0 / 16.0)
            P.stt(qf_[:], qT[:, tok], 32.0 ** -0.5, egi[:], ALU.mult, ALU.mult, r=[P.R("gla.q", tt), reg], w=[rqf])
            P.stt(qi[:], qT[:, tok], 32.0 ** -0.5, ieg[:], ALU.mult, ALU.mult, r=[P.R("gla.q", tt), P.R("gla.ieg")], w=[P.R("gla.qi")])
            for h in range(4):
                P.stt(kih[:, h, :], kT[:, tok], hm[:, h:h + 1], ieg[:], ALU.mult, ALU.mult,
                      r=[P.R("gla.k", tt), P.R("gla.ieg"), P.R("cst")], w=[P.R("gla.kih")])
                P.stt(kfh[:, h, :], kT[:, tok], hm[:, h:h + 1], egi[:], ALU.mult, ALU.mult,
                      r=[P.R("gla.k", tt), reg, P.R("cst")], w=[P.R("gla.kfh")])
            for cc in range(2):
                P.tt(klc_[64 * cc:64 * cc + 64, cc, :], ktok[64 * cc:64 * cc + 64, i, :], er[64 * cc:64 * cc + 64, :], ALU.mult,
                     r=[P.R("gla.ktok", i), P.R("gla.er")], w=[rklc])
            pS = [bank(C), bank(C)]
            for h in range(4):
                pb, prb = pS[h // 2]
                o0 = (h % 2) * 256
                P.mm(pb[:, o0:o0 + 128], kih[:, h, :], qf_[:], r=[P.R("gla.kih"), rqf], w=[prb])
                P.mm(pb[:, o0 + 128:o0 + 256], kfh[:, h, :], qi[:], r=[P.R("gla.kfh"), P.R("gla.qi")], w=[prb])
            for h in range(4):
                pb, prb = pS[h // 2]
                o0 = (h % 2) * 256
                P.tt(attn[:, h, :, :], pb[:, o0:o0 + 256].rearrange("p (f t) -> p f t", f=2), C.mub[:], ALU.mult,
                     r=[prb, P.R("cstb")], w=[P.R("gla.attn")])
            for par in range(2):
                P.cp(vpad[:, par::2, 64 * par:64 * par + 64], gv[:, i, :].rearrange("p (h e) -> p h e", h=4)[:, par::2, :],
                     r=[P.R("gla.gv", i)], w=[P.R("gla.vpad")])
            pO = [(C.ps[4 + 2 * (i % 2) + m], P.R("ps", 4 + 2 * (i % 2) + m)) for m in range(2)]
            for m in range(2):
                po, pro = pO[m]
                for hh in range(2):
                    h = 2 * m + hh
                    for fb in range(2):
                        P.mm(po[:, 0:128], vpad[:, h, :], attn[:, h, fb, :],
                             start=(fb == 0 and hh == 0), stop=False, r=[P.R("gla.vpad"), P.R("gla.attn")], w=[pro])
            return pO

        def phaseB(i, pO):
            tok = slice(i * 128, (i + 1) * 128)
            tt = i // 4
            egi = eg[i % 2]
            reg = P.R("gla.eg", i % 2)
            qf_ = qf2[i % 2]
            rqf = P.R("gla.qf", i % 2)
            klc_ = klc2[i % 2]
            rklc = P.R("gla.klc", i % 2)
            for cc in range(2):
                sprev = Sb[sbi[0] % 3]
                rsprev = P.R("gla.Sb", sbi[0] % 3)
                for m in range(2):
                    po, pro = pO[m]
                    P.mm(po[:, cc * 64:(cc + 1) * 64], sprev[:, m * 128:(m + 1) * 128], qf_[:, cc * 64:(cc + 1) * 64],
                         start=False, stop=(cc == 1), r=[rsprev, rqf], w=[pro])
                pD, prD = bank(C)
                P.mm(pD[:, 0:256], klc_[:, cc, :], gv[:, i, :],
                     r=[rklc, P.R("gla.gv", i)], w=[prD])
                P.stt(S[:], S[:], egi[:, 63 + 64 * cc:64 + 64 * cc], pD[:, 0:256], ALU.mult, ALU.add,
                      r=[P.R("gla.S"), reg, prD], w=[P.R("gla.S")])
                sbi[0] += 1
                P.tt(Sb[sbi[0] % 3][:], S[:], C.smb[:], ALU.mult, r=[P.R("gla.S"), P.R("cstb")], w=[P.R("gla.Sb", sbi[0] % 3)])
            for m in range(2):
                po, pro = pO[m]
                P.act(sqg[:], po[:, 0:128], AF.Square, r=[pro], w=[P.R("gla.sq")])
                pN, prN = bank(C)
                P.mm(pN[:, 0:128], C.bob[:], sqg[:], r=[P.R("gla.sq"), P.R("cstb")], w=[prN])
                P.act(rstd[:], pN[:, 0:128], AF.Ln, r=[prN, P.R("cstb")], w=[P.R("gla.rstd")], bias=C.epsc[:], scale=1.0 / 64.0)
                P.act(rstd[:], rstd[:], AF.Exp, r=[P.R("gla.rstd")], w=[P.R("gla.rstd")], scale=-0.5)
                P.stt(on[:], po[:, 0:128], C.sp[:, b + SP_GLA_NG + m:b + SP_GLA_NG + m + 1], rstd[:], ALU.mult, ALU.mult,
                      r=[pro, P.R("gla.rstd"), P.R("spar")], w=[P.R("gla.on")])
                P.tt(glaT[:, m, tok], on[:], glaT[:, m, tok], ALU.mult, r=[P.R("gla.on"), P.R("mx.gla", m, tt)],
                     w=[P.R("mx.gla", m, tt)])

        nt = C.gla_ntiles
        pOs = {}
        if nt:
            pOs[0] = phaseA(0)
        for i in range(nt):
            if i + 1 < nt:
                pOs[i + 1] = phaseA(i + 1)
            phaseB(i, pOs.pop(i))
    P.free("gla")


def fox_branch(C, l, win, hT, foT):
    P, nc = C.P, C.nc
    with ExitStack() as es:
        def sb(name, shape, dt):
            return es.enter_context(sbt(nc, name, shape, dt))
        fq = sb("fq", [128, SEQ], BF16)
        fkp = sb("fkp", [128, 2, SEQ], BF16)
        fvp = sb("fvp", [128, 16, 2, 128], BF16)
        lf = sb("flf", [128, 16, 8], F32)
        lfb = sb("flfb", [128, 128], BF16)
        NF = sb("fNF", [128, 16, 8], F32)
        Cc = sb("fCc", [128, 16, 8], F32)
        NF2 = sb("fNF2", [128, 16, 16], F32)
        NFThl = sb("fNFT", [128, SEQ], BF16)
        sel = sb("fsel", [128, 8, 128], BF16)
        identb = sb("fidb", [128, 128], BF16)
        mneg = sb("fmneg", [128, 128], BF16)
        t32 = sb("ft32", [16, 512], F32)
        h16 = sb("fh16", [16, 512], BF16)
        h32 = sb("fh32", [16, 512], F32)
        fbB = sb("ffbB", [128, 8], F32)
        onec = sb("fonec", [128, 1], F32)
        onesAB = sb("fones", [128, 2, 128], BF16)
        pt = [sb(f"fpt{i}", [128, 512], BF16) for i in range(3)]
        rl = sb("frl", [128, 512], F32)
        P.memset(onec[:], 1.0, w=[P.R("fox.c")])
        P.memset(onesAB[:], 0.0, w=[P.R("fox.c")])
        P.memset(onesAB[:, 0, 0:64], 1.0, w=[P.R("fox.c")])
        P.memset(onesAB[:, 1, 64:128], 1.0, w=[P.R("fox.c")])
        P.memset(fkp[:], 0.0, w=[P.R("fox.kz")])
        P.memset(fvp[:], 0.0, w=[P.R("fox.vz")])
        P.dma("sp", fbB[:], C.dr["bp"][:, l * BPL + BP_FOXB:l * BPL + BP_FOXB + 8], "D_fbB", writes=[P.R("fox.fbB")])
        slot, fres, fsem = C.ring2.next()
        wff = slot[:, 0:64].rearrange("p (kc n) -> p kc n", kc=8)
        wload(C, wff, win[:, :, O_FF:O_FF + 8], fres, fsem)
        for i in range(16):
            ps, pr = bank(C)
            for kc in range(8):
                P.mm(ps[:, 0:8], hT[:, kc, i * 128:(i + 1) * 128], wff[:, kc, :], start=(kc == 0), stop=(kc == 7),
                     r=[fres, P.R("mx.h", i // 4)], w=[pr], inc=(kc == 7))
            P.tt(lf[:, i, :], ps[:, 0:8], fbB[:], ALU.add, r=[pr, P.R("fox.fbB")], w=[P.R("fox.lf")])
        lff = lf[:].rearrange("p j h -> p (j h)")
        P.act(lff, lff, AF.Exp, r=[P.R("fox.lf")], w=[P.R("fox.lf")], scale=-1.0)
        P.act(lff, lff, AF.Ln, r=[P.R("fox.lf"), P.R("fox.c")], w=[P.R("fox.lf")], bias=onec[:])
        P.cp(lfb[:], lff, r=[P.R("fox.lf")], w=[P.R("fox.lfb")])
        psT, prT = bank(C)
        P.mm(psT[:, 0:128], C.onesb[:], lfb[:], r=[P.R("fox.lfb"), P.R("cstb")], w=[prT])
        psL_, prL_ = bank(C)
        P.mm(psL_[:, 0:128], C.tib[:], lfb[:], r=[P.R("fox.lfb"), P.R("cstb")], w=[prL_])
        rC = P.R("fox.Cc")
        P.cp(Cc[:, 0, :], psT[:, 0:8], r=[prT], w=[rC])
        for i in range(1, 16):
            P.tt(Cc[:, i, :], psT[:, i * 8:(i + 1) * 8], Cc[:, i - 1, :], ALU.add, r=[prT, rC], w=[rC])
        rN = P.R("fox.NF")
        P.cp(NF[:, 0, :], psL_[:, 0:8], r=[prL_], w=[rN])
        P.tt(NF[:, 1:16, :].rearrange("p j h -> p (j h)"), psL_[:, 8:128], Cc[:, 0:15, :].rearrange("p j h -> p (j h)"), ALU.add,
             r=[prL_, rC], w=[rN])
        P.ts(NF2[:, :, 0:8], NF[:], -1.0, ALU.mult, r=[rN], w=[P.R("fox.NF2")])
        P.ts(NF2[:, :, 8:16], NF[:], -1.0, ALU.mult, r=[rN], w=[P.R("fox.NF2")])
        P.memset(NFThl[:], 0.0, w=[P.R("fox.NFT")])
        P.cp(sel[:], C.cst[:, C_SEL:C_SEL + 8].unsqueeze(2).to_broadcast([128, 8, 128]), r=[P.R("cst")], w=[P.R("fox.c")])
        P.cp(identb[:], C.ident, r=[P.R("cst")], w=[P.R("fox.c")])
        P.ts(mneg[:], C.cst[:, C_TI:C_TI + 128], -1.0, ALU.add, 30000.0, ALU.mult, r=[P.R("cst")], w=[P.R("fox.c")])
        m0 = C.cst[0:16, C_M01:C_M01 + 1]
        m1 = C.cst[0:16, C_M01 + 1:C_M01 + 2]
        rq = P.R("fox.hl")
        for blk in range(4):
            ps, pr = bank(C)
            for k4 in range(4):
                P.tr(ps[0:16, k4 * 128:(k4 + 1) * 128], NF2[:, blk * 4 + k4, :], C.ident, r=[P.R("fox.NF2"), P.R("cst")], w=[pr])
            P.cp(t32[:], ps[0:16, :], r=[pr], w=[rq], eng="act")
            P.cp(h16[:], t32[:], r=[rq], w=[rq])
            P.cp(h32[:], h16[:], r=[rq], w=[rq])
            P.tt(t32[:], t32[:], h32[:], ALU.subtract, r=[rq], w=[rq])
            P.ts(h32[:], h32[:], m0, ALU.mult, r=[rq, P.R("cst")], w=[rq])
            P.stt(NFThl[0:16, blk * 512:(blk + 1) * 512], t32[:], m1, h32[:], ALU.mult, ALU.add, r=[rq, P.R("cst")], w=[P.R("fox.NFT")])
        pk = 0
        for m in range(4):
            slot, wres, wsem = C.ring8.next()
            wv = slot[:, 0:3072].rearrange("p (kc g n) -> p kc g n", kc=8, g=3)
            for g, off in enumerate((O_FQ, O_FK, O_FV)):
                wload(C, wv[:, :, g, :], win[:, :, off + m * 128:off + (m + 1) * 128], wres, wsem)

            def ev_q(ps, pr, tt):
                P.op("act", lambda e, o=fq[:, tt * 512:(tt + 1) * 512], i=ps[:]: e.mul(out=o, in_=i, mul=0.125), [pr], [P.R("fox.q", tt)])

            def ev_k(ps, pr, tt):
                P.cp(fkp[0:64, 0, tt * 512:(tt + 1) * 512], ps[0:64, :], r=[pr, P.R("fox.kz")], w=[P.R("fox.k", tt)], eng="act")
                P.cp(fkp[64:128, 1, tt * 512:(tt + 1) * 512], ps[64:128, :], r=[pr, P.R("fox.kz")], w=[P.R("fox.k", tt)])
            proj_fm(C, wv[:, :, 0, :], wres, 0, 128, hT, ev_q)
            proj_fm(C, wv[:, :, 1, :], wres, 0, 128, hT, ev_k)
            for i in range(16):
                ps, pr = bank(C)
                for kc in range(8):
                    P.mm(ps[:, 0:128], hT[:, kc, i * 128:(i + 1) * 128], wv[:, kc, 2, :], start=(kc == 0), stop=(kc == 7),
                         r=[wres, P.R("mx.h", i // 4)], w=[pr], inc=(kc == 7))
                P.cp(fvp[:, i, 0, 0:64], ps[:, 0:64], r=[pr, P.R("fox.vz")], w=[P.R("fox.v", i)], eng="act")
                P.cp(fvp[:, i, 1, 64:128], ps[:, 64:128], r=[pr, P.R("fox.vz")], w=[P.R("fox.v", i)])
            for qg in range(4):
                par = (m * 4 + qg) % 2
                psO, prO = C.ps[4 + par], P.R("ps", 4 + par)
                psL, prL = C.ps[6 + par], P.R("ps", 6 + par)
                nj = 4 * qg + 4
                its = [(hh, j) for hh in range(2) for j in range(nj)]
                pend = {}

                def emit_S(k):
                    hh, j = its[k]
                    i_lo = max(j, 4 * qg)
                    ncol = (nj - i_lo) * 128
                    c0 = (i_lo - 4 * qg) * 128
                    psS, prS = bank(C)
                    diag = (j >= 4 * qg)
                    P.mm(psS[:, c0:c0 + ncol], fkp[:, hh, j * 128:(j + 1) * 128], fq[:, i_lo * 128:nj * 128],
                         start=True, stop=False, r=[P.R("fox.k", j // 4), P.R("fox.q", qg)], w=[prS], inc=False)
                    P.mm(psS[:, c0:c0 + ncol], sel[:, 2 * m + hh, :], NFThl[:, i_lo * 128:nj * 128],
                         start=False, stop=(not diag), r=[P.R("fox.c"), P.R("fox.NFT")], w=[prS], inc=(not diag))
                    if diag:
                        cd = (j - 4 * qg) * 128
                        P.mm(psS[:, cd:cd + 128], identb[:], mneg[:], start=False, stop=True, r=[P.R("fox.c")], w=[prS])
                    pend[k] = (psS, prS, c0, ncol)
                LOOK = 2
                for k in range(min(LOOK, len(its))):
                    emit_S(k)
                for k, (hh, j) in enumerate(its):
                    if k + LOOK < len(its):
                        emit_S(k + LOOK)
                    h = 2 * m + hh
                    psS, prS, c0, ncol = pend.pop(k)
                    ptile = pt[pk % 3]
                    rpt = P.R("fox.pt", pk % 3)
                    pk += 1
                    P.act(ptile[:, c0:c0 + ncol], psS[:, c0:c0 + ncol], AF.Exp, r=[prS, rN], w=[rpt],
                          bias=NF[:, j, h:h + 1])
                    first = (hh == 0 and j == 0)
                    last = (hh == 1 and j == nj - 1)
                    P.mm(psO[:, c0:c0 + ncol], fvp[:, j, hh, :], ptile[:, c0:c0 + ncol], start=first, stop=last,
                         r=[P.R("fox.v", j), rpt], w=[prO])
                    P.mm(psL[:, c0:c0 + ncol], onesAB[:, hh, :], ptile[:, c0:c0 + ncol], start=first, stop=last,
                         r=[P.R("fox.c"), rpt], w=[prL])
                P.act(rl[:], psL[:], AF.Ln, r=[prL], w=[P.R("fox.rl")])
                P.act(rl[:], rl[:], AF.Exp, r=[P.R("fox.rl")], w=[P.R("fox.rl")], scale=-1.0)
                P.tt(foT[:, m, qg * 512:(qg + 1) * 512], psO[:], rl[:], ALU.mult, r=[prO, P.R("fox.rl")], w=[P.R("mx.fo", m, qg)])
    P.free("fox")


def merge(C, l, win, hT, s5T, glaT, foT):
    P, nc = C.P, C.nc
    wgla = C.dr["w_gla_up"][l].rearrange("(kc p) n -> p kc n", p=128)
    ws5 = C.dr["w_s5_up"][l].rearrange("(kc p) n -> p kc n", p=128)
    wfox = C.dr["w_fox_up"][l].rearrange("(kc p) n -> p kc n", p=128)
    wmo = C.dr["w_mix_out"][l].rearrange("(kc p) n -> p kc n", p=128)
    srcs = ((glaT, 2, "mx.gla", wgla, 0), (s5T, 2, "mx.s5", ws5, 2), (foT, 4, "mx.fo", wfox, 4))
    with ExitStack() as es:
        mixT = es.enter_context(sbt(nc, "mixT", [128, 8, 1024], BF16))
        for half in range(2):
            with ExitStack() as ea:
                sig = [ea.enter_context(sbt(nc, f"msig{i}", [128, 512], F32)) for i in range(2)]
                acc = [ea.enter_context(sbt(nc, f"macc{i}", [128, 512], F32)) for i in range(2)]
                tm = ea.enter_context(sbt(nc, "mtm", [128, 512], F32))
                it = 0
                for c in range(8):
                    slot, wres, wsem = C.ring8.next()
                    wv = slot[:].rearrange("p (k n) -> p k n", k=32)
                    for (srcT, nk, rname, wd, k0) in srcs:
                        wload(C, wv[:, k0:k0 + nk, :], wd[:, :, c * 128:(c + 1) * 128], wres, wsem)
                    for bi in range(3):
                        g0 = O_GATES + bi * 1024 + c * 128
                        wload(C, wv[:, 8 + 8 * bi:16 + 8 * bi, :], win[:, :, g0:g0 + 128], wres, wsem)
                    for t2 in range(2):
                        tt = half * 2 + t2
                        tl = slice(tt * 512, (tt + 1) * 512)
                        a = it % 2
                        racc = P.R("mg.acc", a)
                        it += 1
                        for bi, (srcT, nk, rname, wd, k0) in enumerate(srcs):
                            psU, prU = bank(C, 0, 8)
                            for kc in range(nk):
                                P.mm(psU[:], wv[:, k0 + kc, :], srcT[:, kc, tl], start=(kc == 0), stop=(kc == nk - 1),
                                     r=[wres, P.R(rname, kc, tt)], w=[prU], inc=(kc == nk - 1))
                            psG, prG = bank(C, 0, 8)
                            for kc in range(8):
                                P.mm(psG[:], wv[:, 8 + 8 * bi + kc, :], hT[:, kc, tl], start=(kc == 0), stop=(kc == 7),
                                     r=[wres, P.R("mx.h", tt)], w=[prG], inc=(kc == 7))
                            sg = sig[bi % 2]
                            rsg = P.R("mg.sig", bi % 2)
                            P.act(sg[:], psG[:], AF.Sigmoid, r=[prG], w=[rsg])
                            if bi == 0:
                                P.tt(acc[a][:], psU[:], sg[:], ALU.mult, r=[prU, rsg], w=[racc])
                            elif bi == 1:
                                P.tt(tm[:], psU[:], sg[:], ALU.mult, r=[prU, rsg], w=[P.R("mg.tm")])
                                P.tt(acc[a][:], acc[a][:], tm[:], ALU.add, r=[racc, P.R("mg.tm")], w=[racc])
                            else:
                                P.tt(tm[:], psU[:], sg[:], ALU.mult, r=[prU, rsg], w=[P.R("mg.tm")])
                                P.tt(mixT[:, c, t2 * 512:(t2 + 1) * 512], acc[a][:], tm[:], ALU.add,
                                     r=[racc, P.R("mg.tm")], w=[P.R("mg.mix", t2)])
            P.free("mg.sig")
            P.free("mg.acc")
            P.free("mg.tm")
            with ExitStack() as eb:
                ysb = eb.enter_context(sbt(nc, "ymx", [128, 8, 512], F32))
                sq = eb.enter_context(sbt(nc, "sq", [128, 2, 512], BF16))
                rs = eb.enter_context(sbt(nc, "rs", [128, 512], F32))
                for t2 in range(2):
                    tt = half * 2 + t2
                    for ob in range(2):
                        slot, sres, ssem = C.ring8.next()
                        sv = slot[:].rearrange("p (kc n) -> p kc n", kc=8)
                        wload(C, sv, wmo[:, :, ob * 512:(ob + 1) * 512], sres, ssem)
                        for m4 in range(4):
                            mc = ob * 4 + m4
                            ps, pr = bank(C, 0, 8)
                            for kc in range(8):
                                P.mm(ps[:], sv[:, kc, m4 * 128:(m4 + 1) * 128], mixT[:, kc, t2 * 512:(t2 + 1) * 512],
                                     start=(kc == 0), stop=(kc == 7), r=[sres, P.R("mg.mix", t2)], w=[pr], inc=(kc == 7))
                            P.cp(ysb[:, mc, :], ps[:], r=[pr], w=[P.R("mg.y")], eng=("act" if mc % 2 else "dve"))
                    postnorm_add(C, l, SP_MIX_POST, ysb[:], 1.0, tt, sq, rs, P.R("mg.y"))
            P.free("mg.y")
            P.free("nrm")
    P.free("mg")
    P.free("nrm")


def gelu_tanh(C, out, x, t1, t2, r, w, rt):
    P = C.P
    P.act(t1, x, AF.Square, r=r, w=[rt], scale=0.044715 ** 0.5)
    P.stt(t1, t1, 1.0, x, ALU.add, ALU.mult, r=[rt] + list(r), w=[rt])
    P.act(t1, t1, AF.Tanh, r=[rt], w=[rt], scale=0.7978845608028654)
    P.op("act", lambda e: e.mul(out=t2, in_=x, mul=0.5), list(r), [rt])
    P.stt(out, t1, 1.0, t2, ALU.add, ALU.mult, r=[rt], w=w)

def _cols(v):
    v = np.asarray(v, np.float32)
    return np.ascontiguousarray(v.reshape(-1, 128).T)


def make_consts():
    c = np.zeros((128, NCONST), np.float32)
    p = np.arange(128)
    c[:, C_ID:C_ID + 128] = np.eye(128)
    c[:, C_TI:C_TI + 128] = (p[:, None] <= p[None, :])
    same = (p[:, None] // 64) == (p[None, :] // 64)
    c[:, C_MU:C_MU + 128] = same & (p[:, None] <= p[None, :])
    c[:, C_ML:C_ML + 128] = same & (p[:, None] > p[None, :])
    c[:, C_BO:C_BO + 128] = same
    c[:, C_HM:C_HM + 4] = (p[:, None] // 32) == np.arange(4)[None, :]
    c[:, C_SM:C_SM + 256] = (p[:, None] // 32) == (np.arange(256)[None, :] // 64)
    c[:, C_IT:C_IT + 128] = p[None, :]
    c[:, C_IP] = p
    c[:, C_SEL:C_SEL + 8] = (p[:, None] == np.arange(8)[None, :]) | (p[:, None] == 8 + np.arange(8)[None, :])
    c[:, C_M01] = p < 8
    c[:, C_M01 + 1] = (p >= 8) & (p < 16)
    return c


def pack_small(inp):
    sp = np.zeros((128, DEPTH * SPL), np.float32)
    bp = np.zeros((128, DEPTH * BPL), np.float32)
    for l in range(DEPTH):
        b = l * SPL
        for off, name in ((SP_FFN1_PRE, "ffn1_pre_g"), (SP_FFN1_POST, "ffn1_post_g"), (SP_MIX_PRE, "mix_pre_g"),
                          (SP_MIX_POST, "mix_post_g"), (SP_XA_PRE, "xa_pre_g"), (SP_XA_MEM, "xa_mem_g"),
                          (SP_XA_POST, "xa_post_g"), (SP_FFN2_PRE, "ffn2_pre_g"), (SP_FFN2_POST, "ffn2_post_g")):
            sp[:, b + off:b + off + 8] = _cols(inp[name][l])
        sp[:, b + SP_GLA_B:b + SP_GLA_B + 1] = _cols(inp["gla_gate_b"][l])
        sp[:, b + SP_GLA_NG:b + SP_GLA_NG + 2] = _cols(inp["gla_norm_g"][l])
        sp[:, b + SP_S5_D:b + SP_S5_D + 2] = _cols(inp["s5_d"][l])
        sp[:, b + SP_GLU_B:b + SP_GLU_B + 2] = _cols(inp["s5_glu_b"][l])
        sp[:, b + SP_ARE:b + SP_ARE + 8] = _cols(inp["s5_a_re"][l].reshape(-1))
        sp[:, b + SP_AIM:b + SP_AIM + 8] = _cols(inp["s5_a_im"][l].reshape(-1))
        ldt = np.repeat(np.asarray(inp["s5_log_dt"][l], np.float32), 64)
        sp[:, b + SP_LDT:b + SP_LDT + 8] = _cols(ldt)
        q = l * BPL
        bp[:, q + BP_FOXB:q + BP_FOXB + 8] = np.asarray(inp["fox_f_b"][l], np.float32)[None, :]
        bp[:, q + BP_GLAB:q + BP_GLAB + 128] = np.asarray(inp["gla_gate_b"][l], np.float32)[None, :]
        bp[:, q + BP_ARE:q + BP_ARE + 1024] = np.asarray(inp["s5_a_re"][l], np.float32).reshape(1, -1)
        bp[:, q + BP_AIM:q + BP_AIM + 1024] = np.asarray(inp["s5_a_im"][l], np.float32).reshape(1, -1)
        bp[:, q + BP_LDT:q + BP_LDT + 1024] = ldt[None, :]
    return sp, bp


def pack_s5(inp):
    bb_re = np.zeros((DEPTH, 256, 1024), np.float32)
    bb_im = np.zeros((DEPTH, 256, 1024), np.float32)
    cc_re = np.zeros((DEPTH, 1024, 256), np.float32)
    cc_im = np.zeros((DEPTH, 1024, 256), np.float32)
    for g in range(16):
        bb_re[:, g * 16:(g + 1) * 16, g * 64:(g + 1) * 64] = np.transpose(inp["s5_b_re"][:, g], (0, 2, 1))
        bb_im[:, g * 16:(g + 1) * 16, g * 64:(g + 1) * 64] = np.transpose(inp["s5_b_im"][:, g], (0, 2, 1))
        cc_re[:, g * 64:(g + 1) * 64, g * 16:(g + 1) * 16] = np.transpose(inp["s5_c_re"][:, g], (0, 2, 1))
        cc_im[:, g * 64:(g + 1) * 64, g * 16:(g + 1) * 16] = np.transpose(inp["s5_c_im"][:, g], (0, 2, 1))
    return bb_re, bb_im, cc_re, cc_im


ALL_STAGES = [(k, l) for l in range(DEPTH) for k in ("ffn1", "mix", "xa", "ffn2")]
BIG = ("ffn1_w_gu", "ffn1_w_down", "ffn2_w_gu", "ffn2_w_down", "w_in", "gla_gate_w", "w_gla_up", "s5_glu_w",
       "w_s5_up", "w_fox_up", "w_mix_out", "xa_w_q", "xa_w_kv", "xa_w_o")


def make_in_maps(inp, cores):
    sp, bp = pack_small(inp)
    bb_re, bb_im, cc_re, cc_im = pack_s5(inp)
    consts = make_consts()
    shared = {k: np.ascontiguousarray(np.asarray(inp[k], np.float32)) for k in BIG}
    gwp = np.zeros((DEPTH, 128, 128), np.float32)
    gwp[:, 0:16, :] = np.asarray(inp["gla_gate_w"], np.float32)
    shared["gla_gate_w"] = gwp
    shared.update(bb_re=bb_re, bb_im=bb_im, cc_re=cc_re, cc_im=cc_im, sp=sp, bp=bp, consts=consts)
    maps = []
    for b in cores:
        m = dict(shared)
        m["x"] = np.ascontiguousarray(np.asarray(inp["x"][b], np.float32))
        m["mem"] = np.ascontiguousarray(np.asarray(inp["mem"][b], np.float32))
        maps.append(m)
    return maps


def kernel(**inputs):
    nc, C = build_program(ALL_STAGES)
    maps = make_in_maps(inputs, list(range(8)))
    res = run_bass_kernel_spmd(nc, maps, core_ids=list(range(8)))
    return np.stack([r["y"] for r in res.results], axis=0).astype(np.float32)
```

```python
import numpy as np
from contextlib import ExitStack
import concourse.bass as bass
import concourse.mybir as mybir
from concourse.bass_utils import run_bass_kernel_spmd

F32 = mybir.dt.float32
BF16 = mybir.dt.bfloat16
AF = mybir.ActivationFunctionType
ALU = mybir.AluOpType

DEPTH = 4
D = 1024
SEQ = 2048
NMEM = 256
DFF = 2816
DIN = 5656
EPS = 1e-6
O_GQ, O_GK, O_GV, O_GR, O_GD, O_SU, O_FQ, O_FK, O_FV, O_FF, O_GATES = 0, 128, 256, 512, 768, 784, 1040, 1552, 2064, 2576, 2584

SP_FFN1_PRE, SP_FFN1_POST, SP_MIX_PRE, SP_MIX_POST, SP_XA_PRE, SP_XA_MEM, SP_XA_POST, SP_FFN2_PRE, SP_FFN2_POST = 0, 8, 16, 24, 32, 40, 48, 56, 64
SP_GLA_B, SP_GLA_NG, SP_S5_D, SP_GLU_B, SP_ARE, SP_AIM, SP_LDT = 72, 73, 75, 77, 79, 87, 95
SPL = 103
BP_FOXB, BP_GLAB, BP_ARE, BP_AIM, BP_LDT = 0, 8, 136, 1160, 2184
BPL = 3208
C_ID, C_TI, C_MU, C_ML, C_BO, C_HM, C_SM, C_IT, C_IP, C_SEL, C_M01 = 0, 128, 256, 384, 512, 640, 644, 900, 1028, 1029, 1037
NCONST = 1039

ENGS = ("pe", "act", "dve", "pool", "sp")


class Res:
    __slots__ = ("w", "rs", "excl")

    def __init__(self, rs):
        self.w = None
        self.rs = rs
        self.excl = False


class Tk:
    __slots__ = ("key", "val", "clk")

    def __init__(self, key, val, clk):
        self.key = key
        self.val = val
        self.clk = clk


class Prog:
    def __init__(self, nc):
        self.nc = nc
        self.q = {e: [] for e in ENGS}
        self.sems = {}
        self.cnt = {}
        self.known = {e: {} for e in ENGS}
        self.res = {}
        self.grave = {}
        self.pe_pending = []
        self.n_wait = 0
        self.n_op = 0
        for e in ENGS:
            self._sem("E_" + e)

    def _sem(self, key):
        if key not in self.sems:
            self.sems[key] = self.nc.alloc_semaphore(key)
            self.cnt[key] = 0
        return self.sems[key]

    def R(self, *key):
        r = self.res.get(key)
        if r is None:
            r = Res(list(self.grave.values()))
            r.excl = (key[0] == "ps")
            self.res[key] = r
        return r

    def free(self, prefix):
        dead = [k for k in self.res if k[0] == prefix or (isinstance(k[0], str) and k[0].startswith(prefix + "."))]
        for k in dead:
            r = self.res.pop(k)
            for t in ([r.w] if r.w is not None else []) + r.rs:
                g = self.grave.get(t.key)
                if g is None or g.val < t.val:
                    self.grave[t.key] = t

    def _deps(self, eng, reads, writes):
        if eng != "pe" and self.pe_pending:
            for (rr, ww) in self.pe_pending:
                for w in writes:
                    assert all(w is not x for x in rr) and all(w is not x for x in ww), "write to resource with pending PE access"
                for r in reads:
                    assert all(r is not x for x in ww), "read of resource with pending PE write"
        tks = []
        own = "E_" + eng
        for r in reads:
            if r.w is not None:
                tks.append(r.w)
            if r.excl:
                tks.extend(t for t in r.rs if t.key != own)
        for w in writes:
            if w.w is not None:
                tks.append(w.w)
            tks.extend(w.rs)
        kn = self.known[eng]
        need = {}
        for t in tks:
            if eng == "pe" and t.key == "E_pe":
                continue
            if kn.get(t.key, 0) >= t.val:
                continue
            o = need.get(t.key)
            if o is None or o.val < t.val:
                need[t.key] = t
        for key, t in need.items():
            if kn.get(key, 0) >= t.val:
                continue
            self.q[eng].append(("wait", key, t.val))
            self.n_wait += 1
            for k2, v2 in t.clk.items():
                if kn.get(k2, 0) < v2:
                    kn[k2] = v2
            kn[key] = max(kn.get(key, 0), t.val)

    def op(self, eng, fn, reads=(), writes=(), inc=True):
        self._deps(eng, reads, writes)
        self.n_op += 1
        if not inc:
            self.q[eng].append(("op", fn, None, 0))
            self.pe_pending.append((list(reads), list(writes)))
            return None
        key = "E_" + eng
        self.cnt[key] += 1
        clk = dict(self.known[eng])
        clk[key] = self.cnt[key]
        tk = Tk(key, self.cnt[key], clk)
        self.q[eng].append(("op", fn, key, 1))
        allr = [list(reads)]
        allw = [list(writes)]
        if eng == "pe" and self.pe_pending:
            for (rr, ww) in self.pe_pending:
                allr.append(rr)
                allw.append(ww)
            self.pe_pending = []
        for ww in allw:
            for w in ww:
                w.w = tk
                w.rs = []
        for rr in allr:
            for r in rr:
                if r.w is not tk:
                    r.rs.append(tk)
        return tk

    def dma(self, eng, out, in_, semkey, reads=(), writes=()):
        self._deps(eng, reads, writes)
        self._sem(semkey)
        self.cnt[semkey] += 16
        clk = dict(self.known[eng])
        clk[semkey] = self.cnt[semkey]
        tk = Tk(semkey, self.cnt[semkey], clk)
        self.q[eng].append(("op", lambda e: e.dma_start(out=out, in_=in_), semkey, 16))
        for w in writes:
            w.w = tk
            w.rs = []
        for r in reads:
            r.rs.append(tk)
        return tk

    def mm(self, out, lhsT, rhs, start=True, stop=True, r=(), w=(), inc=True):
        return self.op("pe", lambda e: e.matmul(out, lhsT=lhsT, rhs=rhs, start=start, stop=stop), r, w, inc)

    def tr(self, out, in_, ident, r=(), w=()):
        return self.op("pe", lambda e: e.transpose(out, in_, ident), r, w, True)

    def act(self, out, in_, func, r=(), w=(), bias=None, scale=None):
        kw = {}
        if bias is not None:
            kw["bias"] = bias
        if scale is not None:
            kw["scale"] = scale
        return self.op("act", lambda e: e.activation(out=out, in_=in_, func=func, **kw), r, w)

    def tt(self, out, in0, in1, op, r=(), w=(), eng="dve"):
        return self.op(eng, lambda e: e.tensor_tensor(out=out, in0=in0, in1=in1, op=op), r, w)

    def ts(self, out, in0, s1, op0, s2=None, op1=None, r=(), w=(), eng="dve"):
        if op1 is None:
            return self.op(eng, lambda e: e.tensor_scalar(out=out, in0=in0, scalar1=s1, scalar2=None, op0=op0), r, w)
        return self.op(eng, lambda e: e.tensor_scalar(out=out, in0=in0, scalar1=s1, scalar2=s2, op0=op0, op1=op1), r, w)

    def stt(self, out, in0, scalar, in1, op0, op1, r=(), w=(), eng="dve"):
        return self.op(eng, lambda e: e.scalar_tensor_tensor(out=out, in0=in0, scalar=scalar, in1=in1, op0=op0, op1=op1), r, w)

    def cp(self, out, in_, r=(), w=(), eng="dve"):
        if eng == "act":
            return self.op("act", lambda e: e.copy(out=out, in_=in_), r, w)
        return self.op(eng, lambda e: e.tensor_copy(out=out, in_=in_), r, w)

    def recip(self, out, in_, r=(), w=()):
        return self.op("dve", lambda e: e.reciprocal(out=out, in_=in_), r, w)

    def memset(self, ap, val, w=(), eng="dve"):
        return self.op(eng, lambda e: e.memset(ap, val), (), w)

    def finish(self):
        nc = self.nc
        sems = self.sems
        q = self.q

        def replay(name):
            def body(e):
                for it in q[name]:
                    if it[0] == "wait":
                        e.wait_ge(sems[it[1]], it[2])
                    else:
                        ins = it[1](e)
                        if it[2] is not None:
                            ins.then_inc(sems[it[2]], it[3])
            return body

        with nc.Block() as block:
            block.sync(replay("sp"))
            block.tensor(replay("pe"))
            block.scalar(replay("act"))
            block.vector(replay("dve"))
            block.gpsimd(replay("pool"))


class Ring:
    def __init__(self, P, name, tiles):
        self.P = P
        self.name = name
        self.tiles = tiles
        self.i = 0

    def next(self):
        k = self.i % len(self.tiles)
        self.i += 1
        return self.tiles[k], self.P.R(self.name, k), f"D_{self.name}{k}"


class Ctx:
    pass


_UID = [0]


def sbt(nc, name, shape, dt):
    _UID[0] += 1
    return nc.sbuf_tensor(f"{name}_{_UID[0]}", list(shape), dt)


def build_program(stages, dbg=None, branches=("s5", "gla", "fox"), dbg_branch=None, gla_ntiles=16, gla_stop=99):
    nc = bass.Bass("TRN2", target_bir_lowering=False)
    P = Prog(nc)
    C = Ctx()
    C.nc, C.P = nc, P
    C.branches, C.dbg_branch = branches, dbg_branch
    C.gla_ntiles = gla_ntiles
    C.gla_stop = gla_stop
    dr = {}

    def din(name, shape):
        dr[name] = nc.dram_tensor(name, list(shape), F32, kind="ExternalInput").ap()
        return dr[name]

    din("x", [SEQ, D])
    din("mem", [NMEM, D])
    for w in (1, 2):
        din(f"ffn{w}_w_gu", [DEPTH, D, 2 * DFF])
        din(f"ffn{w}_w_down", [DEPTH, DFF, D])
    din("w_in", [DEPTH, D, DIN])
    din("gla_gate_w", [DEPTH, 128, 128])
    din("w_gla_up", [DEPTH, 256, D])
    din("s5_glu_w", [DEPTH, 256, 256])
    din("w_s5_up", [DEPTH, 256, D])
    din("w_fox_up", [DEPTH, 512, D])
    din("w_mix_out", [DEPTH, D, D])
    din("xa_w_q", [DEPTH, D, D])
    din("xa_w_kv", [DEPTH, D, 2 * D])
    din("xa_w_o", [DEPTH, D, D])
    din("bb_re", [DEPTH, 256, 1024])
    din("bb_im", [DEPTH, 256, 1024])
    din("cc_re", [DEPTH, 1024, 256])
    din("cc_im", [DEPTH, 1024, 256])
    din("sp", [128, DEPTH * SPL])
    din("bp", [128, DEPTH * BPL])
    din("consts", [128, NCONST])
    y = nc.dram_tensor("y", [SEQ, D], F32, kind="ExternalOutput").ap()
    C.dr = dr
    if dbg is not None:
        C.dbg = nc.dram_tensor("dbg", list(dbg), F32, kind="ExternalOutput").ap()

    with ExitStack() as es:
        def sb(name, shape, dt):
            return es.enter_context(sbt(nc, name, shape, dt))

        C.xT = sb("xT", [128, 8, SEQ], F32)
        C.cst = sb("cst", [128, NCONST], F32)
        C.sp = sb("spar", [128, DEPTH * SPL], F32)
        C.onesb = sb("onesb", [128, 128], BF16)
        C.tib = sb("tib", [128, 128], BF16)
        C.mub = sb("mub", [128, 2, 128], BF16)
        C.bob = sb("bob", [128, 128], BF16)
        C.smb = sb("smb", [128, 256], BF16)
        C.epsc = sb("epsc", [128, 1], F32)
        C.ring8 = Ring(P, "r8", [sb(f"r8_{i}", [128, 4096], BF16) for i in range(3)])
        C.ring2 = Ring(P, "r2", [sb(f"r2_{i}", [128, 1024], BF16) for i in range(3)])
        C.ps = [es.enter_context(nc.psum_tensor(f"ps{i}", [128, 512], F32)) for i in range(8)]
        C.psi = 0

        P.dma("sp", C.cst[:], dr["consts"], "D_c0", writes=[P.R("cst")])
        P.dma("sp", C.sp[:], dr["sp"], "D_c1", writes=[P.R("spar")])
        P.memset(C.onesb[:], 1.0, w=[P.R("cstb")])
        P.memset(C.epsc[:], EPS, w=[P.R("cstb")])
        P.cp(C.tib[:], C.cst[:, C_TI:C_TI + 128], r=[P.R("cst")], w=[P.R("cstb")])
        P.cp(C.mub[:, 0, :], C.cst[:, C_MU:C_MU + 128], r=[P.R("cst")], w=[P.R("cstb")])
        P.cp(C.mub[:, 1, :], C.cst[:, C_ML:C_ML + 128], r=[P.R("cst")], w=[P.R("cstb")])
        P.cp(C.bob[:], C.cst[:, C_BO:C_BO + 128], r=[P.R("cst")], w=[P.R("cstb")])
        P.cp(C.smb[:], C.cst[:, C_SM:C_SM + 256], r=[P.R("cst")], w=[P.R("cstb")])
        C.ident = C.cst[:, C_ID:C_ID + 128]

        load_x(C)
        for (kind, l) in stages:
            if kind == "ffn1":
                ffn(C, l, 1)
            elif kind == "ffn2":
                ffn(C, l, 2)
            elif kind == "xa":
                xattn(C, l)
            elif kind == "mix":
                mixer(C, l)
        store_x(C, y)
        P.finish()
    C.stats = (P.n_op, P.n_wait)
    return nc, C


def bank(C, lo=0, hi=4):
    k = lo + (C.psi % (hi - lo))
    C.psi += 1
    return C.ps[k], C.P.R("ps", k)


def load_x(C):
    P, nc = C.P, C.nc
    xd = C.dr["x"]
    with ExitStack() as es:
        st = [es.enter_context(sbt(nc, f"xst{i}", [128, D], F32)) for i in range(2)]
        for i in range(16):
            s = i % 2
            P.dma("sp", st[s][:], xd[i * 128:(i + 1) * 128, :], f"D_xs{s}", writes=[P.R("xst", s)])
            for g in range(2):
                ps, pr = bank(C)
                for c4 in range(4):
                    c = g * 4 + c4
                    P.tr(ps[:, c4 * 128:(c4 + 1) * 128], st[s][:, c * 128:(c + 1) * 128], C.ident,
                         r=[P.R("xst", s), P.R("cst")], w=[pr])
                P.cp(C.xT[:, g * 4:(g + 1) * 4, i * 128:(i + 1) * 128],
                     ps[:].rearrange("p (c t) -> p c t", c=4), r=[pr], w=[P.R("x", i // 4)],
                     eng=("act" if g else "dve"))
        P.free("xst")


def store_x(C, y):
    P, nc = C.P, C.nc
    with ExitStack() as es:
        st = [es.enter_context(sbt(nc, f"yst{i}", [128, D], F32)) for i in range(2)]
        for i in range(16):
            s = i % 2
            for g in range(2):
                ps, pr = bank(C)
                for c4 in range(4):
                    c = g * 4 + c4
                    P.tr(ps[:, c4 * 128:(c4 + 1) * 128], C.xT[:, c, i * 128:(i + 1) * 128], C.ident,
                         r=[P.R("x", i // 4), P.R("cst")], w=[pr])
                P.cp(st[s][:, g * 512:(g + 1) * 512], ps[:], r=[pr], w=[P.R("yst", s)],
                     eng=("act" if g else "dve"))
            P.dma("sp", y[i * 128:(i + 1) * 128, :], st[s][:], f"D_ys{s}", reads=[P.R("yst", s)])
        for s in range(2):
            key = f"D_ys{s}"
            P.q["sp"].append(("wait", key, P.cnt[key]))
        P.free("yst")


def gcol(C, l, off, c):
    return C.sp[:, l * SPL + off + c:l * SPL + off + c + 1]


def norm_stats(C, src3, ntok, sq, rs, r, rres):
    P = C.P
    ps, pr = bank(C)
    for c in range(8):
        k = c % 2
        P.act(sq[:, k, :ntok], src3[:, c, :], AF.Square, r=r, w=[P.R("nrm.sq", k)])
        P.mm(ps[:, :ntok], C.onesb[:], sq[:, k, :ntok], start=(c == 0), stop=(c == 7),
             r=[P.R("nrm.sq", k), P.R("cstb")], w=[pr], inc=True)
    P.act(rs[:, :ntok], ps[:, :ntok], AF.Ln, r=[pr, P.R("cstb")], w=[rres], bias=C.epsc[:], scale=1.0 / D)
    P.act(rs[:, :ntok], rs[:, :ntok], AF.Exp, r=[rres], w=[rres], scale=-0.5)


def prenorm(C, l, goff, src3, ntok, dst3, sq, rs, rsrc, rdst, gsp=None):
    P = C.P
    rres = P.R("nrm.rs")
    norm_stats(C, src3, ntok, sq, rs, rsrc, rres)
    for c in range(8):
        P.stt(dst3[:, c, :], src3[:, c, :], gcol(C, l, goff, c), rs[:, :ntok], ALU.mult, ALU.mult,
              r=list(rsrc) + [rres, P.R("spar")], w=rdst)


def postnorm_add(C, l, goff, ysb3, wgt, tt, sq, rs, ryres):
    P = C.P
    rres = P.R("nrm.rs")
    norm_stats(C, ysb3, 512, sq, rs, [ryres], rres)
    for c in range(8):
        P.stt(ysb3[:, c, :], ysb3[:, c, :], gcol(C, l, goff, c), rs[:, :512], ALU.mult, ALU.mult,
              r=[ryres, rres, P.R("spar")], w=[ryres])
    xt = C.xT[:, :, tt * 512:(tt + 1) * 512]
    P.stt(xt, ysb3, float(wgt), xt, ALU.mult, ALU.add, r=[ryres, P.R("x", tt)], w=[P.R("x", tt)])


def wload(C, dst, src, res, sem):
    C.P.dma("pool", dst, src, sem, writes=[res])


def ffn(C, l, which):
    P, nc = C.P, C.nc
    wgu = C.dr[f"ffn{which}_w_gu"][l].rearrange("(kc p) n -> p kc n", p=128)
    wdn = C.dr[f"ffn{which}_w_down"][l].rearrange("(j p) n -> p j n", p=128)
    pre = SP_FFN1_PRE if which == 1 else SP_FFN2_PRE
    post = SP_FFN1_POST if which == 1 else SP_FFN2_POST
    for half in range(2):
        with ExitStack() as es:
            aT = es.enter_context(sbt(nc, "aT", [128, 22, 1024], BF16))
            sq = es.enter_context(sbt(nc, "sq", [128, 2, 512], BF16))
            rs = es.enter_context(sbt(nc, "rs", [128, 512], F32))
            with ExitStack() as es1:
                hT = es1.enter_context(sbt(nc, "hT", [128, 8, 1024], BF16))
                sg = [es1.enter_context(sbt(nc, f"sg{i}", [128, 512], F32)) for i in range(2)]
                for t2 in range(2):
                    tt = half * 2 + t2
                    prenorm(C, l, pre, C.xT[:, :, tt * 512:(tt + 1) * 512], 512, hT[:, :, t2 * 512:(t2 + 1) * 512],
                            sq, rs, [P.R("x", tt)], [P.R("ffn.h", t2)])
                k = 0
                for jb in range(11):
                    slot, sres, ssem = C.ring8.next()
                    sv = slot[:].rearrange("p (kc g n) -> p kc g n", kc=8, g=2)
                    wload(C, sv[:, :, 0, :], wgu[:, :, jb * 256:(jb + 1) * 256], sres, ssem)
                    wload(C, sv[:, :, 1, :], wgu[:, :, DFF + jb * 256:DFF + (jb + 1) * 256], sres, ssem)
                    for jj in range(2):
                        j = jb * 2 + jj
                        for t2 in range(2):
                            psA, prA = bank(C)
                            psB, prB = bank(C)
                            for g, (ps_, pr_) in enumerate(((psA, prA), (psB, prB))):
                                for kc in range(8):
                                    P.mm(ps_[:], sv[:, kc, g, jj * 128:(jj + 1) * 128], hT[:, kc, t2 * 512:(t2 + 1) * 512],
                                         start=(kc == 0), stop=(kc == 7), r=[sres, P.R("ffn.h", t2)], w=[pr_], inc=(kc == 7))
                            s = k % 2
                            k += 1
                            P.act(sg[s][:], psA[:], AF.Silu, r=[prA], w=[P.R("ffn.sg", s)])
                            P.tt(aT[:, j, t2 * 512:(t2 + 1) * 512], psB[:], sg[s][:], ALU.mult,
                                 r=[prB, P.R("ffn.sg", s)], w=[P.R("ffn.a", j, t2)])
            P.free("ffn.h")
            P.free("ffn.sg")
            ysb = es.enter_context(sbt(nc, "ysb", [128, 8, 1024], F32))
            for mq in range(4):
                acc = [[(C.ps[4 + m2 * 2 + t2], P.R("ps", 4 + m2 * 2 + t2)) for t2 in range(2)] for m2 in range(2)]
                for jb in range(11):
                    slot, sres, ssem = C.ring2.next()
                    sv = slot[:, 0:512].rearrange("p (j n) -> p j n", j=2)
                    wload(C, sv, wdn[:, jb * 2:jb * 2 + 2, mq * 256:(mq + 1) * 256], sres, ssem)
                    for jj in range(2):
                        j = jb * 2 + jj
                        for m2 in range(2):
                            for t2 in range(2):
                                P.mm(acc[m2][t2][0][:], sv[:, jj, m2 * 128:(m2 + 1) * 128], aT[:, j, t2 * 512:(t2 + 1) * 512],
                                     start=(j == 0), stop=(j == 21), r=[sres, P.R("ffn.a", j, t2)], w=[acc[m2][t2][1]],
                                     inc=(j == 21 or (jj == 1 and m2 == 1 and t2 == 1)))
                for m2 in range(2):
                    for t2 in range(2):
                        P.cp(ysb[:, mq * 2 + m2, t2 * 512:(t2 + 1) * 512], acc[m2][t2][0][:], r=[acc[m2][t2][1]],
                             w=[P.R("ffn.y", t2)], eng=("act" if (m2 + t2) % 2 else "dve"))
            for t2 in range(2):
                postnorm_add(C, l, post, ysb[:, :, t2 * 512:(t2 + 1) * 512], 0.5, half * 2 + t2, sq, rs, P.R("ffn.y", t2))
        P.free("ffn")
        P.free("nrm")


def xattn(C, l):
    P, nc = C.P, C.nc
    wq_d = C.dr["xa_w_q"][l].rearrange("(kc p) n -> p kc n", p=128)
    wkv_d = C.dr["xa_w_kv"][l].rearrange("(kc p) n -> p kc n", p=128)
    wo_d = C.dr["xa_w_o"][l].rearrange("(kc p) n -> p kc n", p=128)
    with ExitStack() as es:
        def sb(name, shape, dt):
            return es.enter_context(sbt(nc, name, shape, dt))
        memn = sb("memn", [128, 8, NMEM], BF16)
        kT = sb("kT", [128, 8, NMEM], BF16)
        v = sb("v", [128, 2, D], BF16)
        wq = sb("wq", [128, 8, D], BF16)
        sq = sb("sq", [128, 2, 512], BF16)
        rs = sb("rs", [128, 512], F32)
        hT = sb("hT", [128, 8, 512], BF16)
        qT = sb("qT", [128, 8, 512], BF16)
        oT = sb("oT", [128, 8, 512], BF16)
        ysb = sb("ysb", [128, 8, 512], F32)
        pt = [sb(f"pt{i}", [128, 512], BF16) for i in range(4)]
        rl = [sb(f"rl{i}", [128, 512], F32) for i in range(2)]
        for hf in range(2):
            wload(C, wq[:, :, hf * 512:(hf + 1) * 512], wq_d[:, :, hf * 512:(hf + 1) * 512], P.R("xa.wq"), "D_xawq")
        with ExitStack() as es1:
            memT = es1.enter_context(sbt(nc, "memT", [128, 8, NMEM], F32))
            mst = [es1.enter_context(sbt(nc, f"mst{i}", [128, D], F32)) for i in range(2)]
            for i in range(2):
                P.dma("sp", mst[i][:], C.dr["mem"][i * 128:(i + 1) * 128, :], f"D_ms{i}", writes=[P.R("xa.mst", i)])
                for g in range(2):
                    ps, pr = bank(C)
                    for c4 in range(4):
                        c = g * 4 + c4
                        P.tr(ps[:, c4 * 128:(c4 + 1) * 128], mst[i][:, c * 128:(c + 1) * 128], C.ident,
                             r=[P.R("xa.mst", i), P.R("cst")], w=[pr])
                    P.cp(memT[:, g * 4:(g + 1) * 4, i * 128:(i + 1) * 128], ps[:].rearrange("p (c t) -> p c t", c=4),
                         r=[pr], w=[P.R("xa.memT")])
            prenorm(C, l, SP_XA_MEM, memT[:], NMEM, memn[:], sq, rs, [P.R("xa.memT")], [P.R("xa.memn")])
        P.free("xa.mst")
        P.free("xa.memT")
        for blk in range(4):
            slot, sres, ssem = C.ring8.next()
            sv = slot[:].rearrange("p (kc n) -> p kc n", kc=8)
            wload(C, sv, wkv_d[:, :, blk * 512:(blk + 1) * 512], sres, ssem)
            if blk < 2:
                for o4 in range(4):
                    oc = blk * 4 + o4
                    ps, pr = bank(C)
                    for kc in range(8):
                        P.mm(ps[:, :NMEM], sv[:, kc, o4 * 128:(o4 + 1) * 128], memn[:, kc, :], start=(kc == 0), stop=(kc == 7),
                             r=[sres, P.R("xa.memn")], w=[pr], inc=(kc == 7))
                    P.cp(kT[:, oc, :], ps[:, :NMEM], r=[pr], w=[P.R("xa.kT")], eng="act")
            else:
                vb = blk - 2
                for mt in range(2):
                    ps, pr = bank(C)
                    for kc in range(8):
                        P.mm(ps[:], memn[:, kc, mt * 128:(mt + 1) * 128], sv[:, kc, :], start=(kc == 0), stop=(kc == 7),
                             r=[sres, P.R("xa.memn")], w=[pr], inc=(kc == 7))
                    P.cp(v[:, mt, vb * 512:(vb + 1) * 512], ps[:], r=[pr], w=[P.R("xa.v")], eng="act")
        prenorm(C, l, SP_XA_PRE, C.xT[:, :, 0:512], 512, hT[:], sq, rs, [P.R("x", 0)], [P.R("xa.h")])
        for tt in range(4):
            for oc in range(8):
                ps, pr = bank(C, 0, 8)
                for kc in range(8):
                    P.mm(ps[:], wq[:, kc, oc * 128:(oc + 1) * 128], hT[:, kc, :], start=(kc == 0), stop=(kc == 7),
                         r=[P.R("xa.wq"), P.R("xa.h")], w=[pr], inc=(kc == 7))
                P.cp(qT[:, oc, :], ps[:], r=[pr], w=[P.R("xa.q")], eng=("act" if oc % 2 else "dve"))

            def emit_S(h):
                out = []
                for mt in range(2):
                    ps, pr = bank(C, 0, 8)
                    for dc in range(2):
                        P.mm(ps[:], kT[:, 2 * h + dc, mt * 128:(mt + 1) * 128], qT[:, 2 * h + dc, :], start=(dc == 0), stop=(dc == 1),
                             r=[P.R("xa.kT"), P.R("xa.q")], w=[pr], inc=(dc == 1))
                    out.append((ps, pr))
                return out
            Sp = {0: emit_S(0)}
            for h in range(4):
                if h + 1 < 4:
                    Sp[h + 1] = emit_S(h + 1)
                pts = []
                for mt, (ps, pr) in enumerate(Sp.pop(h)):
                    k = (h * 2 + mt) % 4
                    P.act(pt[k][:], ps[:], AF.Exp, r=[pr], w=[P.R("xa.pt", k)], scale=1.0 / 16.0)
                    pts.append((pt[k], P.R("xa.pt", k)))
                ps, pr = bank(C, 0, 8)
                for mt in range(2):
                    P.mm(ps[:], C.onesb[:], pts[mt][0][:], start=(mt == 0), stop=(mt == 1), r=[pts[mt][1], P.R("cstb")], w=[pr],
                         inc=(mt == 1))
                P.act(rl[h % 2][:], ps[:], AF.Ln, r=[pr], w=[P.R("xa.rl", h % 2)])
                P.act(rl[h % 2][:], rl[h % 2][:], AF.Exp, r=[P.R("xa.rl", h % 2)], w=[P.R("xa.rl", h % 2)], scale=-1.0)
                for ec in range(2):
                    ps, pr = bank(C, 0, 8)
                    for mt in range(2):
                        P.mm(ps[:], v[:, mt, (2 * h + ec) * 128:(2 * h + ec + 1) * 128], pts[mt][0][:], start=(mt == 0), stop=(mt == 1),
                             r=[pts[mt][1], P.R("xa.v")], w=[pr], inc=(mt == 1))
                    P.tt(oT[:, 2 * h + ec, :], ps[:], rl[h % 2][:], ALU.mult, r=[pr, P.R("xa.rl", h % 2)], w=[P.R("xa.o")])
            if tt + 1 < 4:
                prenorm(C, l, SP_XA_PRE, C.xT[:, :, (tt + 1) * 512:(tt + 2) * 512], 512, hT[:], sq, rs,
                        [P.R("x", tt + 1)], [P.R("xa.h")])
            for ob in range(2):
                slot, sres, ssem = C.ring8.next()
                sv = slot[:].rearrange("p (kc n) -> p kc n", kc=8)
                wload(C, sv, wo_d[:, :, ob * 512:(ob + 1) * 512], sres, ssem)
                for m4 in range(4):
                    mc = ob * 4 + m4
                    ps, pr = bank(C)
                    for kc in range(8):
                        P.mm(ps[:], sv[:, kc, m4 * 128:(m4 + 1) * 128], oT[:, kc, :], start=(kc == 0), stop=(kc == 7),
                             r=[sres, P.R("xa.o")], w=[pr], inc=(kc == 7))
                    P.cp(ysb[:, mc, :], ps[:], r=[pr], w=[P.R("xa.y")], eng=("act" if mc % 2 else "dve"))
            postnorm_add(C, l, SP_XA_POST, ysb[:], 1.0, tt, sq, rs, P.R("xa.y"))
    P.free("xa")
    P.free("nrm")


PI = float(np.pi)
TWO_PI = float(2 * np.pi)


def sin_of(C, out, ang, turns, tf, tg, ti, r, w, rt):
    P = C.P
    P.ts(tf, ang, 1.0 / TWO_PI, ALU.mult, float(turns), ALU.add, r=r, w=[rt])
    P.cp(ti, tf, r=[rt], w=[rt])
    P.cp(tg, ti, r=[rt], w=[rt])
    P.tt(tf, tf, tg, ALU.subtract, r=[rt], w=[rt])
    P.ts(tg, tf, 0.5, ALU.is_gt, r=[rt], w=[rt])
    P.tt(tf, tf, tg, ALU.subtract, r=[rt], w=[rt])
    P.ts(tg, tf, -0.5, ALU.is_lt, r=[rt], w=[rt])
    P.tt(tf, tf, tg, ALU.add, r=[rt], w=[rt])
    P.ts(tf, tf, 0.4999999, ALU.min, -0.4999999, ALU.max, r=[rt], w=[rt])
    P.act(out, tf, AF.Sin, r=[rt], w=w, scale=TWO_PI)


def proj_fm(C, wv, wres, c0, ncols, hT, evac):
    P = C.P
    for tt in range(4):
        ps, pr = bank(C)
        for kc in range(8):
            P.mm(ps[:ncols, :], wv[:, kc, c0:c0 + ncols], hT[:, kc, tt * 512:(tt + 1) * 512], start=(kc == 0), stop=(kc == 7),
                 r=[wres, P.R("mx.h", tt)], w=[pr], inc=(kc == 7))
        evac(ps, pr, tt)


def mixer(C, l):
    P, nc = C.P, C.nc
    win = C.dr["w_in"][l].rearrange("(kc p) n -> p kc n", p=128)
    with ExitStack() as es:
        s5T = es.enter_context(sbt(nc, "s5T", [128, 2, SEQ], BF16))
        glaT = es.enter_context(sbt(nc, "glaT", [128, 2, SEQ], BF16))
        foT = es.enter_context(sbt(nc, "foT", [128, 4, SEQ], BF16))
        hT = es.enter_context(sbt(nc, "hTm", [128, 8, SEQ], BF16))
        with ExitStack() as e0:
            sq = e0.enter_context(sbt(nc, "sq", [128, 2, 512], BF16))
            rs = e0.enter_context(sbt(nc, "rs", [128, 512], F32))
            for tt in range(4):
                prenorm(C, l, SP_MIX_PRE, C.xT[:, :, tt * 512:(tt + 1) * 512], 512, hT[:, :, tt * 512:(tt + 1) * 512],
                        sq, rs, [P.R("x", tt)], [P.R("mx.h", tt)])
        P.free("nrm")
        if "s5" in C.branches:
            s5_branch(C, l, win, hT, s5T)
        if "gla" in C.branches:
            gla_branch(C, l, win, hT, glaT)
        if "fox" in C.branches:
            fox_branch(C, l, win, hT, foT)
        if C.dbg_branch is not None:
            src = {"s5": (s5T, 2, "mx.s5"), "gla": (glaT, 2, "mx.gla"), "fox": (foT, 4, "mx.fo")}[C.dbg_branch]
            for tt in range(4):
                rr = [P.R(src[2], m, tt) for m in range(src[1])]
                P.cp(C.xT[:, 0:src[1], tt * 512:(tt + 1) * 512], src[0][:, :, tt * 512:(tt + 1) * 512], r=rr, w=[P.R("x", tt)])
        else:
            merge(C, l, win, hT, s5T, glaT, foT)
    P.free("mx")
    P.free("nrm")


def s5_branch(C, l, win, hT, s5T):
    P, nc = C.P, C.nc
    bbre, bbim = C.dr["bb_re"][l], C.dr["bb_im"][l]
    ccre = C.dr["cc_re"][l].rearrange("(c p) n -> p c n", p=128)
    ccim = C.dr["cc_im"][l].rearrange("(c p) n -> p c n", p=128)
    gluw_d = C.dr["s5_glu_w"][l].rearrange("(kc p) n -> p kc n", p=128)
    b = l * SPL
    with ExitStack() as es:
        def sb(name, shape, dt):
            return es.enter_context(sbt(nc, name, shape, dt))
        uT = sb("uT", [128, SEQ], BF16)
        ENz = sb("ENz", [128, 2, 512], F32)
        EP = sb("EP", [128, 8, 128], F32)
        BB = sb("BB", [128, 2, 512], BF16)
        CC = sb("CC", [128, 4, 2, 128], BF16)
        Tb = [sb(f"Tb{i}", [128, 4, 512], BF16) for i in range(2)]
        ntib = sb("ntib", [128, 128], BF16)
        W = sb("W", [128, 8, 128], F32)
        P4 = sb("P4", [128, 16, 128], BF16)
        tmp = [sb(f"t{i}", [128, 512], F32) for i in range(4)]
        sc = sb("sc", [128, 64], F32)
        carry = sb("carry", [128, 8], F32)
        ysb = sb("ys5", [128, 128], F32)
        ti = sb("ti", [128, 128], mybir.dt.int32)
        rsc, rtb, rtab = P.R("s5.sc"), P.R("s5.tb"), P.R("s5.tab")
        slot, sres, ssem = C.ring8.next()
        wsu = slot[:, 0:2048].rearrange("p (kc n) -> p kc n", kc=8)
        wload(C, wsu, win[:, :, O_SU:O_SU + 256], sres, ssem)
        iota = C.cst[:, C_IT:C_IT + 128]
        P.ts(ntib[:], C.cst[:, C_TI:C_TI + 128], -1.0, ALU.mult, r=[P.R("cst")], w=[P.R("s5.ntib")])
        for hc in range(2):
            def ev(ps, pr, tt):
                P.cp(uT[:, tt * 512:(tt + 1) * 512], ps[:], r=[pr], w=[P.R("s5.u", tt)], eng="act")
            proj_fm(C, wsu, sres, hc * 128, 128, hT, ev)
            are = C.sp[:, b + SP_ARE + 4 * hc:b + SP_ARE + 4 * hc + 4]
            aim = C.sp[:, b + SP_AIM + 4 * hc:b + SP_AIM + 4 * hc + 4]
            ldt = C.sp[:, b + SP_LDT + 4 * hc:b + SP_LDT + 4 * hc + 4]
            col = lambda k: sc[:, 4 * k:4 * k + 4]
            dt_, lamre, lr, li, nlr, mag1, sinli, cosli, abre, abim, den, zre, zim, sA, sB = [col(k) for k in range(15)]
            rw = dict(r=[rsc, P.R("spar")], w=[rsc])
            P.act(dt_, ldt, AF.Exp, **rw)
            P.ts(lamre, are, -1e-4, ALU.min, **rw)
            P.tt(lr, lamre, dt_, ALU.mult, **rw)
            P.tt(li, aim, dt_, ALU.mult, **rw)
            P.ts(nlr, lr, -1.0, ALU.mult, **rw)
            P.act(mag1, lr, AF.Exp, **rw)
            sin_of(C, sinli, li, 0.0, sA, sB, ti[:, 0:4], [rsc], [rsc], rsc)
            sin_of(C, cosli, li, 0.25, sA, sB, ti[:, 0:4], [rsc], [rsc], rsc)
            P.tt(abre, mag1, cosli, ALU.mult, **rw)
            P.tt(abim, mag1, sinli, ALU.mult, **rw)
            P.tt(den, lamre, lamre, ALU.mult, **rw)
            P.tt(sB, aim, aim, ALU.mult, **rw)
            P.tt(den, den, sB, ALU.add, **rw)
            P.recip(den, den, **rw)
            P.ts(abre, abre, -1.0, ALU.add, **rw)
            P.tt(sA, abre, lamre, ALU.mult, **rw)
            P.tt(sB, abim, aim, ALU.mult, **rw)
            P.tt(sA, sA, sB, ALU.add, **rw)
            P.tt(zre, sA, den, ALU.mult, **rw)
            P.tt(sA, abim, lamre, ALU.mult, **rw)
            P.tt(sB, abre, aim, ALU.mult, **rw)
            P.tt(sA, sA, sB, ALU.subtract, **rw)
            P.tt(zim, sA, den, ALU.mult, **rw)
            A_, B_ = tmp[0][:, 0:128], tmp[0][:, 128:256]
            S_, Cc_ = tmp[1][:, 0:128], tmp[1][:, 128:256]
            MP, MN = tmp[2][:, 0:128], tmp[2][:, 128:256]
            RR, E1, E2, RG = tmp[3][:, 0:128], tmp[3][:, 128:256], tmp[3][:, 256:384], tmp[3][:, 384:512]
            tw = dict(r=[rtb, rsc, P.R("cst")], w=[rtb])
            for pc in range(4):
                P.ts(A_, iota, li[:, pc:pc + 1], ALU.mult, **tw)
                P.act(MP, iota, AF.Exp, scale=lr[:, pc:pc + 1], **tw)
                P.act(MN, iota, AF.Exp, scale=nlr[:, pc:pc + 1], **tw)
                sin_of(C, S_, A_, 0.0, RR, RG, ti[:], [rtb], [rtb], rtb)
                sin_of(C, Cc_, A_, 0.25, RR, RG, ti[:], [rtb], [rtb], rtb)
                P.tt(EP[:, pc, :], MP, Cc_, ALU.mult, r=[rtb], w=[rtab])
                P.tt(EP[:, 4 + pc, :], MP, S_, ALU.mult, r=[rtb], w=[rtab])
                P.ts(E1, Cc_, zre[:, pc:pc + 1], ALU.mult, **tw)
                P.stt(E1, S_, zim[:, pc:pc + 1], E1, ALU.mult, ALU.add, **tw)
                P.tt(E1, E1, MN, ALU.mult, **tw)
                P.ts(E2, Cc_, zim[:, pc:pc + 1], ALU.mult, **tw)
                P.ts(B_, S_, zre[:, pc:pc + 1], ALU.mult, **tw)
                P.tt(E2, E2, B_, ALU.subtract, **tw)
                P.tt(E2, E2, MN, ALU.mult, **tw)
                for ri, E in enumerate((E1, E2)):
                    ps, pr = bank(C)
                    P.tr(ps[:, 0:128], E, C.ident, r=[rtb, P.R("cst")], w=[pr])
                    P.cp(ENz[:, ri, pc * 128:(pc + 1) * 128], ps[:, 0:128], r=[pr], w=[rtab], eng="act")
            L1r, L1i = EP[:, 0:4, 1], EP[:, 4:8, 1]
            Er, Ei = EP[:, 0:4, 127], EP[:, 4:8, 127]
            ew = dict(r=[rtab, rsc], w=[rsc])
            P.tt(sc[:, 24:28], L1r, Er, ALU.mult, **ew)
            P.tt(sc[:, 28:32], L1i, Ei, ALU.mult, **ew)
            P.tt(sc[:, 16:20], sc[:, 24:28], sc[:, 28:32], ALU.subtract, **ew)
            P.tt(sc[:, 24:28], L1r, Ei, ALU.mult, **ew)
            P.tt(sc[:, 28:32], L1i, Er, ALU.mult, **ew)
            P.tt(sc[:, 20:24], sc[:, 24:28], sc[:, 28:32], ALU.add, **ew)
            wload(C, BB[:, 0, :], bbre[hc * 128:(hc + 1) * 128, hc * 512:(hc + 1) * 512], P.R("s5.BB"), "D_s5b")
            wload(C, BB[:, 1, :], bbim[hc * 128:(hc + 1) * 128, hc * 512:(hc + 1) * 512], P.R("s5.BB"), "D_s5b")
            wload(C, CC[:, :, 0, :], ccre[:, 4 * hc:4 * hc + 4, hc * 128:(hc + 1) * 128], P.R("s5.CC"), "D_s5c")
            wload(C, CC[:, :, 1, :], ccim[:, 4 * hc:4 * hc + 4, hc * 128:(hc + 1) * 128], P.R("s5.CC"), "D_s5c")
            P.memset(carry[:], 0.0, w=[P.R("s5.carry")])
            Wre, Wim = W[:, 0:4, :], W[:, 4:8, :]
            EPre, EPim = EP[:, 0:4, :], EP[:, 4:8, :]
            tv = [t[:].rearrange("p (a b) -> p a b", a=4) for t in tmp]
            rt = [P.R("s5.t", i) for i in range(4)]
            if l == 0 and hc == 0:
                print("[sbuf] s5 remaining", nc.sbuf_bytes_remaining)

            def front(c):
                tok = slice(c * 128, (c + 1) * 128)
                tt = c // 4
                Tc = Tb[c % 2]
                rV = P.R("s5.V", c % 2)
                psr, prr = bank(C)
                psi, pri = bank(C)
                P.mm(psr[:], uT[:, tok], BB[:, 0, :], r=[P.R("s5.u", tt), P.R("s5.BB")], w=[prr])
                P.mm(psi[:], uT[:, tok], BB[:, 1, :], r=[P.R("s5.u", tt), P.R("s5.BB")], w=[pri])
                P.tt(Tc[:, 0, :], psr[:], ENz[:, 0, :], ALU.mult, r=[prr, rtab], w=[rV])
                P.tt(Tc[:, 1, :], psi[:], ENz[:, 1, :], ALU.mult, r=[pri, rtab], w=[rV])
                P.tt(Tc[:, 2, :], psi[:], ENz[:, 0, :], ALU.mult, r=[pri, rtab], w=[rV])
                P.tt(Tc[:, 3, :], psr[:], ENz[:, 1, :], ALU.mult, r=[prr, rtab], w=[rV])
                pw = [(C.ps[4 + 2 * (c % 2) + ri], P.R("ps", 4 + 2 * (c % 2) + ri)) for ri in range(2)]
                for ri in range(2):
                    for pc in range(4):
                        cs = slice(pc * 128, (pc + 1) * 128)
                        P.mm(pw[ri][0][:, cs], Tc[:, 2 * ri, cs], C.tib[:], start=True, stop=False,
                             r=[rV, P.R("cstb")], w=[pw[ri][1]], inc=False)
                        P.mm(pw[ri][0][:, cs], Tc[:, 2 * ri + 1, cs], (ntib[:] if ri == 0 else C.tib[:]), start=False, stop=True,
                             r=[rV, P.R("cstb"), P.R("s5.ntib")], w=[pw[ri][1]], inc=(pc == 3))
                return pw

            def back(c, pw):
                tok = slice(c * 128, (c + 1) * 128)
                tt = c // 4
                for ri in range(2):
                    for pc in range(4):
                        k = ri * 4 + pc
                        P.act(W[:, k, :], pw[ri][0][:, pc * 128:(pc + 1) * 128], AF.Identity, bias=carry[:, k:k + 1],
                              r=[pw[ri][1], P.R("s5.carry")], w=[P.R("s5.W")])
                rW = P.R("s5.W")
                xv = W[:, 0:8, 127].rearrange("p (a b) -> p a b", a=2)
                Lrb = sc[:, 16:20].unsqueeze(1).to_broadcast([128, 2, 4])
                Lib = sc[:, 20:24].unsqueeze(1).to_broadcast([128, 2, 4])
                cw = dict(r=[rW, rsc], w=[rsc])
                P.tt(sc[:, 0:8].rearrange("p (a b) -> p a b", a=2), Lrb, xv, ALU.mult, **cw)
                P.tt(sc[:, 8:16].rearrange("p (a b) -> p a b", a=2), Lib, xv, ALU.mult, **cw)
                P.tt(carry[:, 0:4], sc[:, 0:4], sc[:, 12:16], ALU.subtract, r=[rsc], w=[P.R("s5.carry")])
                P.tt(carry[:, 4:8], sc[:, 4:8], sc[:, 8:12], ALU.add, r=[rsc], w=[P.R("s5.carry")])
                rP = P.R("s5.Zb")
                P.tt(P4[:, 0:4, :], EPre, Wre, ALU.mult, r=[rtab, rW], w=[rP])
                P.stt(P4[:, 4:8, :], EPim, -1.0, Wim, ALU.mult, ALU.mult, r=[rtab, rW], w=[rP])
                P.stt(P4[:, 8:12, :], EPre, -1.0, Wim, ALU.mult, ALU.mult, r=[rtab, rW], w=[rP])
                P.stt(P4[:, 12:16, :], EPim, -1.0, Wre, ALU.mult, ALU.mult, r=[rtab, rW], w=[rP])
                py, pyr = bank(C)
                for pc in range(4):
                    for q4 in range(4):
                        P.mm(py[:, :128], CC[:, pc, q4 // 2, :], P4[:, 4 * q4 + pc, :], start=(pc == 0 and q4 == 0),
                             stop=(pc == 3 and q4 == 3), r=[P.R("s5.CC"), rP], w=[pyr], inc=(pc == 3 and q4 == 3))
                return py, pyr

            def tail(c, py, pyr):
                tok = slice(c * 128, (c + 1) * 128)
                tt = c // 4
                P.stt(ysb[:], uT[:, tok], gcol(C, l, SP_S5_D, hc), py[:, :128], ALU.mult, ALU.add,
                      r=[pyr, P.R("s5.u", tt), P.R("spar")], w=[P.R("s5.ys")])
                gelu_tanh(C, s5T[:, hc, tok], ysb[:], tmp[0][:, 0:128], tmp[0][:, 128:256], [P.R("s5.ys")], [P.R("mx.s5", hc, tt)], rt[0])

            pws = {0: front(0)}
            pys = {}
            for c in range(16):
                if c + 1 < 16:
                    pws[c + 1] = front(c + 1)
                pys[c] = back(c, pws.pop(c))
                if c >= 1:
                    tail(c - 1, *pys.pop(c - 1))
            tail(15, *pys.pop(15))
        slot, gres, gsem = C.ring2.next()
        gluw = slot[:, 0:512].rearrange("p (kc n) -> p kc n", kc=2)
        wload(C, gluw, gluw_d, gres, gsem)
        for tt in range(4):
            tl = slice(tt * 512, (tt + 1) * 512)
            for oc in range(2):
                ps, pr = bank(C)
                for kc in range(2):
                    P.mm(ps[:], gluw[:, kc, oc * 128:(oc + 1) * 128], s5T[:, kc, tl], start=(kc == 0), stop=(kc == 1),
                         r=[gres, P.R("mx.s5", kc, tt)], w=[pr], inc=(kc == 1))
                P.act(tmp[oc][:], ps[:], AF.Sigmoid, bias=gcol(C, l, SP_GLU_B, oc), r=[pr, P.R("spar")], w=[rt[oc]])
            for oc in range(2):
                P.tt(s5T[:, oc, tl], s5T[:, oc, tl], tmp[oc][:], ALU.mult, r=[rt[oc], P.R("mx.s5", oc, tt)], w=[P.R("mx.s5", oc, tt)])
    P.free("s5")


def gla_branch(C, l, win, hT, glaT):
    P, nc = C.P, C.nc
    b = l * SPL
    with ExitStack() as es:
        def sb(name, shape, dt):
            return es.enter_context(sbt(nc, name, shape, dt))
        qT = sb("gqT", [128, SEQ], BF16)
        kT = sb("gkT", [128, SEQ], BF16)
        ktok = sb("gktok", [128, 16, 128], BF16)
        gv = sb("gv", [128, 16, 256], BF16)
        gdT = sb("gdT", [128, SEQ], BF16)
        gw = sb("ggw", [128, 128], BF16)
        vpad = sb("gvpad", [128, 4, 128], BF16)
        klc = sb("gklc", [128, 2, 128], BF16)
        klcb = sb("gklcb", [128, 2, 128], BF16)
        qfb = sb("gqfb", [128, 128], BF16)
        bB = sb("gbB", [128, 128], F32)
        onec = sb("onec", [128, 1], F32)
        zt = sb("gzt", [128, 128], F32)
        nla = sb("gnla", [128, 128], BF16)
        eg = [sb(f"geg{i}", [128, 128], F32) for i in range(2)]
        ieg = sb("gieg", [128, 128], F32)
        er = sb("ger", [128, 128], F32)
        qf = sb("gqf", [128, 128], BF16)
        qi = sb("gqi", [128, 128], BF16)
        kih = sb("gkih", [128, 4, 128], BF16)
        kfh = sb("gkfh", [128, 4, 128], BF16)
        kl = sb("gkl", [128, 128], BF16)
        attn = sb("gattn", [128, 4, 2, 128], BF16)
        S = sb("gS", [128, 256], F32)
        Sb = [sb(f"gSb{i}", [128, 256], BF16) for i in range(3)]
        sqg = sb("gsq", [128, 128], BF16)
        rstd = sb("grstd", [128, 128], F32)
        on = sb("gon", [128, 128], F32)
        rtmp = sb("grtmp", [128, 512], F32)
        hm = C.cst[:, C_HM:C_HM + 4]
        MUf = C.cst[:, C_MU:C_MU + 128]
        MLf = C.cst[:, C_ML:C_ML + 128]
        P.memset(onec[:], 1.0, w=[P.R("gla.c")])
        P.memset(S[:], 0.0, w=[P.R("gla.S")])
        P.memset(Sb[0][:], 0.0, w=[P.R("gla.Sb", 0)])
        P.dma("sp", bB[:], C.dr["bp"][:, l * BPL + BP_GLAB:l * BPL + BP_GLAB + 128], "D_gbB", writes=[P.R("gla.bB")])
        P.memset(vpad[:], 0.0, w=[P.R("gla.vpad")])
        P.memset(klc[:], 0.0, w=[P.R("gla.klc", 0)])
        P.memset(klcb[:], 0.0, w=[P.R("gla.klc", 1)])
        wload(C, gw[:], C.dr["gla_gate_w"][l], P.R("gla.gw"), "D_ggw")
        slotA, resA, semA = C.ring8.next()
        wA = slotA[:].rearrange("p (kc n) -> p kc n", kc=8)
        wload(C, wA, win[:, :, 0:512], resA, semA)
        slotD, resD, semD = C.ring8.next()
        win_gd = slotD[:, 0:1024].rearrange("p (kc n) -> p kc n", kc=8)
        wload(C, win_gd, win[:, :, 768:896], resD, semD)
        wB = slotD[:, 1024:3072].rearrange("p (kc n) -> p kc n", kc=8)
        wload(C, wB, win[:, :, 512:768], resD, semD)
        resB = resD

        def ev_q(ps, pr, tt):
            P.cp(qT[:, tt * 512:(tt + 1) * 512], ps[:], r=[pr], w=[P.R("gla.q", tt)], eng="act")

        def ev_k(ps, pr, tt):
            P.cp(kT[:, tt * 512:(tt + 1) * 512], ps[:], r=[pr], w=[P.R("gla.k", tt)], eng="act")
        if C.gla_stop >= 2:
            proj_fm(C, wA, resA, 0, 128, hT, ev_q)
            proj_fm(C, wA, resA, 128, 128, hT, ev_k)
        for m in range(2 if C.gla_stop >= 3 else 0):
            def ev_r(ps, pr, tt, m=m):
                P.act(rtmp[:], ps[:], AF.Sigmoid, r=[pr], w=[P.R("gla.rtmp")])
                P.tt(glaT[:, m, tt * 512:(tt + 1) * 512], ps[:], rtmp[:], ALU.mult, r=[pr, P.R("gla.rtmp")], w=[P.R("mx.gla", m, tt)])
            proj_fm(C, wB, resB, m * 128, 128, hT, ev_r)

        def ev_d(ps, pr, tt):
            P.cp(gdT[:, tt * 512:(tt + 1) * 512], ps[:], r=[pr], w=[P.R("gla.gd", tt)], eng="act")
        if C.gla_stop >= 4:
            proj_fm(C, win_gd, resD, 0, 128, hT, ev_d)
        for i in range(16 if C.gla_stop >= 5 else 0):
            ps, pr = bank(C)
            for kc in range(8):
                P.mm(ps[:, 0:384], hT[:, kc, i * 128:(i + 1) * 128], wA[:, kc, 128:512], start=(kc == 0), stop=(kc == 7),
                     r=[resA, P.R("mx.h", i // 4)], w=[pr], inc=(kc == 7))
            P.cp(ktok[:, i, :], ps[:, 0:128], r=[pr], w=[P.R("gla.ktok", i)], eng="act")
            P.cp(gv[:, i, :], ps[:, 128:384], r=[pr], w=[P.R("gla.gv", i)])
        sbi = [0]
        qf2 = [qf, qfb]
        klc2 = [klc, klcb]

        def phaseA(i):
            tok = slice(i * 128, (i + 1) * 128)
            tt = i // 4
            egi = eg[i % 2]
            reg = P.R("gla.eg", i % 2)
            qf_ = qf2[i % 2]
            rqf = P.R("gla.qf", i % 2)
            klc_ = klc2[i % 2]
            rklc = P.R("gla.klc", i % 2)
            ps1, pr1 = bank(C)
            P.mm(ps1[:, 0:128], gdT[:, tok], gw[:], r=[P.R("gla.gd", tt), P.R("gla.gw")], w=[pr1])
            P.tt(zt[:], ps1[:, 0:128], bB[:], ALU.add, r=[pr1, P.R("gla.bB")], w=[P.R("gla.zt")])
            P.act(zt[:], zt[:], AF.Exp, r=[P.R("gla.zt")], w=[P.R("gla.zt")], scale=-1.0)
            P.act(nla[:], zt[:], AF.Ln, r=[P.R("gla.zt"), P.R("gla.c")], w=[P.R("gla.nla")], bias=onec[:])
            psG, prG = bank(C)
            P.mm(psG[:, 0:128], nla[:], C.mub[:, 0, :], r=[P.R("gla.nla"), P.R("cstb")], w=[prG])
            psR, prR = bank(C)
            P.mm(psR[:, 0:128], C.mub[:, 1, :], nla[:], r=[P.R("gla.nla"), P.R("cstb")], w=[prR])
            P.act(egi[:], psG[:, 0:128], AF.Exp, r=[prG], w=[reg], scale=-1.0 / 16.0)
            P.act(ieg[:], psG[:, 0:128], AF.Exp, r=[prG], w=[P.R("gla.ieg")], scale=1.0 / 16.0)
            P.act(er[:], psR[:, 0:128], AF.Exp, r=[prR], w=[P.R("gla.er")], scale=-1.0 / 16.0)
            P.stt(qf_[:], qT[:, tok], 32.0 ** -0.5, egi[:], ALU.mult, ALU.mult, r=[P.R("gla.q", tt), reg], w=[rqf])
            P.stt(qi[:], qT[:, tok], 32.0 ** -0.5, ieg[:], ALU.mult, ALU.mult, r=[P.R("gla.q", tt), P.R("gla.ieg")], w=[P.R("gla.qi")])
            for h in range(4):
                P.stt(kih[:, h, :], kT[:, tok], hm[:, h:h + 1], ieg[:], ALU.mult, ALU.mult,
                      r=[P.R("gla.k", tt), P.R("gla.ieg"), P.R("cst")], w=[P.R("gla.kih")])
                P.stt(kfh[:, h, :], kT[:, tok], hm[:, h:h + 1], egi[:], ALU.mult, ALU.mult,
                      r=[P.R("gla.k", tt), reg, P.R("cst")], w=[P.R("gla.kfh")])
            for cc in range(2):
                P.tt(klc_[64 * cc:64 * cc + 64, cc, :], ktok[64 * cc:64 * cc + 64, i, :], er[64 * cc:64 * cc + 64, :], ALU.mult,
                     r=[P.R("gla.ktok", i), P.R("gla.er")], w=[rklc])
            pS = [bank(C), bank(C)]
            for h in range(4):
                pb, prb = pS[h // 2]
                o0 = (h % 2) * 256
                P.mm(pb[:, o0:o0 + 128], kih[:, h, :], qf_[:], r=[P.R("gla.kih"), rqf], w=[prb])
                P.mm(pb[:, o0 + 128:o0 + 256], kfh[:, h, :], qi[:], r=[P.R("gla.kfh"), P.R("gla.qi")], w=[prb])
            for h in range(4):
                pb, prb = pS[h // 2]
                o0 = (h % 2) * 256
                P.tt(attn[:, h, :, :], pb[:, o0:o0 + 256].rearrange("p (f t) -> p f t", f=2), C.mub[:], ALU.mult,
                     r=[prb, P.R("cstb")], w=[P.R("gla.attn")])
            for par in range(2):
                P.cp(vpad[:, par::2, 64 * par:64 * par + 64], gv[:, i, :].rearrange("p (h e) -> p h e", h=4)[:, par::2, :],
                     r=[P.R("gla.gv", i)], w=[P.R("gla.vpad")])
            pO = [(C.ps[4 + 2 * (i % 2) + m], P.R("ps", 4 + 2 * (i % 2) + m)) for m in range(2)]
            for m in range(2):
                po, pro = pO[m]
                for hh in range(2):
                    h = 2 * m + hh
                    for fb in range(2):
                        P.mm(po[:, 0:128], vpad[:, h, :], attn[:, h, fb, :],
                             start=(fb == 0 and hh == 0), stop=False, r=[P.R("gla.vpad"), P.R("gla.attn")], w=[pro])
            return pO

        def phaseB(i, pO):
            tok = slice(i * 128, (i + 1) * 128)
            tt = i // 4
            egi = eg[i % 2]
            reg = P.R("gla.eg", i % 2)
            qf_ = qf2[i % 2]
            rqf = P.R("gla.qf", i % 2)
            klc_ = klc2[i % 2]
            rklc = P.R("gla.klc", i % 2)
            for cc in range(2):
                sprev = Sb[sbi[0] % 3]
                rsprev = P.R("gla.Sb", sbi[0] % 3)
                for m in range(2):
                    po, pro = pO[m]
                    P.mm(po[:, cc * 64:(cc + 1) * 64], sprev[:, m * 128:(m + 1) * 128], qf_[:, cc * 64:(cc + 1) * 64],
                         start=False, stop=(cc == 1), r=[rsprev, rqf], w=[pro])
                pD, prD = bank(C)
                P.mm(pD[:, 0:256], klc_[:, cc, :], gv[:, i, :],
                     r=[rklc, P.R("gla.gv", i)], w=[prD])
                P.stt(S[:], S[:], egi[:, 63 + 64 * cc:64 + 64 * cc], pD[:, 0:256], ALU.mult, ALU.add,
                      r=[P.R("gla.S"), reg, prD], w=[P.R("gla.S")])
                sbi[0] += 1
                P.tt(Sb[sbi[0] % 3][:], S[:], C.smb[:], ALU.mult, r=[P.R("gla.S"), P.R("cstb")], w=[P.R("gla.Sb", sbi[0] % 3)])
            for m in range(2):
                po, pro = pO[m]
                P.act(sqg[:], po[:, 0:128], AF.Square, r=[pro], w=[P.R("gla.sq")])
                pN, prN = bank(C)
                P.mm(pN[:, 0:128], C.bob[:], sqg[:], r=[P.R("gla.sq"), P.R("cstb")], w=[prN])
                P.act(rstd[:], pN[:, 0:128], AF.Ln, r=[prN, P.R("cstb")], w=[P.R("gla.rstd")], bias=C.epsc[:], scale=1.0 / 64.0)
                P.act(rstd[:], rstd[:], AF.Exp, r=[P.R("gla.rstd")], w=[P.R("gla.rstd")], scale=-0.5)
                P.stt(on[:], po[:, 0:128], C.sp[:, b + SP_GLA_NG + m:b + SP_GLA_NG + m + 1], rstd[:], ALU.mult, ALU.mult,
                      r=[pro, P.R("gla.rstd"), P.R("spar")], w=[P.R("gla.on")])
                P.tt(glaT[:, m, tok], on[:], glaT[:, m, tok], ALU.mult, r=[P.R("gla.on"), P.R("mx.gla", m, tt)],
                     w=[P.R("mx.gla", m, tt)])

        nt = C.gla_ntiles
        pOs = {}
        if nt:
            pOs[0] = phaseA(0)
        for i in range(nt):
            if i + 1 < nt:
                pOs[i + 1] = phaseA(i + 1)
            phaseB(i, pOs.pop(i))
    P.free("gla")


def fox_branch(C, l, win, hT, foT):
    P, nc = C.P, C.nc
    with ExitStack() as es:
        def sb(name, shape, dt):
            return es.enter_context(sbt(nc, name, shape, dt))
        fq = sb("fq", [128, SEQ], BF16)
        fkp = sb("fkp", [128, 2, SEQ], BF16)
        fvp = sb("fvp", [128, 16, 2, 128], BF16)
        lf = sb("flf", [128, 16, 8], F32)
        lfb = sb("flfb", [128, 128], BF16)
        NF = sb("fNF", [128, 16, 8], F32)
        Cc = sb("fCc", [128, 16, 8], F32)
        NF2 = sb("fNF2", [128, 16, 16], F32)
        NFThl = sb("fNFT", [128, SEQ], BF16)
        sel = sb("fsel", [128, 8, 128], BF16)
        identb = sb("fidb", [128, 128], BF16)
        mneg = sb("fmneg", [128, 128], BF16)
        t32 = sb("ft32", [16, 512], F32)
        h16 = sb("fh16", [16, 512], BF16)
        h32 = sb("fh32", [16, 512], F32)
        fbB = sb("ffbB", [128, 8], F32)
        onec = sb("fonec", [128, 1], F32)
        onesAB = sb("fones", [128, 2, 128], BF16)
        pt = [sb(f"fpt{i}", [128, 512], BF16) for i in range(3)]
        rl = sb("frl", [128, 512], F32)
        P.memset(onec[:], 1.0, w=[P.R("fox.c")])
        P.memset(onesAB[:], 0.0, w=[P.R("fox.c")])
        P.memset(onesAB[:, 0, 0:64], 1.0, w=[P.R("fox.c")])
        P.memset(onesAB[:, 1, 64:128], 1.0, w=[P.R("fox.c")])
        P.memset(fkp[:], 0.0, w=[P.R("fox.kz")])
        P.memset(fvp[:], 0.0, w=[P.R("fox.vz")])
        P.dma("sp", fbB[:], C.dr["bp"][:, l * BPL + BP_FOXB:l * BPL + BP_FOXB + 8], "D_fbB", writes=[P.R("fox.fbB")])
        slot, fres, fsem = C.ring2.next()
        wff = slot[:, 0:64].rearrange("p (kc n) -> p kc n", kc=8)
        wload(C, wff, win[:, :, O_FF:O_FF + 8], fres, fsem)
        for i in range(16):
            ps, pr = bank(C)
            for kc in range(8):
                P.mm(ps[:, 0:8], hT[:, kc, i * 128:(i + 1) * 128], wff[:, kc, :], start=(kc == 0), stop=(kc == 7),
                     r=[fres, P.R("mx.h", i // 4)], w=[pr], inc=(kc == 7))
            P.tt(lf[:, i, :], ps[:, 0:8], fbB[:], ALU.add, r=[pr, P.R("fox.fbB")], w=[P.R("fox.lf")])
        lff = lf[:].rearrange("p j h -> p (j h)")
        P.act(lff, lff, AF.Exp, r=[P.R("fox.lf")], w=[P.R("fox.lf")], scale=-1.0)
        P.act(lff, lff, AF.Ln, r=[P.R("fox.lf"), P.R("fox.c")], w=[P.R("fox.lf")], bias=onec[:])
        P.cp(lfb[:], lff, r=[P.R("fox.lf")], w=[P.R("fox.lfb")])
        psT, prT = bank(C)
        P.mm(psT[:, 0:128], C.onesb[:], lfb[:], r=[P.R("fox.lfb"), P.R("cstb")], w=[prT])
        psL_, prL_ = bank(C)
        P.mm(psL_[:, 0:128], C.tib[:], lfb[:], r=[P.R("fox.lfb"), P.R("cstb")], w=[prL_])
        rC = P.R("fox.Cc")
        P.cp(Cc[:, 0, :], psT[:, 0:8], r=[prT], w=[rC])
        for i in range(1, 16):
            P.tt(Cc[:, i, :], psT[:, i * 8:(i + 1) * 8], Cc[:, i - 1, :], ALU.add, r=[prT, rC], w=[rC])
        rN = P.R("fox.NF")
        P.cp(NF[:, 0, :], psL_[:, 0:8], r=[prL_], w=[rN])
        P.tt(NF[:, 1:16, :].rearrange("p j h -> p (j h)"), psL_[:, 8:128], Cc[:, 0:15, :].rearrange("p j h -> p (j h)"), ALU.add,
             r=[prL_, rC], w=[rN])
        P.ts(NF2[:, :, 0:8], NF[:], -1.0, ALU.mult, r=[rN], w=[P.R("fox.NF2")])
        P.ts(NF2[:, :, 8:16], NF[:], -1.0, ALU.mult, r=[rN], w=[P.R("fox.NF2")])
        P.memset(NFThl[:], 0.0, w=[P.R("fox.NFT")])
        P.cp(sel[:], C.cst[:, C_SEL:C_SEL + 8].unsqueeze(2).to_broadcast([128, 8, 128]), r=[P.R("cst")], w=[P.R("fox.c")])
        P.cp(identb[:], C.ident, r=[P.R("cst")], w=[P.R("fox.c")])
        P.ts(mneg[:], C.cst[:, C_TI:C_TI + 128], -1.0, ALU.add, 30000.0, ALU.mult, r=[P.R("cst")], w=[P.R("fox.c")])
        m0 = C.cst[0:16, C_M01:C_M01 + 1]
        m1 = C.cst[0:16, C_M01 + 1:C_M01 + 2]
        rq = P.R("fox.hl")
        for blk in range(4):
            ps, pr = bank(C)
            for k4 in range(4):
                P.tr(ps[0:16, k4 * 128:(k4 + 1) * 128], NF2[:, blk * 4 + k4, :], C.ident, r=[P.R("fox.NF2"), P.R("cst")], w=[pr])
            P.cp(t32[:], ps[0:16, :], r=[pr], w=[rq], eng="act")
            P.cp(h16[:], t32[:], r=[rq], w=[rq])
            P.cp(h32[:], h16[:], r=[rq], w=[rq])
            P.tt(t32[:], t32[:], h32[:], ALU.subtract, r=[rq], w=[rq])
            P.ts(h32[:], h32[:], m0, ALU.mult, r=[rq, P.R("cst")], w=[rq])
            P.stt(NFThl[0:16, blk * 512:(blk + 1) * 512], t32[:], m1, h32[:], ALU.mult, ALU.add, r=[rq, P.R("cst")], w=[P.R("fox.NFT")])
        pk = 0
        for m in range(4):
            slot, wres, wsem = C.ring8.next()
            wv = slot[:, 0:3072].rearrange("p (kc g n) -> p kc g n", kc=8, g=3)
            for g, off in enumerate((O_FQ, O_FK, O_FV)):
                wload(C, wv[:, :, g, :], win[:, :, off + m * 128:off + (m + 1) * 128], wres, wsem)

            def ev_q(ps, pr, tt):
                P.op("act", lambda e, o=fq[:, tt * 512:(tt + 1) * 512], i=ps[:]: e.mul(out=o, in_=i, mul=0.125), [pr], [P.R("fox.q", tt)])

            def ev_k(ps, pr, tt):
                P.cp(fkp[0:64, 0, tt * 512:(tt + 1) * 512], ps[0:64, :], r=[pr, P.R("fox.kz")], w=[P.R("fox.k", tt)], eng="act")
                P.cp(fkp[64:128, 1, tt * 512:(tt + 1) * 512], ps[64:128, :], r=[pr, P.R("fox.kz")], w=[P.R("fox.k", tt)])
            proj_fm(C, wv[:, :, 0, :], wres, 0, 128, hT, ev_q)
            proj_fm(C, wv[:, :, 1, :], wres, 0, 128, hT, ev_k)
            for i in range(16):
                ps, pr = bank(C)
                for kc in range(8):
                    P.mm(ps[:, 0:128], hT[:, kc, i * 128:(i + 1) * 128], wv[:, kc, 2, :], start=(kc == 0), stop=(kc == 7),
                         r=[wres, P.R("mx.h", i // 4)], w=[pr], inc=(kc == 7))
                P.cp(fvp[:, i, 0, 0:64], ps[:, 0:64], r=[pr, P.R("fox.vz")], w=[P.R("fox.v", i)], eng="act")
                P.cp(fvp[:, i, 1, 64:128], ps[:, 64:128], r=[pr, P.R("fox.vz")], w=[P.R("fox.v", i)])
            for qg in range(4):
                par = (m * 4 + qg) % 2
                psO, prO = C.ps[4 + par], P.R("ps", 4 + par)
                psL, prL = C.ps[6 + par], P.R("ps", 6 + par)
                nj = 4 * qg + 4
                its = [(hh, j) for hh in range(2) for j in range(nj)]
                pend = {}

                def emit_S(k):
                    hh, j = its[k]
                    i_lo = max(j, 4 * qg)
                    ncol = (nj - i_lo) * 128
                    c0 = (i_lo - 4 * qg) * 128
                    psS, prS = bank(C)
                    diag = (j >= 4 * qg)
                    P.mm(psS[:, c0:c0 + ncol], fkp[:, hh, j * 128:(j + 1) * 128], fq[:, i_lo * 128:nj * 128],
                         start=True, stop=False, r=[P.R("fox.k", j // 4), P.R("fox.q", qg)], w=[prS], inc=False)
                    P.mm(psS[:, c0:c0 + ncol], sel[:, 2 * m + hh, :], NFThl[:, i_lo * 128:nj * 128],
                         start=False, stop=(not diag), r=[P.R("fox.c"), P.R("fox.NFT")], w=[prS], inc=(not diag))
                    if diag:
                        cd = (j - 4 * qg) * 128
                        P.mm(psS[:, cd:cd + 128], identb[:], mneg[:], start=False, stop=True, r=[P.R("fox.c")], w=[prS])
                    pend[k] = (psS, prS, c0, ncol)
                LOOK = 2
                for k in range(min(LOOK, len(its))):
                    emit_S(k)
                for k, (hh, j) in enumerate(its):
                    if k + LOOK < len(its):
                        emit_S(k + LOOK)
                    h = 2 * m + hh
                    psS, prS, c0, ncol = pend.pop(k)
                    ptile = pt[pk % 3]
                    rpt = P.R("fox.pt", pk % 3)
                    pk += 1
                    P.act(ptile[:, c0:c0 + ncol], psS[:, c0:c0 + ncol], AF.Exp, r=[prS, rN], w=[rpt],
                          bias=NF[:, j, h:h + 1])
                    first = (hh == 0 and j == 0)
                    last = (hh == 1 and j == nj - 1)
                    P.mm(psO[:, c0:c0 + ncol], fvp[:, j, hh, :], ptile[:, c0:c0 + ncol], start=first, stop=last,
                         r=[P.R("fox.v", j), rpt], w=[prO])
                    P.mm(psL[:, c0:c0 + ncol], onesAB[:, hh, :], ptile[:, c0:c0 + ncol], start=first, stop=last,
                         r=[P.R("fox.c"), rpt], w=[prL])
                P.act(rl[:], psL[:], AF.Ln, r=[prL], w=[P.R("fox.rl")])
                P.act(rl[:], rl[:], AF.Exp, r=[P.R("fox.rl")], w=[P.R("fox.rl")], scale=-1.0)
                P.tt(foT[:, m, qg * 512:(qg + 1) * 512], psO[:], rl[:], ALU.mult, r=[prO, P.R("fox.rl")], w=[P.R("mx.fo", m, qg)])
    P.free("fox")


def merge(C, l, win, hT, s5T, glaT, foT):
    P, nc = C.P, C.nc
    wgla = C.dr["w_gla_up"][l].rearrange("(kc p) n -> p kc n", p=128)
    ws5 = C.dr["w_s5_up"][l].rearrange("(kc p) n -> p kc n", p=128)
    wfox = C.dr["w_fox_up"][l].rearrange("(kc p) n -> p kc n", p=128)
    wmo = C.dr["w_mix_out"][l].rearrange("(kc p) n -> p kc n", p=128)
    srcs = ((glaT, 2, "mx.gla", wgla, 0), (s5T, 2, "mx.s5", ws5, 2), (foT, 4, "mx.fo", wfox, 4))
    with ExitStack() as es:
        mixT = es.enter_context(sbt(nc, "mixT", [128, 8, 1024], BF16))
        for half in range(2):
            with ExitStack() as ea:
                sig = [ea.enter_context(sbt(nc, f"msig{i}", [128, 512], F32)) for i in range(2)]
                acc = [ea.enter_context(sbt(nc, f"macc{i}", [128, 512], F32)) for i in range(2)]
                tm = ea.enter_context(sbt(nc, "mtm", [128, 512], F32))
                it = 0
                for c in range(8):
                    slot, wres, wsem = C.ring8.next()
                    wv = slot[:].rearrange("p (k n) -> p k n", k=32)
                    for (srcT, nk, rname, wd, k0) in srcs:
                        wload(C, wv[:, k0:k0 + nk, :], wd[:, :, c * 128:(c + 1) * 128], wres, wsem)
                    for bi in range(3):
                        g0 = O_GATES + bi * 1024 + c * 128
                        wload(C, wv[:, 8 + 8 * bi:16 + 8 * bi, :], win[:, :, g0:g0 + 128], wres, wsem)
                    for t2 in range(2):
                        tt = half * 2 + t2
                        tl = slice(tt * 512, (tt + 1) * 512)
                        a = it % 2
                        racc = P.R("mg.acc", a)
                        it += 1
                        for bi, (srcT, nk, rname, wd, k0) in enumerate(srcs):
                            psU, prU = bank(C, 0, 8)
                            for kc in range(nk):
                                P.mm(psU[:], wv[:, k0 + kc, :], srcT[:, kc, tl], start=(kc == 0), stop=(kc == nk - 1),
                                     r=[wres, P.R(rname, kc, tt)], w=[prU], inc=(kc == nk - 1))
                            psG, prG = bank(C, 0, 8)
                            for kc in range(8):
                                P.mm(psG[:], wv[:, 8 + 8 * bi + kc, :], hT[:, kc, tl], start=(kc == 0), stop=(kc == 7),
                                     r=[wres, P.R("mx.h", tt)], w=[prG], inc=(kc == 7))
                            sg = sig[bi % 2]
                            rsg = P.R("mg.sig", bi % 2)
                            P.act(sg[:], psG[:], AF.Sigmoid, r=[prG], w=[rsg])
                            if bi == 0:
                                P.tt(acc[a][:], psU[:], sg[:], ALU.mult, r=[prU, rsg], w=[racc])
                            elif bi == 1:
                                P.tt(tm[:], psU[:], sg[:], ALU.mult, r=[prU, rsg], w=[P.R("mg.tm")])
                                P.tt(acc[a][:], acc[a][:], tm[:], ALU.add, r=[racc, P.R("mg.tm")], w=[racc])
                            else:
                                P.tt(tm[:], psU[:], sg[:], ALU.mult, r=[prU, rsg], w=[P.R("mg.tm")])
                                P.tt(mixT[:, c, t2 * 512:(t2 + 1) * 512], acc[a][:], tm[:], ALU.add,
                                     r=[racc, P.R("mg.tm")], w=[P.R("mg.mix", t2)])
            P.free("mg.sig")
            P.free("mg.acc")
            P.free("mg.tm")
            with ExitStack() as eb:
                ysb = eb.enter_context(sbt(nc, "ymx", [128, 8, 512], F32))
                sq = eb.enter_context(sbt(nc, "sq", [128, 2, 512], BF16))
                rs = eb.enter_context(sbt(nc, "rs", [128, 512], F32))
                for t2 in range(2):
                    tt = half * 2 + t2
                    for ob in range(2):
                        slot, sres, ssem = C.ring8.next()
                        sv = slot[:].rearrange("p (kc n) -> p kc n", kc=8)
                        wload(C, sv, wmo[:, :, ob * 512:(ob + 1) * 512], sres, ssem)
                        for m4 in range(4):
                            mc = ob * 4 + m4
                            ps, pr = bank(C, 0, 8)
                            for kc in range(8):
                                P.mm(ps[:], sv[:, kc, m4 * 128:(m4 + 1) * 128], mixT[:, kc, t2 * 512:(t2 + 1) * 512],
                                     start=(kc == 0), stop=(kc == 7), r=[sres, P.R("mg.mix", t2)], w=[pr], inc=(kc == 7))
                            P.cp(ysb[:, mc, :], ps[:], r=[pr], w=[P.R("mg.y")], eng=("act" if mc % 2 else "dve"))
                    postnorm_add(C, l, SP_MIX_POST, ysb[:], 1.0, tt, sq, rs, P.R("mg.y"))
            P.free("mg.y")
            P.free("nrm")
    P.free("mg")
    P.free("nrm")


def gelu_tanh(C, out, x, t1, t2, r, w, rt):
    P = C.P
    P.act(t1, x, AF.Square, r=r, w=[rt], scale=0.044715 ** 0.5)
    P.stt(t1, t1, 1.0, x, ALU.add, ALU.mult, r=[rt] + list(r), w=[rt])
    P.act(t1, t1, AF.Tanh, r=[rt], w=[rt], scale=0.7978845608028654)
    P.op("act", lambda e: e.mul(out=t2, in_=x, mul=0.5), list(r), [rt])
    P.stt(out, t1, 1.0, t2, ALU.add, ALU.mult, r=[rt], w=w)

def _cols(v):
    v = np.asarray(v, np.float32)
    return np.ascontiguousarray(v.reshape(-1, 128).T)


def make_consts():
    c = np.zeros((128, NCONST), np.float32)
    p = np.arange(128)
    c[:, C_ID:C_ID + 128] = np.eye(128)
    c[:, C_TI:C_TI + 128] = (p[:, None] <= p[None, :])
    same = (p[:, None] // 64) == (p[None, :] // 64)
    c[:, C_MU:C_MU + 128] = same & (p[:, None] <= p[None, :])
    c[:, C_ML:C_ML + 128] = same & (p[:, None] > p[None, :])
    c[:, C_BO:C_BO + 128] = same
    c[:, C_HM:C_HM + 4] = (p[:, None] // 32) == np.arange(4)[None, :]
    c[:, C_SM:C_SM + 256] = (p[:, None] // 32) == (np.arange(256)[None, :] // 64)
    c[:, C_IT:C_IT + 128] = p[None, :]
    c[:, C_IP] = p
    c[:, C_SEL:C_SEL + 8] = (p[:, None] == np.arange(8)[None, :]) | (p[:, None] == 8 + np.arange(8)[None, :])
    c[:, C_M01] = p < 8
    c[:, C_M01 + 1] = (p >= 8) & (p < 16)
    return c


def pack_small(inp):
    sp = np.zeros((128, DEPTH * SPL), np.float32)
    bp = np.zeros((128, DEPTH * BPL), np.float32)
    for l in range(DEPTH):
        b = l * SPL
        for off, name in ((SP_FFN1_PRE, "ffn1_pre_g"), (SP_FFN1_POST, "ffn1_post_g"), (SP_MIX_PRE, "mix_pre_g"),
                          (SP_MIX_POST, "mix_post_g"), (SP_XA_PRE, "xa_pre_g"), (SP_XA_MEM, "xa_mem_g"),
                          (SP_XA_POST, "xa_post_g"), (SP_FFN2_PRE, "ffn2_pre_g"), (SP_FFN2_POST, "ffn2_post_g")):
            sp[:, b + off:b + off + 8] = _cols(inp[name][l])
        sp[:, b + SP_GLA_B:b + SP_GLA_B + 1] = _cols(inp["gla_gate_b"][l])
        sp[:, b + SP_GLA_NG:b + SP_GLA_NG + 2] = _cols(inp["gla_norm_g"][l])
        sp[:, b + SP_S5_D:b + SP_S5_D + 2] = _cols(inp["s5_d"][l])
        sp[:, b + SP_GLU_B:b + SP_GLU_B + 2] = _cols(inp["s5_glu_b"][l])
        sp[:, b + SP_ARE:b + SP_ARE + 8] = _cols(inp["s5_a_re"][l].reshape(-1))
        sp[:, b + SP_AIM:b + SP_AIM + 8] = _cols(inp["s5_a_im"][l].reshape(-1))
        ldt = np.repeat(np.asarray(inp["s5_log_dt"][l], np.float32), 64)
        sp[:, b + SP_LDT:b + SP_LDT + 8] = _cols(ldt)
        q = l * BPL
        bp[:, q + BP_FOXB:q + BP_FOXB + 8] = np.asarray(inp["fox_f_b"][l], np.float32)[None, :]
        bp[:, q + BP_GLAB:q + BP_GLAB + 128] = np.asarray(inp["gla_gate_b"][l], np.float32)[None, :]
        bp[:, q + BP_ARE:q + BP_ARE + 1024] = np.asarray(inp["s5_a_re"][l], np.float32).reshape(1, -1)
        bp[:, q + BP_AIM:q + BP_AIM + 1024] = np.asarray(inp["s5_a_im"][l], np.float32).reshape(1, -1)
        bp[:, q + BP_LDT:q + BP_LDT + 1024] = ldt[None, :]
    return sp, bp


def pack_s5(inp):
    bb_re = np.zeros((DEPTH, 256, 1024), np.float32)
    bb_im = np.zeros((DEPTH, 256, 1024), np.float32)
    cc_re = np.zeros((DEPTH, 1024, 256), np.float32)
    cc_im = np.zeros((DEPTH, 1024, 256), np.float32)
    for g in range(16):
        bb_re[:, g * 16:(g + 1) * 16, g * 64:(g + 1) * 64] = np.transpose(inp["s5_b_re"][:, g], (0, 2, 1))
        bb_im[:, g * 16:(g + 1) * 16, g * 64:(g + 1) * 64] = np.transpose(inp["s5_b_im"][:, g], (0, 2, 1))
        cc_re[:, g * 64:(g + 1) * 64, g * 16:(g + 1) * 16] = np.transpose(inp["s5_c_re"][:, g], (0, 2, 1))
        cc_im[:, g * 64:(g + 1) * 64, g * 16:(g + 1) * 16] = np.transpose(inp["s5_c_im"][:, g], (0, 2, 1))
    return bb_re, bb_im, cc_re, cc_im


ALL_STAGES = [(k, l) for l in range(DEPTH) for k in ("ffn1", "mix", "xa", "ffn2")]
BIG = ("ffn1_w_gu", "ffn1_w_down", "ffn2_w_gu", "ffn2_w_down", "w_in", "gla_gate_w", "w_gla_up", "s5_glu_w",
       "w_s5_up", "w_fox_up", "w_mix_out", "xa_w_q", "xa_w_kv", "xa_w_o")


def make_in_maps(inp, cores):
    sp, bp = pack_small(inp)
    bb_re, bb_im, cc_re, cc_im = pack_s5(inp)
    consts = make_consts()
    shared = {k: np.ascontiguousarray(np.asarray(inp[k], np.float32)) for k in BIG}
    gwp = np.zeros((DEPTH, 128, 128), np.float32)
    gwp[:, 0:16, :] = np.asarray(inp["gla_gate_w"], np.float32)
    shared["gla_gate_w"] = gwp
    shared.update(bb_re=bb_re, bb_im=bb_im, cc_re=cc_re, cc_im=cc_im, sp=sp, bp=bp, consts=consts)
    maps = []
    for b in cores:
        m = dict(shared)
        m["x"] = np.ascontiguousarray(np.asarray(inp["x"][b], np.float32))
        m["mem"] = np.ascontiguousarray(np.asarray(inp["mem"][b], np.float32))
        maps.append(m)
    return maps


def kernel(**inputs):
    nc, C = build_program(ALL_STAGES)
    maps = make_in_maps(inputs, list(range(8)))
    res = run_bass_kernel_spmd(nc, maps, core_ids=list(range(8)))
    return np.stack([r["y"] for r in res.results], axis=0).astype(np.float32)
```

```python
import numpy as np
from contextlib import ExitStack
import concourse.bass as bass
import concourse.mybir as mybir
from concourse.bass_utils import run_bass_kernel_spmd

F32 = mybir.dt.float32
BF16 = mybir.dt.bfloat16
AF = mybir.ActivationFunctionType
ALU = mybir.AluOpType

DEPTH = 4
D = 1024
SEQ = 2048
NMEM = 256
DFF = 2816
DIN = 5656
EPS = 1e-6
O_GQ, O_GK, O_GV, O_GR, O_GD, O_SU, O_FQ, O_FK, O_FV, O_FF, O_GATES = 0, 128, 256, 512, 768, 784, 1040, 1552, 2064, 2576, 2584

SP_FFN1_PRE, SP_FFN1_POST, SP_MIX_PRE, SP_MIX_POST, SP_XA_PRE, SP_XA_MEM, SP_XA_POST, SP_FFN2_PRE, SP_FFN2_POST = 0, 8, 16, 24, 32, 40, 48, 56, 64
SP_GLA_B, SP_GLA_NG, SP_S5_D, SP_GLU_B, SP_ARE, SP_AIM, SP_LDT = 72, 73, 75, 77, 79, 87, 95
SPL = 103
BP_FOXB, BP_GLAB, BP_ARE, BP_AIM, BP_LDT = 0, 8, 136, 1160, 2184
BPL = 3208
C_ID, C_TI, C_MU, C_ML, C_BO, C_HM, C_SM, C_IT, C_IP, C_SEL, C_M01 = 0, 128, 256, 384, 512, 640, 644, 900, 1028, 1029, 1037
NCONST = 1039

ENGS = ("pe", "act", "dve", "pool", "sp")


class Res:
    __slots__ = ("w", "rs", "excl")

    def __init__(self, rs):
        self.w = None
        self.rs = rs
        self.excl = False


class Tk:
    __slots__ = ("key", "val", "clk")

    def __init__(self, key, val, clk):
        self.key = key
        self.val = val
        self.clk = clk


class Prog:
    def __init__(self, nc):
        self.nc = nc
        self.q = {e: [] for e in ENGS}
        self.sems = {}
        self.cnt = {}
        self.known = {e: {} for e in ENGS}
        self.res = {}
        self.grave = {}
        self.pe_pending = []
        self.n_wait = 0
        self.n_op = 0
        for e in ENGS:
            self._sem("E_" + e)

    def _sem(self, key):
        if key not in self.sems:
            self.sems[key] = self.nc.alloc_semaphore(key)
            self.cnt[key] = 0
        return self.sems[key]

    def R(self, *key):
        r = self.res.get(key)
        if r is None:
            r = Res(list(self.grave.values()))
            r.excl = (key[0] == "ps")
            self.res[key] = r
        return r

    def free(self, prefix):
        dead = [k for k in self.res if k[0] == prefix or (isinstance(k[0], str) and k[0].startswith(prefix + "."))]
        for k in dead:
            r = self.res.pop(k)
            for t in ([r.w] if r.w is not None else []) + r.rs:
                g = self.grave.get(t.key)
                if g is None or g.val < t.val:
                    self.grave[t.key] = t

    def _deps(self, eng, reads, writes):
        if eng != "pe" and self.pe_pending:
            for (rr, ww) in self.pe_pending:
                for w in writes:
                    assert all(w is not x for x in rr) and all(w is not x for x in ww), "write to resource with pending PE access"
                for r in reads:
                    assert all(r is not x for x in ww), "read of resource with pending PE write"
        tks = []
        own = "E_" + eng
        for r in reads:
            if r.w is not None:
                tks.append(r.w)
            if r.excl:
                tks.extend(t for t in r.rs if t.key != own)
        for w in writes:
            if w.w is not None:
                tks.append(w.w)
            tks.extend(w.rs)
        kn = self.known[eng]
        need = {}
        for t in tks:
            if eng == "pe" and t.key == "E_pe":
                continue
            if kn.get(t.key, 0) >= t.val:
                continue
            o = need.get(t.key)
            if o is None or o.val < t.val:
                need[t.key] = t
        for key, t in need.items():
            if kn.get(key, 0) >= t.val:
                continue
            self.q[eng].append(("wait", key, t.val))
            self.n_wait += 1
            for k2, v2 in t.clk.items():
                if kn.get(k2, 0) < v2:
                    kn[k2] = v2
            kn[key] = max(kn.get(key, 0), t.val)

    def op(self, eng, fn, reads=(), writes=(), inc=True):
        self._deps(eng, reads, writes)
        self.n_op += 1
        if not inc:
            self.q[eng].append(("op", fn, None, 0))
            self.pe_pending.append((list(reads), list(writes)))
            return None
        key = "E_" + eng
        self.cnt[key] += 1
        clk = dict(self.known[eng])
        clk[key] = self.cnt[key]
        tk = Tk(key, self.cnt[key], clk)
        self.q[eng].append(("op", fn, key, 1))
        allr = [list(reads)]
        allw = [list(writes)]
        if eng == "pe" and self.pe_pending:
            for (rr, ww) in self.pe_pending:
                allr.append(rr)
                allw.append(ww)
            self.pe_pending = []
        for ww in allw:
            for w in ww:
                w.w = tk
                w.rs = []
        for rr in allr:
            for r in rr:
                if r.w is not tk:
                    r.rs.append(tk)
        return tk

    def dma(self, eng, out, in_, semkey, reads=(), writes=()):
        self._deps(eng, reads, writes)
        self._sem(semkey)
        self.cnt[semkey] += 16
        clk = dict(self.known[eng])
        clk[semkey] = self.cnt[semkey]
        tk = Tk(semkey, self.cnt[semkey], clk)
        self.q[eng].append(("op", lambda e: e.dma_start(out=out, in_=in_), semkey, 16))
        for w in writes:
            w.w = tk
            w.rs = []
        for r in reads:
            r.rs.append(tk)
        return tk

    def mm(self, out, lhsT, rhs, start=True, stop=True, r=(), w=(), inc=True):
        return self.op("pe", lambda e: e.matmul(out, lhsT=lhsT, rhs=rhs, start=start, stop=stop), r, w, inc)

    def tr(self, out, in_, ident, r=(), w=()):
        return self.op("pe", lambda e: e.transpose(out, in_, ident), r, w, True)

    def act(self, out, in_, func, r=(), w=(), bias=None, scale=None):
        kw = {}
        if bias is not None:
            kw["bias"] = bias
        if scale is not None:
            kw["scale"] = scale
        return self.op("act", lambda e: e.activation(out=out, in_=in_, func=func, **kw), r, w)

    def tt(self, out, in0, in1, op, r=(), w=(), eng="dve"):
        return self.op(eng, lambda e: e.tensor_tensor(out=out, in0=in0, in1=in1, op=op), r, w)

    def ts(self, out, in0, s1, op0, s2=None, op1=None, r=(), w=(), eng="dve"):
        if op1 is None:
            return self.op(eng, lambda e: e.tensor_scalar(out=out, in0=in0, scalar1=s1, scalar2=None, op0=op0), r, w)
        return self.op(eng, lambda e: e.tensor_scalar(out=out, in0=in0, scalar1=s1, scalar2=s2, op0=op0, op1=op1), r, w)

    def stt(self, out, in0, scalar, in1, op0, op1, r=(), w=(), eng="dve"):
        return self.op(eng, lambda e: e.scalar_tensor_tensor(out=out, in0=in0, scalar=scalar, in1=in1, op0=op0, op1=op1), r, w)

    def cp(self, out, in_, r=(), w=(), eng="dve"):
        if eng == "act":
            return self.op("act", lambda e: e.copy(out=out, in_=in_), r, w)
        return self.op(eng, lambda e: e.tensor_copy(out=out, in_=in_), r, w)

    def recip(self, out, in_, r=(), w=()):
        return self.op("dve", lambda e: e.reciprocal(out=out, in_=in_), r, w)

    def memset(self, ap, val, w=(), eng="dve"):
        return self.op(eng, lambda e: e.memset(ap, val), (), w)

    def finish(self):
        nc = self.nc
        sems = self.sems
        q = self.q

        def replay(name):
            def body(e):
                for it in q[name]:
                    if it[0] == "wait":
                        e.wait_ge(sems[it[1]], it[2])
                    else:
                        ins = it[1](e)
                        if it[2] is not None:
                            ins.then_inc(sems[it[2]], it[3])
            return body

        with nc.Block() as block:
            block.sync(replay("sp"))
            block.tensor(replay("pe"))
            block.scalar(replay("act"))
            block.vector(replay("dve"))
            block.gpsimd(replay("pool"))


class Ring:
    def __init__(self, P, name, tiles):
        self.P = P
        self.name = name
        self.tiles = tiles
        self.i = 0

    def next(self):
        k = self.i % len(self.tiles)
        self.i += 1
        return self.tiles[k], self.P.R(self.name, k), f"D_{self.name}{k}"


class Ctx:
    pass


_UID = [0]


def sbt(nc, name, shape, dt):
    _UID[0] += 1
    return nc.sbuf_tensor(f"{name}_{_UID[0]}", list(shape), dt)


def build_program(stages, dbg=None, branches=("s5", "gla", "fox"), dbg_branch=None, gla_ntiles=16, gla_stop=99):
    nc = bass.Bass("TRN2", target_bir_lowering=False)
    P = Prog(nc)
    C = Ctx()
    C.nc, C.P = nc, P
    C.branches, C.dbg_branch = branches, dbg_branch
    C.gla_ntiles = gla_ntiles
    C.gla_stop = gla_stop
    dr = {}

    def din(name, shape):
        dr[name] = nc.dram_tensor(name, list(shape), F32, kind="ExternalInput").ap()
        return dr[name]

    din("x", [SEQ, D])
    din("mem", [NMEM, D])
    for w in (1, 2):
        din(f"ffn{w}_w_gu", [DEPTH, D, 2 * DFF])
        din(f"ffn{w}_w_down", [DEPTH, DFF, D])
    din("w_in", [DEPTH, D, DIN])
    din("gla_gate_w", [DEPTH, 128, 128])
    din("w_gla_up", [DEPTH, 256, D])
    din("s5_glu_w", [DEPTH, 256, 256])
    din("w_s5_up", [DEPTH, 256, D])
    din("w_fox_up", [DEPTH, 512, D])
    din("w_mix_out", [DEPTH, D, D])
    din("xa_w_q", [DEPTH, D, D])
    din("xa_w_kv", [DEPTH, D, 2 * D])
    din("xa_w_o", [DEPTH, D, D])
    din("bb_re", [DEPTH, 256, 1024])
    din("bb_im", [DEPTH, 256, 1024])
    din("cc_re", [DEPTH, 1024, 256])
    din("cc_im", [DEPTH, 1024, 256])
    din("sp", [128, DEPTH * SPL])
    din("bp", [128, DEPTH * BPL])
    din("consts", [128, NCONST])
    y = nc.dram_tensor("y", [SEQ, D], F32, kind="ExternalOutput").ap()
    C.dr = dr
    if dbg is not None:
        C.dbg = nc.dram_tensor("dbg", list(dbg), F32, kind="ExternalOutput").ap()

    with ExitStack() as es:
        def sb(name, shape, dt):
            return es.enter_context(sbt(nc, name, shape, dt))

        C.xT = sb("xT", [128, 8, SEQ], F32)
        C.cst = sb("cst", [128, NCONST], F32)
        C.sp = sb("spar", [128, DEPTH * SPL], F32)
        C.onesb = sb("onesb", [128, 128], BF16)
        C.tib = sb("tib", [128, 128], BF16)
        C.mub = sb("mub", [128, 2, 128], BF16)
        C.bob = sb("bob", [128, 128], BF16)
        C.smb = sb("smb", [128, 256], BF16)
        C.epsc = sb("epsc", [128, 1], F32)
        C.ring8 = Ring(P, "r8", [sb(f"r8_{i}", [128, 4096], BF16) for i in range(3)])
        C.ring2 = Ring(P, "r2", [sb(f"r2_{i}", [128, 1024], BF16) for i in range(3)])
        C.ps = [es.enter_context(nc.psum_tensor(f"ps{i}", [128, 512], F32)) for i in range(8)]
        C.psi = 0

        P.dma("sp", C.cst[:], dr["consts"], "D_c0", writes=[P.R("cst")])
        P.dma("sp", C.sp[:], dr["sp"], "D_c1", writes=[P.R("spar")])
        P.memset(C.onesb[:], 1.0, w=[P.R("cstb")])
        P.memset(C.epsc[:], EPS, w=[P.R("cstb")])
        P.cp(C.tib[:], C.cst[:, C_TI:C_TI + 128], r=[P.R("cst")], w=[P.R("cstb")])
        P.cp(C.mub[:, 0, :], C.cst[:, C_MU:C_MU + 128], r=[P.R("cst")], w=[P.R("cstb")])
        P.cp(C.mub[:, 1, :], C.cst[:, C_ML:C_ML + 128], r=[P.R("cst")], w=[P.R("cstb")])
        P.cp(C.bob[:], C.cst[:, C_BO:C_BO + 128], r=[P.R("cst")], w=[P.R("cstb")])
        P.cp(C.smb[:], C.cst[:, C_SM:C_SM + 256], r=[P.R("cst")], w=[P.R("cstb")])
        C.ident = C.cst[:, C_ID:C_ID + 128]

        load_x(C)
        for (kind, l) in stages:
            if kind == "ffn1":
                ffn(C, l, 1)
            elif kind == "ffn2":
                ffn(C, l, 2)
            elif kind == "xa":
                xattn(C, l)
            elif kind == "mix":
                mixer(C, l)
        store_x(C, y)
        P.finish()
    C.stats = (P.n_op, P.n_wait)
    return nc, C


def bank(C, lo=0, hi=4):
    k = lo + (C.psi % (hi - lo))
    C.psi += 1
    return C.ps[k], C.P.R("ps", k)


def load_x(C):
    P, nc = C.P, C.nc
    xd = C.dr["x"]
    with ExitStack() as es:
        st = [es.enter_context(sbt(nc, f"xst{i}", [128, D], F32)) for i in range(2)]
        for i in range(16):
            s = i % 2
            P.dma("sp", st[s][:], xd[i * 128:(i + 1) * 128, :], f"D_xs{s}", writes=[P.R("xst", s)])
            for g in range(2):
                ps, pr = bank(C)
                for c4 in range(4):
                    c = g * 4 + c4
                    P.tr(ps[:, c4 * 128:(c4 + 1) * 128], st[s][:, c * 128:(c + 1) * 128], C.ident,
                         r=[P.R("xst", s), P.R("cst")], w=[pr])
                P.cp(C.xT[:, g * 4:(g + 1) * 4, i * 128:(i + 1) * 128],
                     ps[:].rearrange("p (c t) -> p c t", c=4), r=[pr], w=[P.R("x", i // 4)],
                     eng=("act" if g else "dve"))
        P.free("xst")


def store_x(C, y):
    P, nc = C.P, C.nc
    with ExitStack() as es:
        st = [es.enter_context(sbt(nc, f"yst{i}", [128, D], F32)) for i in range(2)]
        for i in range(16):
            s = i % 2
            for g in range(2):
                ps, pr = bank(C)
                for c4 in range(4):
                    c = g * 4 + c4
                    P.tr(ps[:, c4 * 128:(c4 + 1) * 128], C.xT[:, c, i * 128:(i + 1) * 128], C.ident,
                         r=[P.R("x", i // 4), P.R("cst")], w=[pr])
                P.cp(st[s][:, g * 512:(g + 1) * 512], ps[:], r=[pr], w=[P.R("yst", s)],
                     eng=("act" if g else "dve"))
            P.dma("sp", y[i * 128:(i + 1) * 128, :], st[s][:], f"D_ys{s}", reads=[P.R("yst", s)])
        for s in range(2):
            key = f"D_ys{s}"
            P.q["sp"].append(("wait", key, P.cnt[key]))
        P.free("yst")


def gcol(C, l, off, c):
    return C.sp[:, l * SPL + off + c:l * SPL + off + c + 1]


def norm_stats(C, src3, ntok, sq, rs, r, rres):
    P = C.P
    ps, pr = bank(C)
    for c in range(8):
        k = c % 2
        P.act(sq[:, k, :ntok], src3[:, c, :], AF.Square, r=r, w=[P.R("nrm.sq", k)])
        P.mm(ps[:, :ntok], C.onesb[:], sq[:, k, :ntok], start=(c == 0), stop=(c == 7),
             r=[P.R("nrm.sq", k), P.R("cstb")], w=[pr], inc=True)
    P.act(rs[:, :ntok], ps[:, :ntok], AF.Ln, r=[pr, P.R("cstb")], w=[rres], bias=C.epsc[:], scale=1.0 / D)
    P.act(rs[:, :ntok], rs[:, :ntok], AF.Exp, r=[rres], w=[rres], scale=-0.5)


def prenorm(C, l, goff, src3, ntok, dst3, sq, rs, rsrc, rdst, gsp=None):
    P = C.P
    rres = P.R("nrm.rs")
    norm_stats(C, src3, ntok, sq, rs, rsrc, rres)
    for c in range(8):
        P.stt(dst3[:, c, :], src3[:, c, :], gcol(C, l, goff, c), rs[:, :ntok], ALU.mult, ALU.mult,
              r=list(rsrc) + [rres, P.R("spar")], w=rdst)


def postnorm_add(C, l, goff, ysb3, wgt, tt, sq, rs, ryres):
    P = C.P
    rres = P.R("nrm.rs")
    norm_stats(C, ysb3, 512, sq, rs, [ryres], rres)
    for c in range(8):
        P.stt(ysb3[:, c, :], ysb3[:, c, :], gcol(C, l, goff, c), rs[:, :512], ALU.mult, ALU.mult,
              r=[ryres, rres, P.R("spar")], w=[ryres])
    xt = C.xT[:, :, tt * 512:(tt + 1) * 512]
    P.stt(xt, ysb3, float(wgt), xt, ALU.mult, ALU.add, r=[ryres, P.R("x", tt)], w=[P.R("x", tt)])


def wload(C, dst, src, res, sem):
    C.P.dma("pool", dst, src, sem, writes=[res])


def ffn(C, l, which):
    P, nc = C.P, C.nc
    wgu = C.dr[f"ffn{which}_w_gu"][l].rearrange("(kc p) n -> p kc n", p=128)
    wdn = C.dr[f"ffn{which}_w_down"][l].rearrange("(j p) n -> p j n", p=128)
    pre = SP_FFN1_PRE if which == 1 else SP_FFN2_PRE
    post = SP_FFN1_POST if which == 1 else SP_FFN2_POST
    for half in range(2):
        with ExitStack() as es:
            aT = es.enter_context(sbt(nc, "aT", [128, 22, 1024], BF16))
            sq = es.enter_context(sbt(nc, "sq", [128, 2, 512], BF16))
            rs = es.enter_context(sbt(nc, "rs", [128, 512], F32))
            with ExitStack() as es1:
                hT = es1.enter_context(sbt(nc, "hT", [128, 8, 1024], BF16))
                sg = [es1.enter_context(sbt(nc, f"sg{i}", [128, 512], F32)) for i in range(2)]
                for t2 in range(2):
                    tt = half * 2 + t2
                    prenorm(C, l, pre, C.xT[:, :, tt * 512:(tt + 1) * 512], 512, hT[:, :, t2 * 512:(t2 + 1) * 512],
                            sq, rs, [P.R("x", tt)], [P.R("ffn.h", t2)])
                k = 0
                for jb in range(11):
                    slot, sres, ssem = C.ring8.next()
                    sv = slot[:].rearrange("p (kc g n) -> p kc g n", kc=8, g=2)
                    wload(C, sv[:, :, 0, :], wgu[:, :, jb * 256:(jb + 1) * 256], sres, ssem)
                    wload(C, sv[:, :, 1, :], wgu[:, :, DFF + jb * 256:DFF + (jb + 1) * 256], sres, ssem)
                    for jj in range(2):
                        j = jb * 2 + jj
                        for t2 in range(2):
                            psA, prA = bank(C)
                            psB, prB = bank(C)
                            for g, (ps_, pr_) in enumerate(((psA, prA), (psB, prB))):
                                for kc in range(8):
                                    P.mm(ps_[:], sv[:, kc, g, jj * 128:(jj + 1) * 128], hT[:, kc, t2 * 512:(t2 + 1) * 512],
                                         start=(kc == 0), stop=(kc == 7), r=[sres, P.R("ffn.h", t2)], w=[pr_], inc=(kc == 7))
                            s = k % 2
                            k += 1
                            P.act(sg[s][:], psA[:], AF.Silu, r=[prA], w=[P.R("ffn.sg", s)])
                            P.tt(aT[:, j, t2 * 512:(t2 + 1) * 512], psB[:], sg[s][:], ALU.mult,
                                 r=[prB, P.R("ffn.sg", s)], w=[P.R("ffn.a", j, t2)])
            P.free("ffn.h")
            P.free("ffn.sg")
            ysb = es.enter_context(sbt(nc, "ysb", [128, 8, 1024], F32))
            for mq in range(4):
                acc = [[(C.ps[4 + m2 * 2 + t2], P.R("ps", 4 + m2 * 2 + t2)) for t2 in range(2)] for m2 in range(2)]
                for jb in range(11):
                    slot, sres, ssem = C.ring2.next()
                    sv = slot[:, 0:512].rearrange("p (j n) -> p j n", j=2)
                    wload(C, sv, wdn[:, jb * 2:jb * 2 + 2, mq * 256:(mq + 1) * 256], sres, ssem)
                    for jj in range(2):
                        j = jb * 2 + jj
                        for m2 in range(2):
                            for t2 in range(2):
                                P.mm(acc[m2][t2][0][:], sv[:, jj, m2 * 128:(m2 + 1) * 128], aT[:, j, t2 * 512:(t2 + 1) * 512],
                                     start=(j == 0), stop=(j == 21), r=[sres, P.R("ffn.a", j, t2)], w=[acc[m2][t2][1]],
                                     inc=(j == 21 or (jj == 1 and m2 == 1 and t2 == 1)))
                for m2 in range(2):
                    for t2 in range(2):
                        P.cp(ysb[:, mq * 2 + m2, t2 * 512:(t2 + 1) * 512], acc[m2][t2][0][:], r=[acc[m2][t2][1]],
                             w=[P.R("ffn.y", t2)], eng=("act" if (m2 + t2) % 2 else "dve"))
            for t2 in range(2):
                postnorm_add(C, l, post, ysb[:, :, t2 * 512:(t2 + 1) * 512], 0.5, half * 2 + t2, sq, rs, P.R("ffn.y", t2))
        P.free("ffn")
        P.free("nrm")


def xattn(C, l):
    P, nc = C.P, C.nc
    wq_d = C.dr["xa_w_q"][l].rearrange("(kc p) n -> p kc n", p=128)
    wkv_d = C.dr["xa_w_kv"][l].rearrange("(kc p) n -> p kc n", p=128)
    wo_d = C.dr["xa_w_o"][l].rearrange("(kc p) n -> p kc n", p=128)
    with ExitStack() as es:
        def sb(name, shape, dt):
            return es.enter_context(sbt(nc, name, shape, dt))
        memn = sb("memn", [128, 8, NMEM], BF16)
        kT = sb("kT", [128, 8, NMEM], BF16)
        v = sb("v", [128, 2, D], BF16)
        wq = sb("wq", [128, 8, D], BF16)
        sq = sb("sq", [128, 2, 512], BF16)
        rs = sb("rs", [128, 512], F32)
        hT = sb("hT", [128, 8, 512], BF16)
        qT = sb("qT", [128, 8, 512], BF16)
        oT = sb("oT", [128, 8, 512], BF16)
        ysb = sb("ysb", [128, 8, 512], F32)
        pt = [sb(f"pt{i}", [128, 512], BF16) for i in range(4)]
        rl = [sb(f"rl{i}", [128, 512], F32) for i in range(2)]
        for hf in range(2):
            wload(C, wq[:, :, hf * 512:(hf + 1) * 512], wq_d[:, :, hf * 512:(hf + 1) * 512], P.R("xa.wq"), "D_xawq")
        with ExitStack() as es1:
            memT = es1.enter_context(sbt(nc, "memT", [128, 8, NMEM], F32))
            mst = [es1.enter_context(sbt(nc, f"mst{i}", [128, D], F32)) for i in range(2)]
            for i in range(2):
                P.dma("sp", mst[i][:], C.dr["mem"][i * 128:(i + 1) * 128, :], f"D_ms{i}", writes=[P.R("xa.mst", i)])
                for g in range(2):
                    ps, pr = bank(C)
                    for c4 in range(4):
                        c = g * 4 + c4
                        P.tr(ps[:, c4 * 128:(c4 + 1) * 128], mst[i][:, c * 128:(c + 1) * 128], C.ident,
                             r=[P.R("xa.mst", i), P.R("cst")], w=[pr])
                    P.cp(memT[:, g * 4:(g + 1) * 4, i * 128:(i + 1) * 128], ps[:].rearrange("p (c t) -> p c t", c=4),
                         r=[pr], w=[P.R("xa.memT")])
            prenorm(C, l, SP_XA_MEM, memT[:], NMEM, memn[:], sq, rs, [P.R("xa.memT")], [P.R("xa.memn")])
        P.free("xa.mst")
        P.free("xa.memT")
        for blk in range(4):
            slot, sres, ssem = C.ring8.next()
            sv = slot[:].rearrange("p (kc n) -> p kc n", kc=8)
            wload(C, sv, wkv_d[:, :, blk * 512:(blk + 1) * 512], sres, ssem)
            if blk < 2:
                for o4 in range(4):
                    oc = blk * 4 + o4
                    ps, pr = bank(C)
                    for kc in range(8):
                        P.mm(ps[:, :NMEM], sv[:, kc, o4 * 128:(o4 + 1) * 128], memn[:, kc, :], start=(kc == 0), stop=(kc == 7),
                             r=[sres, P.R("xa.memn")], w=[pr], inc=(kc == 7))
                    P.cp(kT[:, oc, :], ps[:, :NMEM], r=[pr], w=[P.R("xa.kT")], eng="act")
            else:
                vb = blk - 2
                for mt in range(2):
                    ps, pr = bank(C)
                    for kc in range(8):
                        P.mm(ps[:], memn[:, kc, mt * 128:(mt + 1) * 128], sv[:, kc, :], start=(kc == 0), stop=(kc == 7),
                             r=[sres, P.R("xa.memn")], w=[pr], inc=(kc == 7))
                    P.cp(v[:, mt, vb * 512:(vb + 1) * 512], ps[:], r=[pr], w=[P.R("xa.v")], eng="act")
        prenorm(C, l, SP_XA_PRE, C.xT[:, :, 0:512], 512, hT[:], sq, rs, [P.R("x", 0)], [P.R("xa.h")])
        for tt in range(4):
            for oc in range(8):
                ps, pr = bank(C, 0, 8)
                for kc in range(8):
                    P.mm(ps[:], wq[:, kc, oc * 128:(oc + 1) * 128], hT[:, kc, :], start=(kc == 0), stop=(kc == 7),
                         r=[P.R("xa.wq"), P.R("xa.h")], w=[pr], inc=(kc == 7))
                P.cp(qT[:, oc, :], ps[:], r=[pr], w=[P.R("xa.q")], eng=("act" if oc % 2 else "dve"))

            def emit_S(h):
                out = []
                for mt in range(2):
                    ps, pr = bank(C, 0, 8)
                    for dc in range(2):
                        P.mm(ps[:], kT[:, 2 * h + dc, mt * 128:(mt + 1) * 128], qT[:, 2 * h + dc, :], start=(dc == 0), stop=(dc == 1),
                             r=[P.R("xa.kT"), P.R("xa.q")], w=[pr], inc=(dc == 1))
                    out.append((ps, pr))
                return out
            Sp = {0: emit_S(0)}
            for h in range(4):
                if h + 1 < 4:
                    Sp[h + 1] = emit_S(h + 1)
                pts = []
                for mt, (ps, pr) in enumerate(Sp.pop(h)):
                    k = (h * 2 + mt) % 4
                    P.act(pt[k][:], ps[:], AF.Exp, r=[pr], w=[P.R("xa.pt", k)], scale=1.0 / 16.0)
                    pts.append((pt[k], P.R("xa.pt", k)))
                ps, pr = bank(C, 0, 8)
                for mt in range(2):
                    P.mm(ps[:], C.onesb[:], pts[mt][0][:], start=(mt == 0), stop=(mt == 1), r=[pts[mt][1], P.R("cstb")], w=[pr],
                         inc=(mt == 1))
                P.act(rl[h % 2][:], ps[:], AF.Ln, r=[pr], w=[P.R("xa.rl", h % 2)])
                P.act(rl[h % 2][:], rl[h % 2][:], AF.Exp, r=[P.R("xa.rl", h % 2)], w=[P.R("xa.rl", h % 2)], scale=-1.0)
                for ec in range(2):
                    ps, pr = bank(C, 0, 8)
                    for mt in range(2):
                        P.mm(ps[:], v[:, mt, (2 * h + ec) * 128:(2 * h + ec + 1) * 128], pts[mt][0][:], start=(mt == 0), stop=(mt == 1),
                             r=[pts[mt][1], P.R("xa.v")], w=[pr], inc=(mt == 1))
                    P.tt(oT[:, 2 * h + ec, :], ps[:], rl[h % 2][:], ALU.mult, r=[pr, P.R("xa.rl", h % 2)], w=[P.R("xa.o")])
            if tt + 1 < 4:
                prenorm(C, l, SP_XA_PRE, C.xT[:, :, (tt + 1) * 512:(tt + 2) * 512], 512, hT[:], sq, rs,
                        [P.R("x", tt + 1)], [P.R("xa.h")])
            for ob in range(2):
                slot, sres, ssem = C.ring8.next()
                sv = slot[:].rearrange("p (kc n) -> p kc n", kc=8)
                wload(C, sv, wo_d[:, :, ob * 512:(ob + 1) * 512], sres, ssem)
                for m4 in range(4):
                    mc = ob * 4 + m4
                    ps, pr = bank(C)
                    for kc in range(8):
                        P.mm(ps[:], sv[:, kc, m4 * 128:(m4 + 1) * 128], oT[:, kc, :], start=(kc == 0), stop=(kc == 7),
                             r=[sres, P.R("xa.o")], w=[pr], inc=(kc == 7))
                    P.cp(ysb[:, mc, :], ps[:], r=[pr], w=[P.R("xa.y")], eng=("act" if mc % 2 else "dve"))
            postnorm_add(C, l, SP_XA_POST, ysb[:], 1.0, tt, sq, rs, P.R("xa.y"))
    P.free("xa")
    P.free("nrm")


PI = float(np.pi)
TWO_PI = float(2 * np.pi)


def sin_of(C, out, ang, turns, tf, tg, ti, r, w, rt):
    P = C.P
    P.ts(tf, ang, 1.0 / TWO_PI, ALU.mult, float(turns), ALU.add, r=r, w=[rt])
    P.cp(ti, tf, r=[rt], w=[rt])
    P.cp(tg, ti, r=[rt], w=[rt])
    P.tt(tf, tf, tg, ALU.subtract, r=[rt], w=[rt])
    P.ts(tg, tf, 0.5, ALU.is_gt, r=[rt], w=[rt])
    P.tt(tf, tf, tg, ALU.subtract, r=[rt], w=[rt])
    P.ts(tg, tf, -0.5, ALU.is_lt, r=[rt], w=[rt])
    P.tt(tf, tf, tg, ALU.add, r=[rt], w=[rt])
    P.ts(tf, tf, 0.4999999, ALU.min, -0.4999999, ALU.max, r=[rt], w=[rt])
    P.act(out, tf, AF.Sin, r=[rt], w=w, scale=TWO_PI)


def proj_fm(C, wv, wres, c0, ncols, hT, evac):
    P = C.P
    for tt in range(4):
        ps, pr = bank(C)
        for kc in range(8):
            P.mm(ps[:ncols, :], wv[:, kc, c0:c0 + ncols], hT[:, kc, tt * 512:(tt + 1) * 512], start=(kc == 0), stop=(kc == 7),
                 r=[wres, P.R("mx.h", tt)], w=[pr], inc=(kc == 7))
        evac(ps, pr, tt)


def mixer(C, l):
    P, nc = C.P, C.nc
    win = C.dr["w_in"][l].rearrange("(kc p) n -> p kc n", p=128)
    with ExitStack() as es:
        s5T = es.enter_context(sbt(nc, "s5T", [128, 2, SEQ], BF16))
        glaT = es.enter_context(sbt(nc, "glaT", [128, 2, SEQ], BF16))
        foT = es.enter_context(sbt(nc, "foT", [128, 4, SEQ], BF16))
        hT = es.enter_context(sbt(nc, "hTm", [128, 8, SEQ], BF16))
        with ExitStack() as e0:
            sq = e0.enter_context(sbt(nc, "sq", [128, 2, 512], BF16))
            rs = e0.enter_context(sbt(nc, "rs", [128, 512], F32))
            for tt in range(4):
                prenorm(C, l, SP_MIX_PRE, C.xT[:, :, tt * 512:(tt + 1) * 512], 512, hT[:, :, tt * 512:(tt + 1) * 512],
                        sq, rs, [P.R("x", tt)], [P.R("mx.h", tt)])
        P.free("nrm")
        if "s5" in C.branches:
            s5_branch(C, l, win, hT, s5T)
        if "gla" in C.branches:
            gla_branch(C, l, win, hT, glaT)
        if "fox" in C.branches:
            fox_branch(C, l, win, hT, foT)
        if C.dbg_branch is not None:
            src = {"s5": (s5T, 2, "mx.s5"), "gla": (glaT, 2, "mx.gla"), "fox": (foT, 4, "mx.fo")}[C.dbg_branch]
            for tt in range(4):
                rr = [P.R(src[2], m, tt) for m in range(src[1])]
                P.cp(C.xT[:, 0:src[1], tt * 512:(tt + 1) * 512], src[0][:, :, tt * 512:(tt + 1) * 512], r=rr, w=[P.R("x", tt)])
        else:
            merge(C, l, win, hT, s5T, glaT, foT)
    P.free("mx")
    P.free("nrm")


def s5_branch(C, l, win, hT, s5T):
    P, nc = C.P, C.nc
    bbre, bbim = C.dr["bb_re"][l], C.dr["bb_im"][l]
    ccre = C.dr["cc_re"][l].rearrange("(c p) n -> p c n", p=128)
    ccim = C.dr["cc_im"][l].rearrange("(c p) n -> p c n", p=128)
    gluw_d = C.dr["s5_glu_w"][l].rearrange("(kc p) n -> p kc n", p=128)
    b = l * SPL
    with ExitStack() as es:
        def sb(name, shape, dt):
            return es.enter_context(sbt(nc, name, shape, dt))
        uT = sb("uT", [128, SEQ], BF16)
        ENz = sb("ENz", [128, 2, 512], F32)
        EP = sb("EP", [128, 8, 128], F32)
        BB = sb("BB", [128, 2, 512], BF16)
        CC = sb("CC", [128, 4, 2, 128], BF16)
        Tb = [sb(f"Tb{i}", [128, 4, 512], BF16) for i in range(2)]
        ntib = sb("ntib", [128, 128], BF16)
        W = sb("W", [128, 8, 128], F32)
        P4 = sb("P4", [128, 16, 128], BF16)
        tmp = [sb(f"t{i}", [128, 512], F32) for i in range(4)]
        sc = sb("sc", [128, 64], F32)
        carry = sb("carry", [128, 8], F32)
        ysb = sb("ys5", [128, 128], F32)
        ti = sb("ti", [128, 128], mybir.dt.int32)
        rsc, rtb, rtab = P.R("s5.sc"), P.R("s5.tb"), P.R("s5.tab")
        slot, sres, ssem = C.ring8.next()
        wsu = slot[:, 0:2048].rearrange("p (kc n) -> p kc n", kc=8)
        wload(C, wsu, win[:, :, O_SU:O_SU + 256], sres, ssem)
        iota = C.cst[:, C_IT:C_IT + 128]
        P.ts(ntib[:], C.cst[:, C_TI:C_TI + 128], -1.0, ALU.mult, r=[P.R("cst")], w=[P.R("s5.ntib")])
        for hc in range(2):
            def ev(ps, pr, tt):
                P.cp(uT[:, tt * 512:(tt + 1) * 512], ps[:], r=[pr], w=[P.R("s5.u", tt)], eng="act")
            proj_fm(C, wsu, sres, hc * 128, 128, hT, ev)
            are = C.sp[:, b + SP_ARE + 4 * hc:b + SP_ARE + 4 * hc + 4]
            aim = C.sp[:, b + SP_AIM + 4 * hc:b + SP_AIM + 4 * hc + 4]
            ldt = C.sp[:, b + SP_LDT + 4 * hc:b + SP_LDT + 4 * hc + 4]
            col = lambda k: sc[:, 4 * k:4 * k + 4]
            dt_, lamre, lr, li, nlr, mag1, sinli, cosli, abre, abim, den, zre, zim, sA, sB = [col(k) for k in range(15)]
            rw = dict(r=[rsc, P.R("spar")], w=[rsc])
            P.act(dt_, ldt, AF.Exp, **rw)
            P.ts(lamre, are, -1e-4, ALU.min, **rw)
            P.tt(lr, lamre, dt_, ALU.mult, **rw)
            P.tt(li, aim, dt_, ALU.mult, **rw)
            P.ts(nlr, lr, -1.0, ALU.mult, **rw)
            P.act(mag1, lr, AF.Exp, **rw)
            sin_of(C, sinli, li, 0.0, sA, sB, ti[:, 0:4], [rsc], [rsc], rsc)
            sin_of(C, cosli, li, 0.25, sA, sB, ti[:, 0:4], [rsc], [rsc], rsc)
            P.tt(abre, mag1, cosli, ALU.mult, **rw)
            P.tt(abim, mag1, sinli, ALU.mult, **rw)
            P.tt(den, lamre, lamre, ALU.mult, **rw)
            P.tt(sB, aim, aim, ALU.mult, **rw)
            P.tt(den, den, sB, ALU.add, **rw)
            P.recip(den, den, **rw)
            P.ts(abre, abre, -1.0, ALU.add, **rw)
            P.tt(sA, abre, lamre, ALU.mult, **rw)
            P.tt(sB, abim, aim, ALU.mult, **rw)
            P.tt(sA, sA, sB, ALU.add, **rw)
            P.tt(zre, sA, den, ALU.mult, **rw)
            P.tt(sA, abim, lamre, ALU.mult, **rw)
            P.tt(sB, abre, aim, ALU.mult, **rw)
            P.tt(sA, sA, sB, ALU.subtract, **rw)
            P.tt(zim, sA, den, ALU.mult, **rw)
            A_, B_ = tmp[0][:, 0:128], tmp[0][:, 128:256]
            S_, Cc_ = tmp[1][:, 0:128], tmp[1][:, 128:256]
            MP, MN = tmp[2][:, 0:128], tmp[2][:, 128:256]
            RR, E1, E2, RG = tmp[3][:, 0:128], tmp[3][:, 128:256], tmp[3][:, 256:384], tmp[3][:, 384:512]
            tw = dict(r=[rtb, rsc, P.R("cst")], w=[rtb])
            for pc in range(4):
                P.ts(A_, iota, li[:, pc:pc + 1], ALU.mult, **tw)
                P.act(MP, iota, AF.Exp, scale=lr[:, pc:pc + 1], **tw)
                P.act(MN, iota, AF.Exp, scale=nlr[:, pc:pc + 1], **tw)
                sin_of(C, S_, A_, 0.0, RR, RG, ti[:], [rtb], [rtb], rtb)
                sin_of(C, Cc_, A_, 0.25, RR, RG, ti[:], [rtb], [rtb], rtb)
                P.tt(EP[:, pc, :], MP, Cc_, ALU.mult, r=[rtb], w=[rtab])
                P.tt(EP[:, 4 + pc, :], MP, S_, ALU.mult, r=[rtb], w=[rtab])
                P.ts(E1, Cc_, zre[:, pc:pc + 1], ALU.mult, **tw)
                P.stt(E1, S_, zim[:, pc:pc + 1], E1, ALU.mult, ALU.add, **tw)
                P.tt(E1, E1, MN, ALU.mult, **tw)
                P.ts(E2, Cc_, zim[:, pc:pc + 1], ALU.mult, **tw)
                P.ts(B_, S_, zre[:, pc:pc + 1], ALU.mult, **tw)
                P.tt(E2, E2, B_, ALU.subtract, **tw)
                P.tt(E2, E2, MN, ALU.mult, **tw)
                for ri, E in enumerate((E1, E2)):
                    ps, pr = bank(C)
                    P.tr(ps[:, 0:128], E, C.ident, r=[rtb, P.R("cst")], w=[pr])
                    P.cp(ENz[:, ri, pc * 128:(pc + 1) * 128], ps[:, 0:128], r=[pr], w=[rtab], eng="act")
            L1r, L1i = EP[:, 0:4, 1], EP[:, 4:8, 1]
            Er, Ei = EP[:, 0:4, 127], EP[:, 4:8, 127]
            ew = dict(r=[rtab, rsc], w=[rsc])
            P.tt(sc[:, 24:28], L1r, Er, ALU.mult, **ew)
            P.tt(sc[:, 28:32], L1i, Ei, ALU.mult, **ew)
            P.tt(sc[:, 16:20], sc[:, 24:28], sc[:, 28:32], ALU.subtract, **ew)
            P.tt(sc[:, 24:28], L1r, Ei, ALU.mult, **ew)
            P.tt(sc[:, 28:32], L1i, Er, ALU.mult, **ew)
            P.tt(sc[:, 20:24], sc[:, 24:28], sc[:, 28:32], ALU.add, **ew)
            wload(C, BB[:, 0, :], bbre[hc * 128:(hc + 1) * 128, hc * 512:(hc + 1) * 512], P.R("s5.BB"), "D_s5b")
            wload(C, BB[:, 1, :], bbim[hc * 128:(hc + 1) * 128, hc * 512:(hc + 1) * 512], P.R("s5.BB"), "D_s5b")
            wload(C, CC[:, :, 0, :], ccre[:, 4 * hc:4 * hc + 4, hc * 128:(hc + 1) * 128], P.R("s5.CC"), "D_s5c")
            wload(C, CC[:, :, 1, :], ccim[:, 4 * hc:4 * hc + 4, hc * 128:(hc + 1) * 128], P.R("s5.CC"), "D_s5c")
            P.memset(carry[:], 0.0, w=[P.R("s5.carry")])
            Wre, Wim = W[:, 0:4, :], W[:, 4:8, :]
            EPre, EPim = EP[:, 0:4, :], EP[:, 4:8, :]
            tv = [t[:].rearrange("p (a b) -> p a b", a=4) for t in tmp]
            rt = [P.R("s5.t", i) for i in range(4)]
            if l == 0 and hc == 0:
                print("[sbuf] s5 remaining", nc.sbuf_bytes_remaining)

            def front(c):
                tok = slice(c * 128, (c + 1) * 128)
                tt = c // 4
                Tc = Tb[c % 2]
                rV = P.R("s5.V", c % 2)
                psr, prr = bank(C)
                psi, pri = bank(C)
                P.mm(psr[:], uT[:, tok], BB[:, 0, :], r=[P.R("s5.u", tt), P.R("s5.BB")], w=[prr])
                P.mm(psi[:], uT[:, tok], BB[:, 1, :], r=[P.R("s5.u", tt), P.R("s5.BB")], w=[pri])
                P.tt(Tc[:, 0, :], psr[:], ENz[:, 0, :], ALU.mult, r=[prr, rtab], w=[rV])
                P.tt(Tc[:, 1, :], psi[:], ENz[:, 1, :], ALU.mult, r=[pri, rtab], w=[rV])
                P.tt(Tc[:, 2, :], psi[:], ENz[:, 0, :], ALU.mult, r=[pri, rtab], w=[rV])
                P.tt(Tc[:, 3, :], psr[:], ENz[:, 1, :], ALU.mult, r=[prr, rtab], w=[rV])
                pw = [(C.ps[4 + 2 * (c % 2) + ri], P.R("ps", 4 + 2 * (c % 2) + ri)) for ri in range(2)]
                for ri in range(2):
                    for pc in range(4):
                        cs = slice(pc * 128, (pc + 1) * 128)
                        P.mm(pw[ri][0][:, cs], Tc[:, 2 * ri, cs], C.tib[:], start=True, stop=False,
                             r=[rV, P.R("cstb")], w=[pw[ri][1]], inc=False)
                        P.mm(pw[ri][0][:, cs], Tc[:, 2 * ri + 1, cs], (ntib[:] if ri == 0 else C.tib[:]), start=False, stop=True,
                             r=[rV, P.R("cstb"), P.R("s5.ntib")], w=[pw[ri][1]], inc=(pc == 3))
                return pw

            def back(c, pw):
                tok = slice(c * 128, (c + 1) * 128)
                tt = c // 4
                for ri in range(2):
                    for pc in range(4):
                        k = ri * 4 + pc
                        P.act(W[:, k, :], pw[ri][0][:, pc * 128:(pc + 1) * 128], AF.Identity, bias=carry[:, k:k + 1],
                              r=[pw[ri][1], P.R("s5.carry")], w=[P.R("s5.W")])
                rW = P.R("s5.W")
                xv = W[:, 0:8, 127].rearrange("p (a b) -> p a b", a=2)
                Lrb = sc[:, 16:20].unsqueeze(1).to_broadcast([128, 2, 4])
                Lib = sc[:, 20:24].unsqueeze(1).to_broadcast([128, 2, 4])
                cw = dict(r=[rW, rsc], w=[rsc])
                P.tt(sc[:, 0:8].rearrange("p (a b) -> p a b", a=2), Lrb, xv, ALU.mult, **cw)
                P.tt(sc[:, 8:16].rearrange("p (a b) -> p a b", a=2), Lib, xv, ALU.mult, **cw)
                P.tt(carry[:, 0:4], sc[:, 0:4], sc[:, 12:16], ALU.subtract, r=[rsc], w=[P.R("s5.carry")])
                P.tt(carry[:, 4:8], sc[:, 4:8], sc[:, 8:12], ALU.add, r=[rsc], w=[P.R("s5.carry")])
                rP = P.R("s5.Zb")
                P.tt(P4[:, 0:4, :], EPre, Wre, ALU.mult, r=[rtab, rW], w=[rP])
                P.stt(P4[:, 4:8, :], EPim, -1.0, Wim, ALU.mult, ALU.mult, r=[rtab, rW], w=[rP])
                P.stt(P4[:, 8:12, :], EPre, -1.0, Wim, ALU.mult, ALU.mult, r=[rtab, rW], w=[rP])
                P.stt(P4[:, 12:16, :], EPim, -1.0, Wre, ALU.mult, ALU.mult, r=[rtab, rW], w=[rP])
                py, pyr = bank(C)
                for pc in range(4):
                    for q4 in range(4):
                        P.mm(py[:, :128], CC[:, pc, q4 // 2, :], P4[:, 4 * q4 + pc, :], start=(pc == 0 and q4 == 0),
                             stop=(pc == 3 and q4 == 3), r=[P.R("s5.CC"), rP], w=[pyr], inc=(pc == 3 and q4 == 3))
                return py, pyr

            def tail(c, py, pyr):
                tok = slice(c * 128, (c + 1) * 128)
                tt = c // 4
                P.stt(ysb[:], uT[:, tok], gcol(C, l, SP_S5_D, hc), py[:, :128], ALU.mult, ALU.add,
                      r=[pyr, P.R("s5.u", tt), P.R("spar")], w=[P.R("s5.ys")])
                gelu_tanh(C, s5T[:, hc, tok], ysb[:], tmp[0][:, 0:128], tmp[0][:, 128:256], [P.R("s5.ys")], [P.R("mx.s5", hc, tt)], rt[0])

            pws = {0: front(0)}
            pys = {}
            for c in range(16):
                if c + 1 < 16:
                    pws[c + 1] = front(c + 1)
                pys[c] = back(c, pws.pop(c))
                if c >= 1:
                    tail(c - 1, *pys.pop(c - 1))
            tail(15, *pys.pop(15))
        slot, gres, gsem = C.ring2.next()
        gluw = slot[:, 0:512].rearrange("p (kc n) -> p kc n", kc=2)
        wload(C, gluw, gluw_d, gres, gsem)
        for tt in range(4):
            tl = slice(tt * 512, (tt + 1) * 512)
            for oc in range(2):
                ps, pr = bank(C)
                for kc in range(2):
                    P.mm(ps[:], gluw[:, kc, oc * 128:(oc + 1) * 128], s5T[:, kc, tl], start=(kc == 0), stop=(kc == 1),
                         r=[gres, P.R("mx.s5", kc, tt)], w=[pr], inc=(kc == 1))
                P.act(tmp[oc][:], ps[:], AF.Sigmoid, bias=gcol(C, l, SP_GLU_B, oc), r=[pr, P.R("spar")], w=[rt[oc]])
            for oc in range(2):
                P.tt(s5T[:, oc, tl], s5T[:, oc, tl], tmp[oc][:], ALU.mult, r=[rt[oc], P.R("mx.s5", oc, tt)], w=[P.R("mx.s5", oc, tt)])
    P.free("s5")


def gla_branch(C, l, win, hT, glaT):
    P, nc = C.P, C.nc
    b = l * SPL
    with ExitStack() as es:
        def sb(name, shape, dt):
            return es.enter_context(sbt(nc, name, shape, dt))
        qT = sb("gqT", [128, SEQ], BF16)
        kT = sb("gkT", [128, SEQ], BF16)
        ktok = sb("gktok", [128, 16, 128], BF16)
        gv = sb("gv", [128, 16, 256], BF16)
        gdT = sb("gdT", [128, SEQ], BF16)
        gw = sb("ggw", [128, 128], BF16)
        vpad = sb("gvpad", [128, 4, 128], BF16)
        klc = sb("gklc", [128, 2, 128], BF16)
        klcb = sb("gklcb", [128, 2, 128], BF16)
        qfb = sb("gqfb", [128, 128], BF16)
        bB = sb("gbB", [128, 128], F32)
        onec = sb("onec", [128, 1], F32)
        zt = sb("gzt", [128, 128], F32)
        nla = sb("gnla", [128, 128], BF16)
        eg = [sb(f"geg{i}", [128, 128], F32) for i in range(2)]
        ieg = sb("gieg", [128, 128], F32)
        er = sb("ger", [128, 128], F32)
        qf = sb("gqf", [128, 128], BF16)
        qi = sb("gqi", [128, 128], BF16)
        kih = sb("gkih", [128, 4, 128], BF16)
        kfh = sb("gkfh", [128, 4, 128], BF16)
        kl = sb("gkl", [128, 128], BF16)
        attn = sb("gattn", [128, 4, 2, 128], BF16)
        S = sb("gS", [128, 256], F32)
        Sb = [sb(f"gSb{i}", [128, 256], BF16) for i in range(3)]
        sqg = sb("gsq", [128, 128], BF16)
        rstd = sb("grstd", [128, 128], F32)
        on = sb("gon", [128, 128], F32)
        rtmp = sb("grtmp", [128, 512], F32)
        hm = C.cst[:, C_HM:C_HM + 4]
        MUf = C.cst[:, C_MU:C_MU + 128]
        MLf = C.cst[:, C_ML:C_ML + 128]
        P.memset(onec[:], 1.0, w=[P.R("gla.c")])
        P.memset(S[:], 0.0, w=[P.R("gla.S")])
        P.memset(Sb[0][:], 0.0, w=[P.R("gla.Sb", 0)])
        P.dma("sp", bB[:], C.dr["bp"][:, l * BPL + BP_GLAB:l * BPL + BP_GLAB + 128], "D_gbB", writes=[P.R("gla.bB")])
        P.memset(vpad[:], 0.0, w=[P.R("gla.vpad")])
        P.memset(klc[:], 0.0, w=[P.R("gla.klc", 0)])
        P.memset(klcb[:], 0.0, w=[P.R("gla.klc", 1)])
        wload(C, gw[:], C.dr["gla_gate_w"][l], P.R("gla.gw"), "D_ggw")
        slotA, resA, semA = C.ring8.next()
        wA = slotA[:].rearrange("p (kc n) -> p kc n", kc=8)
        wload(C, wA, win[:, :, 0:512], resA, semA)
        slotD, resD, semD = C.ring8.next()
        win_gd = slotD[:, 0:1024].rearrange("p (kc n) -> p kc n", kc=8)
        wload(C, win_gd, win[:, :, 768:896], resD, semD)
        wB = slotD[:, 1024:3072].rearrange("p (kc n) -> p kc n", kc=8)
        wload(C, wB, win[:, :, 512:768], resD, semD)
        resB = resD

        def ev_q(ps, pr, tt):
            P.cp(qT[:, tt * 512:(tt + 1) * 512], ps[:], r=[pr], w=[P.R("gla.q", tt)], eng="act")

        def ev_k(ps, pr, tt):
            P.cp(kT[:, tt * 512:(tt + 1) * 512], ps[:], r=[pr], w=[P.R("gla.k", tt)], eng="act")
        if C.gla_stop >= 2:
            proj_fm(C, wA, resA, 0, 128, hT, ev_q)
            proj_fm(C, wA, resA, 128, 128, hT, ev_k)
        for m in range(2 if C.gla_stop >= 3 else 0):
            def ev_r(ps, pr, tt, m=m):
                P.act(rtmp[:], ps[:], AF.Sigmoid, r=[pr], w=[P.R("gla.rtmp")])
                P.tt(glaT[:, m, tt * 512:(tt + 1) * 512], ps[:], rtmp[:], ALU.mult, r=[pr, P.R("gla.rtmp")], w=[P.R("mx.gla", m, tt)])
            proj_fm(C, wB, resB, m * 128, 128, hT, ev_r)

        def ev_d(ps, pr, tt):
            P.cp(gdT[:, tt * 512:(tt + 1) * 512], ps[:], r=[pr], w=[P.R("gla.gd", tt)], eng="act")
        if C.gla_stop >= 4:
            proj_fm(C, win_gd, resD, 0, 128, hT, ev_d)
        for i in range(16 if C.gla_stop >= 5 else 0):
            ps, pr = bank(C)
            for kc in range(8):
                P.mm(ps[:, 0:384], hT[:, kc, i * 128:(i + 1) * 128], wA[:, kc, 128:512], start=(kc == 0), stop=(kc == 7),
                     r=[resA, P.R("mx.h", i // 4)], w=[pr], inc=(kc == 7))
            P.cp(ktok[:, i, :], ps[:, 0:128], r=[pr], w=[P.R("gla.ktok", i)], eng="act")
            P.cp(gv[:, i, :], ps[:, 128:384], r=[pr], w=[P.R("gla.gv", i)])
        sbi = [0]
        qf2 = [qf, qfb]
        klc2 = [klc, klcb]

        def phaseA(i, part):
            tok = slice(i * 128, (i + 1) * 128)
            tt = i // 4
            egi = eg[i % 2]
            reg = P.R("gla.eg", i % 2)
            qf_ = qf2[i % 2]
            rqf = P.R("gla.qf", i % 2)
            klc_ = klc2[i % 2]
            rklc = P.R("gla.klc", i % 2)
            if part == 0:
                ps1, pr1 = bank(C)
                P.mm(ps1[:, 0:128], gdT[:, tok], gw[:], r=[P.R("gla.gd", tt), P.R("gla.gw")], w=[pr1])
                P.tt(zt[:], ps1[:, 0:128], bB[:], ALU.add, r=[pr1, P.R("gla.bB")], w=[P.R("gla.zt")])
                P.act(zt[:], zt[:], AF.Exp, r=[P.R("gla.zt")], w=[P.R("gla.zt")], scale=-1.0)
                P.act(nla[:], zt[:], AF.Ln, r=[P.R("gla.zt"), P.R("gla.c")], w=[P.R("gla.nla")], bias=onec[:])
                psG, prG = bank(C)
                P.mm(psG[:, 0:128], nla[:], C.mub[:, 0, :], r=[P.R("gla.nla"), P.R("cstb")], w=[prG])
                psR, prR = bank(C)
                P.mm(psR[:, 0:128], C.mub[:, 1, :], nla[:], r=[P.R("gla.nla"), P.R("cstb")], w=[prR])
                P.act(egi[:], psG[:, 0:128], AF.Exp, r=[prG], w=[reg], scale=-1.0 / 16.0)
                P.act(ieg[:], psG[:, 0:128], AF.Exp, r=[prG], w=[P.R("gla.ieg")], scale=1.0 / 16.0)
                P.act(er[:], psR[:, 0:128], AF.Exp, r=[prR], w=[P.R("gla.er")], scale=-1.0 / 16.0)
                return None
            hs = (0, 1) if part == 1 else (2, 3)
            if part == 1:
                P.stt(qf_[:], qT[:, tok], 32.0 ** -0.5, egi[:], ALU.mult, ALU.mult, r=[P.R("gla.q", tt), reg], w=[rqf])
                P.stt(qi[:], qT[:, tok], 32.0 ** -0.5, ieg[:], ALU.mult, ALU.mult, r=[P.R("gla.q", tt), P.R("gla.ieg")], w=[P.R("gla.qi")])
            for h in hs:
                P.stt(kih[:, h, :], kT[:, tok], hm[:, h:h + 1], ieg[:], ALU.mult, ALU.mult,
                      r=[P.R("gla.k", tt), P.R("gla.ieg"), P.R("cst")], w=[P.R("gla.kih")])
                P.stt(kfh[:, h, :], kT[:, tok], hm[:, h:h + 1], egi[:], ALU.mult, ALU.mult,
                      r=[P.R("gla.k", tt), reg, P.R("cst")], w=[P.R("gla.kfh")])
            if part == 1:
                return None
            for cc in range(2):
                P.tt(klc_[64 * cc:64 * cc + 64, cc, :], ktok[64 * cc:64 * cc + 64, i, :], er[64 * cc:64 * cc + 64, :], ALU.mult,
                     r=[P.R("gla.ktok", i), P.R("gla.er")], w=[rklc])
            pS = [bank(C), bank(C)]
            for h in range(4):
                pb, prb = pS[h // 2]
                o0 = (h % 2) * 256
                P.mm(pb[:, o0:o0 + 128], kih[:, h, :], qf_[:], r=[P.R("gla.kih"), rqf], w=[prb])
                P.mm(pb[:, o0 + 128:o0 + 256], kfh[:, h, :], qi[:], r=[P.R("gla.kfh"), P.R("gla.qi")], w=[prb])
            for h in range(4):
                pb, prb = pS[h // 2]
                o0 = (h % 2) * 256
                P.tt(attn[:, h, :, :], pb[:, o0:o0 + 256].rearrange("p (f t) -> p f t", f=2), C.mub[:], ALU.mult,
                     r=[prb, P.R("cstb")], w=[P.R("gla.attn")])
            for par in range(2):
                P.cp(vpad[:, par::2, 64 * par:64 * par + 64], gv[:, i, :].rearrange("p (h e) -> p h e", h=4)[:, par::2, :],
                     r=[P.R("gla.gv", i)], w=[P.R("gla.vpad")])
            pO = [(C.ps[4 + 2 * (i % 2) + m], P.R("ps", 4 + 2 * (i % 2) + m)) for m in range(2)]
            for m in range(2):
                po, pro = pO[m]
                for hh in range(2):
                    h = 2 * m + hh
                    for fb in range(2):
                        P.mm(po[:, 0:128], vpad[:, h, :], attn[:, h, fb, :],
                             start=(fb == 0 and hh == 0), stop=False, r=[P.R("gla.vpad"), P.R("gla.attn")], w=[pro])
            return pO

        def phaseB(i, pO, part):
            tok = slice(i * 128, (i + 1) * 128)
            tt = i // 4
            egi = eg[i % 2]
            reg = P.R("gla.eg", i % 2)
            qf_ = qf2[i % 2]
            rqf = P.R("gla.qf", i % 2)
            klc_ = klc2[i % 2]
            rklc = P.R("gla.klc", i % 2)
            if part < 2:
                cc = part
                sprev = Sb[sbi[0] % 3]
                rsprev = P.R("gla.Sb", sbi[0] % 3)
                for m in range(2):
                    po, pro = pO[m]
                    P.mm(po[:, cc * 64:(cc + 1) * 64], sprev[:, m * 128:(m + 1) * 128], qf_[:, cc * 64:(cc + 1) * 64],
                         start=False, stop=(cc == 1), r=[rsprev, rqf], w=[pro])
                pD, prD = bank(C)
                P.mm(pD[:, 0:256], klc_[:, cc, :], gv[:, i, :],
                     r=[rklc, P.R("gla.gv", i)], w=[prD])
                P.stt(S[:], S[:], egi[:, 63 + 64 * cc:64 + 64 * cc], pD[:, 0:256], ALU.mult, ALU.add,
                      r=[P.R("gla.S"), reg, prD], w=[P.R("gla.S")])
                sbi[0] += 1
                P.tt(Sb[sbi[0] % 3][:], S[:], C.smb[:], ALU.mult, r=[P.R("gla.S"), P.R("cstb")], w=[P.R("gla.Sb", sbi[0] % 3)])
                return
            for m in range(2):
                po, pro = pO[m]
                P.act(sqg[:], po[:, 0:128], AF.Square, r=[pro], w=[P.R("gla.sq")])
                pN, prN = bank(C)
                P.mm(pN[:, 0:128], C.bob[:], sqg[:], r=[P.R("gla.sq"), P.R("cstb")], w=[prN])
                P.act(rstd[:], pN[:, 0:128], AF.Ln, r=[prN, P.R("cstb")], w=[P.R("gla.rstd")], bias=C.epsc[:], scale=1.0 / 64.0)
                P.act(rstd[:], rstd[:], AF.Exp, r=[P.R("gla.rstd")], w=[P.R("gla.rstd")], scale=-0.5)
                P.stt(on[:], po[:, 0:128], C.sp[:, b + SP_GLA_NG + m:b + SP_GLA_NG + m + 1], rstd[:], ALU.mult, ALU.mult,
                      r=[pro, P.R("gla.rstd"), P.R("spar")], w=[P.R("gla.on")])
                P.tt(glaT[:, m, tok], on[:], glaT[:, m, tok], ALU.mult, r=[P.R("gla.on"), P.R("mx.gla", m, tt)],
                     w=[P.R("mx.gla", m, tt)])

        nt = C.gla_ntiles
        pOs = {}
        if nt:
            phaseA(0, 0)
            phaseA(0, 1)
            pOs[0] = phaseA(0, 2)
        for i in range(nt):
            nx = i + 1 < nt
            if nx:
                phaseA(i + 1, 0)
            phaseB(i, pOs[i], 0)
            if nx:
                phaseA(i + 1, 1)
            phaseB(i, pOs[i], 1)
            if nx:
                pOs[i + 1] = phaseA(i + 1, 2)
            phaseB(i, pOs.pop(i), 2)
    P.free("gla")


def fox_branch(C, l, win, hT, foT):
    P, nc = C.P, C.nc
    with ExitStack() as es:
        def sb(name, shape, dt):
            return es.enter_context(sbt(nc, name, shape, dt))
        fq = sb("fq", [128, SEQ], BF16)
        fkp = sb("fkp", [128, 2, SEQ], BF16)
        fvp = sb("fvp", [128, 16, 2, 128], BF16)
        lf = sb("flf", [128, 16, 8], F32)
        lfb = sb("flfb", [128, 128], BF16)
        NF = sb("fNF", [128, 16, 8], F32)
        Cc = sb("fCc", [128, 16, 8], F32)
        NF2 = sb("fNF2", [128, 16, 16], F32)
        NFThl = sb("fNFT", [128, SEQ], BF16)
        sel = sb("fsel", [128, 8, 128], BF16)
        identb = sb("fidb", [128, 128], BF16)
        mneg = sb("fmneg", [128, 128], BF16)
        t32 = sb("ft32", [16, 512], F32)
        h16 = sb("fh16", [16, 512], BF16)
        h32 = sb("fh32", [16, 512], F32)
        fbB = sb("ffbB", [128, 8], F32)
        onec = sb("fonec", [128, 1], F32)
        onesAB = sb("fones", [128, 2, 128], BF16)
        pt = [sb(f"fpt{i}", [128, 512], BF16) for i in range(3)]
        rl = sb("frl", [128, 512], F32)
        P.memset(onec[:], 1.0, w=[P.R("fox.c")])
        P.memset(onesAB[:], 0.0, w=[P.R("fox.c")])
        P.memset(onesAB[:, 0, 0:64], 1.0, w=[P.R("fox.c")])
        P.memset(onesAB[:, 1, 64:128], 1.0, w=[P.R("fox.c")])
        P.memset(fkp[:], 0.0, w=[P.R("fox.kz")])
        P.memset(fvp[:], 0.0, w=[P.R("fox.vz")])
        P.dma("sp", fbB[:], C.dr["bp"][:, l * BPL + BP_FOXB:l * BPL + BP_FOXB + 8], "D_fbB", writes=[P.R("fox.fbB")])
        slot, fres, fsem = C.ring2.next()
        wff = slot[:, 0:64].rearrange("p (kc n) -> p kc n", kc=8)
        wload(C, wff, win[:, :, O_FF:O_FF + 8], fres, fsem)
        for i in range(16):
            ps, pr = bank(C)
            for kc in range(8):
                P.mm(ps[:, 0:8], hT[:, kc, i * 128:(i + 1) * 128], wff[:, kc, :], start=(kc == 0), stop=(kc == 7),
                     r=[fres, P.R("mx.h", i // 4)], w=[pr], inc=(kc == 7))
            P.tt(lf[:, i, :], ps[:, 0:8], fbB[:], ALU.add, r=[pr, P.R("fox.fbB")], w=[P.R("fox.lf")])
        lff = lf[:].rearrange("p j h -> p (j h)")
        P.act(lff, lff, AF.Exp, r=[P.R("fox.lf")], w=[P.R("fox.lf")], scale=-1.0)
        P.act(lff, lff, AF.Ln, r=[P.R("fox.lf"), P.R("fox.c")], w=[P.R("fox.lf")], bias=onec[:])
        P.cp(lfb[:], lff, r=[P.R("fox.lf")], w=[P.R("fox.lfb")])
        psT, prT = bank(C)
        P.mm(psT[:, 0:128], C.onesb[:], lfb[:], r=[P.R("fox.lfb"), P.R("cstb")], w=[prT])
        psL_, prL_ = bank(C)
        P.mm(psL_[:, 0:128], C.tib[:], lfb[:], r=[P.R("fox.lfb"), P.R("cstb")], w=[prL_])
        rC = P.R("fox.Cc")
        P.cp(Cc[:, 0, :], psT[:, 0:8], r=[prT], w=[rC])
        for i in range(1, 16):
            P.tt(Cc[:, i, :], psT[:, i * 8:(i + 1) * 8], Cc[:, i - 1, :], ALU.add, r=[prT, rC], w=[rC])
        rN = P.R("fox.NF")
        P.cp(NF[:, 0, :], psL_[:, 0:8], r=[prL_], w=[rN])
        P.tt(NF[:, 1:16, :].rearrange("p j h -> p (j h)"), psL_[:, 8:128], Cc[:, 0:15, :].rearrange("p j h -> p (j h)"), ALU.add,
             r=[prL_, rC], w=[rN])
        P.ts(NF2[:, :, 0:8], NF[:], -1.0, ALU.mult, r=[rN], w=[P.R("fox.NF2")])
        P.ts(NF2[:, :, 8:16], NF[:], -1.0, ALU.mult, r=[rN], w=[P.R("fox.NF2")])
        P.memset(NFThl[:], 0.0, w=[P.R("fox.NFT")])
        P.cp(sel[:], C.cst[:, C_SEL:C_SEL + 8].unsqueeze(2).to_broadcast([128, 8, 128]), r=[P.R("cst")], w=[P.R("fox.c")])
        P.cp(identb[:], C.ident, r=[P.R("cst")], w=[P.R("fox.c")])
        P.ts(mneg[:], C.cst[:, C_TI:C_TI + 128], -1.0, ALU.add, 30000.0, ALU.mult, r=[P.R("cst")], w=[P.R("fox.c")])
        m0 = C.cst[0:16, C_M01:C_M01 + 1]
        m1 = C.cst[0:16, C_M01 + 1:C_M01 + 2]
        rq = P.R("fox.hl")
        for blk in range(4):
            ps, pr = bank(C)
            for k4 in range(4):
                P.tr(ps[0:16, k4 * 128:(k4 + 1) * 128], NF2[:, blk * 4 + k4, :], C.ident, r=[P.R("fox.NF2"), P.R("cst")], w=[pr])
            P.cp(t32[:], ps[0:16, :], r=[pr], w=[rq], eng="act")
            P.cp(h16[:], t32[:], r=[rq], w=[rq])
            P.cp(h32[:], h16[:], r=[rq], w=[rq])
            P.tt(t32[:], t32[:], h32[:], ALU.subtract, r=[rq], w=[rq])
            P.ts(h32[:], h32[:], m0, ALU.mult, r=[rq, P.R("cst")], w=[rq])
            P.stt(NFThl[0:16, blk * 512:(blk + 1) * 512], t32[:], m1, h32[:], ALU.mult, ALU.add, r=[rq, P.R("cst")], w=[P.R("fox.NFT")])
        pk = 0
        for m in range(4):
            slot, wres, wsem = C.ring8.next()
            wv = slot[:, 0:3072].rearrange("p (kc g n) -> p kc g n", kc=8, g=3)
            for g, off in enumerate((O_FQ, O_FK, O_FV)):
                wload(C, wv[:, :, g, :], win[:, :, off + m * 128:off + (m + 1) * 128], wres, wsem)

            def ev_q(ps, pr, tt):
                P.op("act", lambda e, o=fq[:, tt * 512:(tt + 1) * 512], i=ps[:]: e.mul(out=o, in_=i, mul=0.125), [pr], [P.R("fox.q", tt)])

            def ev_k(ps, pr, tt):
                P.cp(fkp[0:64, 0, tt * 512:(tt + 1) * 512], ps[0:64, :], r=[pr, P.R("fox.kz")], w=[P.R("fox.k", tt)], eng="act")
                P.cp(fkp[64:128, 1, tt * 512:(tt + 1) * 512], ps[64:128, :], r=[pr, P.R("fox.kz")], w=[P.R("fox.k", tt)])
            proj_fm(C, wv[:, :, 0, :], wres, 0, 128, hT, ev_q)
            proj_fm(C, wv[:, :, 1, :], wres, 0, 128, hT, ev_k)
            for i in range(16):
                ps, pr = bank(C)
                for kc in range(8):
                    P.mm(ps[:, 0:128], hT[:, kc, i * 128:(i + 1) * 128], wv[:, kc, 2, :], start=(kc == 0), stop=(kc == 7),
                         r=[wres, P.R("mx.h", i // 4)], w=[pr], inc=(kc == 7))
                P.cp(fvp[:, i, 0, 0:64], ps[:, 0:64], r=[pr, P.R("fox.vz")], w=[P.R("fox.v", i)], eng="act")
                P.cp(fvp[:, i, 1, 64:128], ps[:, 64:128], r=[pr, P.R("fox.vz")], w=[P.R("fox.v", i)])
            for qg in range(4):
                par = (m * 4 + qg) % 2
                psO, prO = C.ps[4 + par], P.R("ps", 4 + par)
                psL, prL = C.ps[6 + par], P.R("ps", 6 + par)
                nj = 4 * qg + 4
                its = [(hh, j) for hh in range(2) for j in range(nj)]
                pend = {}

                def emit_S(k):
                    hh, j = its[k]
                    i_lo = max(j, 4 * qg)
                    ncol = (nj - i_lo) * 128
                    c0 = (i_lo - 4 * qg) * 128
                    psS, prS = bank(C)
                    diag = (j >= 4 * qg)
                    P.mm(psS[:, c0:c0 + ncol], fkp[:, hh, j * 128:(j + 1) * 128], fq[:, i_lo * 128:nj * 128],
                         start=True, stop=False, r=[P.R("fox.k", j // 4), P.R("fox.q", qg)], w=[prS], inc=False)
                    P.mm(psS[:, c0:c0 + ncol], sel[:, 2 * m + hh, :], NFThl[:, i_lo * 128:nj * 128],
                         start=False, stop=(not diag), r=[P.R("fox.c"), P.R("fox.NFT")], w=[prS], inc=(not diag))
                    if diag:
                        cd = (j - 4 * qg) * 128
                        P.mm(psS[:, cd:cd + 128], identb[:], mneg[:], start=False, stop=True, r=[P.R("fox.c")], w=[prS])
                    pend[k] = (psS, prS, c0, ncol)
                LOOK = 2
                for k in range(min(LOOK, len(its))):
                    emit_S(k)
                for k, (hh, j) in enumerate(its):
                    if k + LOOK < len(its):
                        emit_S(k + LOOK)
                    h = 2 * m + hh
                    psS, prS, c0, ncol = pend.pop(k)
                    ptile = pt[pk % 3]
                    rpt = P.R("fox.pt", pk % 3)
                    pk += 1
                    P.act(ptile[:, c0:c0 + ncol], psS[:, c0:c0 + ncol], AF.Exp, r=[prS, rN], w=[rpt],
                          bias=NF[:, j, h:h + 1])
                    first = (hh == 0 and j == 0)
                    last = (hh == 1 and j == nj - 1)
                    P.mm(psO[:, c0:c0 + ncol], fvp[:, j, hh, :], ptile[:, c0:c0 + ncol], start=first, stop=last,
                         r=[P.R("fox.v", j), rpt], w=[prO])
                    P.mm(psL[:, c0:c0 + ncol], onesAB[:, hh, :], ptile[:, c0:c0 + ncol], start=first, stop=last,
                         r=[P.R("fox.c"), rpt], w=[prL])
                P.act(rl[:], psL[:], AF.Ln, r=[prL], w=[P.R("fox.rl")])
                P.act(rl[:], rl[:], AF.Exp, r=[P.R("fox.rl")], w=[P.R("fox.rl")], scale=-1.0)
                P.tt(foT[:, m, qg * 512:(qg + 1) * 512], psO[:], rl[:], ALU.mult, r=[prO, P.R("fox.rl")], w=[P.R("mx.fo", m, qg)])
    P.free("fox")


def merge(C, l, win, hT, s5T, glaT, foT):
    P, nc = C.P, C.nc
    wgla = C.dr["w_gla_up"][l].rearrange("(kc p) n -> p kc n", p=128)
    ws5 = C.dr["w_s5_up"][l].rearrange("(kc p) n -> p kc n", p=128)
    wfox = C.dr["w_fox_up"][l].rearrange("(kc p) n -> p kc n", p=128)
    wmo = C.dr["w_mix_out"][l].rearrange("(kc p) n -> p kc n", p=128)
    srcs = ((glaT, 2, "mx.gla", wgla, 0), (s5T, 2, "mx.s5", ws5, 2), (foT, 4, "mx.fo", wfox, 4))
    with ExitStack() as es:
        mixT = es.enter_context(sbt(nc, "mixT", [128, 8, 1024], BF16))
        for half in range(2):
            with ExitStack() as ea:
                sig = [ea.enter_context(sbt(nc, f"msig{i}", [128, 512], F32)) for i in range(2)]
                acc = [ea.enter_context(sbt(nc, f"macc{i}", [128, 512], F32)) for i in range(2)]
                tm = ea.enter_context(sbt(nc, "mtm", [128, 512], F32))
                it = 0
                for c in range(8):
                    slot, wres, wsem = C.ring8.next()
                    wv = slot[:].rearrange("p (k n) -> p k n", k=32)
                    for (srcT, nk, rname, wd, k0) in srcs:
                        wload(C, wv[:, k0:k0 + nk, :], wd[:, :, c * 128:(c + 1) * 128], wres, wsem)
                    for bi in range(3):
                        g0 = O_GATES + bi * 1024 + c * 128
                        wload(C, wv[:, 8 + 8 * bi:16 + 8 * bi, :], win[:, :, g0:g0 + 128], wres, wsem)
                    for t2 in range(2):
                        tt = half * 2 + t2
                        tl = slice(tt * 512, (tt + 1) * 512)
                        a = it % 2
                        racc = P.R("mg.acc", a)
                        it += 1
                        for bi, (srcT, nk, rname, wd, k0) in enumerate(srcs):
                            psU, prU = bank(C, 0, 8)
                            for kc in range(nk):
                                P.mm(psU[:], wv[:, k0 + kc, :], srcT[:, kc, tl], start=(kc == 0), stop=(kc == nk - 1),
                                     r=[wres, P.R(rname, kc, tt)], w=[prU], inc=(kc == nk - 1))
                            psG, prG = bank(C, 0, 8)
                            for kc in range(8):
                                P.mm(psG[:], wv[:, 8 + 8 * bi + kc, :], hT[:, kc, tl], start=(kc == 0), stop=(kc == 7),
                                     r=[wres, P.R("mx.h", tt)], w=[prG], inc=(kc == 7))
                            sg = sig[bi % 2]
                            rsg = P.R("mg.sig", bi % 2)
                            P.act(sg[:], psG[:], AF.Sigmoid, r=[prG], w=[rsg])
                            if bi == 0:
                                P.tt(acc[a][:], psU[:], sg[:], ALU.mult, r=[prU, rsg], w=[racc])
                            elif bi == 1:
                                P.tt(tm[:], psU[:], sg[:], ALU.mult, r=[prU, rsg], w=[P.R("mg.tm")])
                                P.tt(acc[a][:], acc[a][:], tm[:], ALU.add, r=[racc, P.R("mg.tm")], w=[racc])
                            else:
                                P.tt(tm[:], psU[:], sg[:], ALU.mult, r=[prU, rsg], w=[P.R("mg.tm")])
                                P.tt(mixT[:, c, t2 * 512:(t2 + 1) * 512], acc[a][:], tm[:], ALU.add,
                                     r=[racc, P.R("mg.tm")], w=[P.R("mg.mix", t2)])
            P.free("mg.sig")
            P.free("mg.acc")
            P.free("mg.tm")
            with ExitStack() as eb:
                ysb = eb.enter_context(sbt(nc, "ymx", [128, 8, 512], F32))
                sq = eb.enter_context(sbt(nc, "sq", [128, 2, 512], BF16))
                rs = eb.enter_context(sbt(nc, "rs", [128, 512], F32))
                for t2 in range(2):
                    tt = half * 2 + t2
                    for ob in range(2):
                        slot, sres, ssem = C.ring8.next()
                        sv = slot[:].rearrange("p (kc n) -> p kc n", kc=8)
                        wload(C, sv, wmo[:, :, ob * 512:(ob + 1) * 512], sres, ssem)
                        for m4 in range(4):
                            mc = ob * 4 + m4
                            ps, pr = bank(C, 0, 8)
                            for kc in range(8):
                                P.mm(ps[:], sv[:, kc, m4 * 128:(m4 + 1) * 128], mixT[:, kc, t2 * 512:(t2 + 1) * 512],
                                     start=(kc == 0), stop=(kc == 7), r=[sres, P.R("mg.mix", t2)], w=[pr], inc=(kc == 7))
                            P.cp(ysb[:, mc, :], ps[:], r=[pr], w=[P.R("mg.y")], eng=("act" if mc % 2 else "dve"))
                    postnorm_add(C, l, SP_MIX_POST, ysb[:], 1.0, tt, sq, rs, P.R("mg.y"))
            P.free("mg.y")
            P.free("nrm")
    P.free("mg")
    P.free("nrm")


def gelu_tanh(C, out, x, t1, t2, r, w, rt):
    P = C.P
    P.act(t1, x, AF.Square, r=r, w=[rt], scale=0.044715 ** 0.5)
    P.stt(t1, t1, 1.0, x, ALU.add, ALU.mult, r=[rt] + list(r), w=[rt])
    P.act(t1, t1, AF.Tanh, r=[rt], w=[rt], scale=0.7978845608028654)
    P.op("act", lambda e: e.mul(out=t2, in_=x, mul=0.5), list(r), [rt])
    P.stt(out, t1, 1.0, t2, ALU.add, ALU.mult, r=[rt], w=w)

def _cols(v):
    v = np.asarray(v, np.float32)
    return np.ascontiguousarray(v.reshape(-1, 128).T)


def make_consts():
    c = np.zeros((128, NCONST), np.float32)
    p = np.arange(128)
    c[:, C_ID:C_ID + 128] = np.eye(128)
    c[:, C_TI:C_TI + 128] = (p[:, None] <= p[None, :])
    same = (p[:, None] // 64) == (p[None, :] // 64)
    c[:, C_MU:C_MU + 128] = same & (p[:, None] <= p[None, :])
    c[:, C_ML:C_ML + 128] = same & (p[:, None] > p[None, :])
    c[:, C_BO:C_BO + 128] = same
    c[:, C_HM:C_HM + 4] = (p[:, None] // 32) == np.arange(4)[None, :]
    c[:, C_SM:C_SM + 256] = (p[:, None] // 32) == (np.arange(256)[None, :] // 64)
    c[:, C_IT:C_IT + 128] = p[None, :]
    c[:, C_IP] = p
    c[:, C_SEL:C_SEL + 8] = (p[:, None] == np.arange(8)[None, :]) | (p[:, None] == 8 + np.arange(8)[None, :])
    c[:, C_M01] = p < 8
    c[:, C_M01 + 1] = (p >= 8) & (p < 16)
    return c


def pack_small(inp):
    sp = np.zeros((128, DEPTH * SPL), np.float32)
    bp = np.zeros((128, DEPTH * BPL), np.float32)
    for l in range(DEPTH):
        b = l * SPL
        for off, name in ((SP_FFN1_PRE, "ffn1_pre_g"), (SP_FFN1_POST, "ffn1_post_g"), (SP_MIX_PRE, "mix_pre_g"),
                          (SP_MIX_POST, "mix_post_g"), (SP_XA_PRE, "xa_pre_g"), (SP_XA_MEM, "xa_mem_g"),
                          (SP_XA_POST, "xa_post_g"), (SP_FFN2_PRE, "ffn2_pre_g"), (SP_FFN2_POST, "ffn2_post_g")):
            sp[:, b + off:b + off + 8] = _cols(inp[name][l])
        sp[:, b + SP_GLA_B:b + SP_GLA_B + 1] = _cols(inp["gla_gate_b"][l])
        sp[:, b + SP_GLA_NG:b + SP_GLA_NG + 2] = _cols(inp["gla_norm_g"][l])
        sp[:, b + SP_S5_D:b + SP_S5_D + 2] = _cols(inp["s5_d"][l])
        sp[:, b + SP_GLU_B:b + SP_GLU_B + 2] = _cols(inp["s5_glu_b"][l])
        sp[:, b + SP_ARE:b + SP_ARE + 8] = _cols(inp["s5_a_re"][l].reshape(-1))
        sp[:, b + SP_AIM:b + SP_AIM + 8] = _cols(inp["s5_a_im"][l].reshape(-1))
        ldt = np.repeat(np.asarray(inp["s5_log_dt"][l], np.float32), 64)
        sp[:, b + SP_LDT:b + SP_LDT + 8] = _cols(ldt)
        q = l * BPL
        bp[:, q + BP_FOXB:q + BP_FOXB + 8] = np.asarray(inp["fox_f_b"][l], np.float32)[None, :]
        bp[:, q + BP_GLAB:q + BP_GLAB + 128] = np.asarray(inp["gla_gate_b"][l], np.float32)[None, :]
        bp[:, q + BP_ARE:q + BP_ARE + 1024] = np.asarray(inp["s5_a_re"][l], np.float32).reshape(1, -1)
        bp[:, q + BP_AIM:q + BP_AIM + 1024] = np.asarray(inp["s5_a_im"][l], np.float32).reshape(1, -1)
        bp[:, q + BP_LDT:q + BP_LDT + 1024] = ldt[None, :]
    return sp, bp


def pack_s5(inp):
    bb_re = np.zeros((DEPTH, 256, 1024), np.float32)
    bb_im = np.zeros((DEPTH, 256, 1024), np.float32)
    cc_re = np.zeros((DEPTH, 1024, 256), np.float32)
    cc_im = np.zeros((DEPTH, 1024, 256), np.float32)
    for g in range(16):
        bb_re[:, g * 16:(g + 1) * 16, g * 64:(g + 1) * 64] = np.transpose(inp["s5_b_re"][:, g], (0, 2, 1))
        bb_im[:, g * 16:(g + 1) * 16, g * 64:(g + 1) * 64] = np.transpose(inp["s5_b_im"][:, g], (0, 2, 1))
        cc_re[:, g * 64:(g + 1) * 64, g * 16:(g + 1) * 16] = np.transpose(inp["s5_c_re"][:, g], (0, 2, 1))
        cc_im[:, g * 64:(g + 1) * 64, g * 16:(g + 1) * 16] = np.transpose(inp["s5_c_im"][:, g], (0, 2, 1))
    return bb_re, bb_im, cc_re, cc_im


ALL_STAGES = [(k, l) for l in range(DEPTH) for k in ("ffn1", "mix", "xa", "ffn2")]
BIG = ("ffn1_w_gu", "ffn1_w_down", "ffn2_w_gu", "ffn2_w_down", "w_in", "gla_gate_w", "w_gla_up", "s5_glu_w",
       "w_s5_up", "w_fox_up", "w_mix_out", "xa_w_q", "xa_w_kv", "xa_w_o")


def make_in_maps(inp, cores):
    sp, bp = pack_small(inp)
    bb_re, bb_im, cc_re, cc_im = pack_s5(inp)
    consts = make_consts()
    shared = {k: np.ascontiguousarray(np.asarray(inp[k], np.float32)) for k in BIG}
    gwp = np.zeros((DEPTH, 128, 128), np.float32)
    gwp[:, 0:16, :] = np.asarray(inp["gla_gate_w"], np.float32)
    shared["gla_gate_w"] = gwp
    shared.update(bb_re=bb_re, bb_im=bb_im, cc_re=cc_re, cc_im=cc_im, sp=sp, bp=bp, consts=consts)
    maps = []
    for b in cores:
        m = dict(shared)
        m["x"] = np.ascontiguousarray(np.asarray(inp["x"][b], np.float32))
        m["mem"] = np.ascontiguousarray(np.asarray(inp["mem"][b], np.float32))
        maps.append(m)
    return maps


def kernel(**inputs):
    nc, C = build_program(ALL_STAGES)
    maps = make_in_maps(inputs, list(range(8)))
    res = run_bass_kernel_spmd(nc, maps, core_ids=list(range(8)))
    return np.stack([r["y"] for r in res.results], axis=0).astype(np.float32)
```
